# Optimizing a Trainium2 kernel written in Bass

```python
import math
import jax, jax.numpy as jnp
from jax import lax
import numpy as np

D_MODEL = 1024
BATCH = 8
SEQ = 4096
DEPTH = 4

N_MIXERS = 2
SB_HEADS = 16
SB_HEAD_DIM = D_MODEL // SB_HEADS
Q_BLOCK = 128
HG_EXPAND = 128
HG_HEADS = D_MODEL // HG_EXPAND
HG_KEY_DIM = HG_EXPAND
HG_VAL_DIM = D_MODEL // HG_HEADS
HG_CHUNK = 64
D_FF = 4 * D_MODEL
N_SB = (DEPTH + N_MIXERS - 1) // N_MIXERS
N_HG = DEPTH // N_MIXERS
EPS = 1e-6

kernel_name = "stick_breaking_hgrn2_hybrid"


def rmsnorm(x, gain):
    xf = x.astype(jnp.float32)
    y = xf * lax.rsqrt(jnp.mean(xf * xf, axis=-1, keepdims=True) + EPS)
    return (y * gain.astype(jnp.float32)).astype(x.dtype)


def stick_breaking_attention(q, k, v):
    seq = q.shape[2]
    scale = 1.0 / math.sqrt(q.shape[-1])
    outs = []
    for t0 in range(0, seq, Q_BLOCK):
        t1 = t0 + Q_BLOCK
        z = jnp.einsum('bhqd,bhkd->bhqk', q[:, :, t0:t1], k[:, :, :t1]).astype(jnp.float32) * scale
        causal = jnp.arange(t1)[None, :] < (t0 + jnp.arange(Q_BLOCK))[:, None]
        log_stay = jnp.where(causal, jax.nn.log_sigmoid(-z), 0.0)
        log_between = lax.cumsum(log_stay, axis=3, reverse=True) - log_stay
        weights = jnp.where(causal, jnp.exp(jax.nn.log_sigmoid(z) + log_between), 0.0)
        outs.append(jnp.einsum('bhqk,bhkd->bhqd', weights.astype(v.dtype), v[:, :, :t1]))
    return jnp.concatenate(outs, axis=2)


def stick_breaking_mixer(h, w_qkv, q_gain, k_gain, w_o):
    bsz, seq, _ = h.shape
    qkv = h @ w_qkv
    q, k, v = jnp.split(qkv, 3, axis=-1)
    q = rmsnorm(q.reshape(bsz, seq, SB_HEADS, SB_HEAD_DIM), q_gain)
    k = rmsnorm(k.reshape(bsz, seq, SB_HEADS, SB_HEAD_DIM), k_gain)
    v = v.reshape(bsz, seq, SB_HEADS, SB_HEAD_DIM)
    q, k, v = (jnp.transpose(a, (0, 2, 1, 3)) for a in (q, k, v))
    o = stick_breaking_attention(q, k, v)
    o = jnp.transpose(o, (0, 2, 1, 3)).reshape(bsz, seq, D_MODEL)
    return o @ w_o


def hgrn2_chunk_scan(q, k, v, log_f):
    bsz, nh, seq, dk = q.shape
    dv = v.shape[-1]
    n_chunks = seq // HG_CHUNK

    def to_chunks(a):
        return jnp.moveaxis(a.astype(jnp.float32).reshape(bsz, nh, n_chunks, HG_CHUNK, a.shape[-1]), 2, 0)

    incl = jnp.tril(jnp.ones((HG_CHUNK, HG_CHUNK), dtype=bool))

    def step(state, inp):
        qc, kc, vc, lfc = inp
        b = jnp.cumsum(lfc, axis=2)
        inter = jnp.einsum('bhck,bhkv->bhcv', qc * jnp.exp(b), state)
        diff = b[:, :, :, None, :] - b[:, :, None, :, :]
        decay = jnp.where(incl[:, :, None], jnp.exp(jnp.minimum(diff, 0.0)), 0.0)
        scores = jnp.einsum('bhtk,bhsk,bhtsk->bhts', qc, kc, decay)
        intra = jnp.einsum('bhts,bhsv->bhtv', scores, vc)
        b_last = b[:, :, -1:, :]
        new_state = (jnp.exp(b_last[:, :, 0, :])[..., None] * state
                     + jnp.einsum('bhsk,bhsv->bhkv', kc * jnp.exp(b_last - b), vc))
        return new_state, inter + intra

    init = jnp.zeros((bsz, nh, dk, dv), jnp.float32)
    _, ys = lax.scan(step, init, (to_chunks(q), to_chunks(k), to_chunks(v), to_chunks(log_f)))
    return jnp.moveaxis(ys, 0, 2).reshape(bsz, nh, seq, dv)


def hgrn2_mixer(h, w_in, lower_bound, norm_gain, w_o):
    bsz, seq, _ = h.shape
    proj = h @ w_in
    q, f, i, g = jnp.split(proj, 4, axis=-1)
    q = jax.nn.silu(q)
    lb = lower_bound.astype(jnp.float32)
    forget = lb + (1.0 - lb) * jax.nn.sigmoid(f.astype(jnp.float32))
    log_f = jnp.log(forget)
    k = -jnp.expm1(log_f)

    def heads(a, d):
        return jnp.transpose(a.reshape(bsz, seq, HG_HEADS, d), (0, 2, 1, 3))

    o = hgrn2_chunk_scan(heads(q, HG_KEY_DIM), heads(k, HG_KEY_DIM),
                         heads(i, HG_VAL_DIM), heads(log_f, HG_KEY_DIM))
    o = jnp.transpose(o, (0, 2, 1, 3)).astype(h.dtype)
    o = rmsnorm(o, norm_gain).reshape(bsz, seq, D_MODEL)
    o = o * jax.nn.sigmoid(g)
    return o @ w_o


def squared_relu_mlp(h, w1, w2):
    a = jax.nn.relu(h @ w1)
    return (a * a) @ w2


def setup_inputs(seed: int = 0) -> dict:
    key = jax.random.key(seed)
    ks = jax.random.split(key, 12)
    d_in = D_MODEL ** -0.5
    res = (2 * DEPTH) ** -0.5
    nrm = jax.random.normal
    return {
        "x": nrm(ks[0], (BATCH, SEQ, D_MODEL), jnp.float32),
        "norm_gains": 1.0 + 0.02 * nrm(ks[1], (DEPTH, 2, D_MODEL), jnp.float32),
        "sb_w_qkv": nrm(ks[2], (N_SB, D_MODEL, 3 * D_MODEL), jnp.float32) * d_in,
        "sb_q_gain": 1.0 + 0.02 * nrm(ks[3], (N_SB, SB_HEAD_DIM), jnp.float32),
        "sb_k_gain": 1.0 + 0.02 * nrm(ks[4], (N_SB, SB_HEAD_DIM), jnp.float32),
        "sb_w_o": nrm(ks[5], (N_SB, D_MODEL, D_MODEL), jnp.float32) * d_in * res,
        "hg_w_in": nrm(ks[6], (N_HG, D_MODEL, 4 * D_MODEL), jnp.float32) * d_in,
        "hg_lb_logits": 0.5 * nrm(ks[7], (N_HG, D_MODEL), jnp.float32),
        "hg_norm_gain": 1.0 + 0.02 * nrm(ks[8], (N_HG, HG_VAL_DIM), jnp.float32),
        "hg_w_o": nrm(ks[9], (N_HG, D_MODEL, D_MODEL), jnp.float32) * d_in * res,
        "mlp_w1": nrm(ks[10], (DEPTH, D_MODEL, D_FF), jnp.float32) * d_in,
        "mlp_w2": nrm(ks[11], (DEPTH, D_FF, D_MODEL), jnp.float32) * (D_FF ** -0.5) * res,
    }


def reference(x, norm_gains, sb_w_qkv, sb_q_gain, sb_k_gain, sb_w_o,
              hg_w_in, hg_lb_logits, hg_norm_gain, hg_w_o, mlp_w1, mlp_w2):
    p = jax.nn.softmax(hg_lb_logits.astype(jnp.float32), axis=0)
    lower_bounds = jnp.cumsum(p, axis=0) - p[0:1]
    for layer in range(DEPTH):
        j = layer // N_MIXERS
        h = rmsnorm(x, norm_gains[layer, 0])
        if layer % N_MIXERS == 0:
            x = x + stick_breaking_mixer(h, sb_w_qkv[j], sb_q_gain[j], sb_k_gain[j], sb_w_o[j])
        else:
            x = x + hgrn2_mixer(h, hg_w_in[j], lower_bounds[j], hg_norm_gain[j], hg_w_o[j])
        h = rmsnorm(x, norm_gains[layer, 1])
        x = x + squared_relu_mlp(h, mlp_w1[layer], mlp_w2[layer])
    return x
```

```python
import contextlib
import numpy as np
import concourse.bass as bass
import concourse.mybir as mybir
from concourse.bass_utils import run_bass_kernel_spmd

F32 = mybir.dt.float32
BF16 = mybir.dt.bfloat16
AF = mybir.ActivationFunctionType
ALU = mybir.AluOpType

D = 1024
S = 4096
DEPTH = 4
NH = 16
DH = 64
DFF = 4096
C = D // 128
NT = 512
EPS = 1e-6


class Buf:
    def __init__(self, name):
        self.name = name
        self.w = None
        self.r = []


class Sched:
    ENG = ("pe", "act", "dve", "pool", "sp")

    def __init__(self, nc, stack):
        self.nc = nc
        self.stack = stack
        self.sem = {e: stack.enter_context(nc.semaphore("s_" + e)) for e in self.ENG if e != "sp"}
        self.cnt = {e: 0 for e in self.sem}
        self.prog = {e: [] for e in self.ENG}
        self.seen = {e: {} for e in self.ENG}
        self.dsem = {}
        self.dcnt = {}

    def _need(self, eng, toks):
        best = {}
        for t in toks:
            if t is None:
                continue
            s, v = t
            if v > best.get(id(s), (s, 0))[1]:
                best[id(s)] = (s, v)
        for s, v in best.values():
            if self.seen[eng].get(id(s), 0) < v:
                self.seen[eng][id(s)] = v
                self.prog[eng].append(("wait", s, v))

    def _deps(self, reads, writes):
        toks = [b.w for b in reads]
        for b in writes:
            toks.append(b.w)
            toks.extend(b.r)
        return toks

    def op(self, eng, fns, reads=(), writes=()):
        if callable(fns):
            fns = [fns]
        n0 = len(self.prog[eng])
        deps = self._deps(reads, writes)
        if eng == "pe":
            deps = [t for t in deps if t is not None and t[0] is not self.sem["pe"]]
        self._need(eng, deps)
        if eng in SELFWAIT_ENGINES and len(self.prog[eng]) == n0 and self.cnt[eng] > 0:
            self.seen[eng][id(self.sem[eng])] = self.cnt[eng]
            self.prog[eng].append(("wait", self.sem[eng], self.cnt[eng]))
        self.cnt[eng] += 1
        tok = (self.sem[eng], self.cnt[eng])
        self.prog[eng].append(("op", fns, self.sem[eng], 1))
        for b in reads:
            b.r.append(tok)
        for b in writes:
            b.w = tok
            b.r = []
        return tok

    def dma(self, q, fn, reads=(), writes=(), key=None):
        key = q + "_" + (key or writes[0].name)
        if key not in self.dsem:
            self.dsem[key] = self.stack.enter_context(self.nc.semaphore("d_" + key))
            self.dcnt[key] = 0
        self._need(q, self._deps(reads, writes))
        self.dcnt[key] += 16
        tok = (self.dsem[key], self.dcnt[key])
        self.prog[q].append(("op", [fn], self.dsem[key], 16))
        for b in reads:
            b.r.append(tok)
        for b in writes:
            b.w = tok
            b.r = []
        return tok

    def barrier(self):
        toks = [(self.sem[e], self.cnt[e]) for e in self.sem if self.cnt[e]]
        toks += [(self.dsem[k], self.dcnt[k]) for k in self.dsem if self.dcnt[k]]
        for e in self.ENG:
            self._need(e, toks)

    def emit(self):
        nc = self.nc
        engobj = {"pe": "tensor", "act": "scalar", "dve": "vector", "pool": "gpsimd", "sp": "sync"}
        with nc.Block() as block:
            for e in self.ENG:
                prog = self.prog[e]

                def body(eng, prog=prog):
                    for item in prog:
                        if item[0] == "wait":
                            eng.wait_ge(item[1], item[2])
                        else:
                            _, fns, sem, inc = item
                            ins = None
                            n_before = nc.n_instructions() if inc == 16 else 0
                            for f in fns:
                                ins = f(eng)
                            if inc == 16 and nc.n_instructions() - n_before != 1:
                                raise RuntimeError(f"dma_start was split into {nc.n_instructions() - n_before} instructions: the tracker's +16 per DMA "
                                                   f"would be wrong (each piece adds 16). Reshape this transfer.")
                            ins.then_inc(sem, inc)
                getattr(block, engobj[e])(body)
        self.prog = {e: [] for e in self.ENG}


class Prog:
    def __init__(self, layers, mixers=None, mlps=None, hg_blocks=None, hg_parts=3, allow_hgrn=False):
        self.allow_hgrn = allow_hgrn
        self.hg_blocks = hg_blocks
        self.hg_parts = hg_parts
        self.layers = layers
        self.mixers = layers if mixers is None else mixers
        self.mlps = layers if mlps is None else mlps

    def build(self):
        nc = bass.Bass("TRN2", target_bir_lowering=False)
        self.nc = nc
        nc.allow_low_precision("bf16 matmul operands with fp32 PSUM accumulation (problem tolerance is set for this)")
        dt = nc.dram_tensor
        self.x_in = dt("x", [S, D], F32, kind="ExternalInput").ap()
        self.out = dt("out", [S, D], F32, kind="ExternalOutput").ap()
        self.ng = dt("norm_gains", [128, DEPTH * 2 * C], F32, kind="ExternalInput").ap()
        self.ident = dt("ident", [128, 128], F32, kind="ExternalInput").ap()
        self.onesD = dt("onesD", [128, 128], F32, kind="ExternalInput").ap()
        sbm = [l for l in self.layers if l in self.mixers and l % 2 == 0]
        hgm = [l for l in self.layers if l in self.mixers and l % 2 == 1]
        mlm = [l for l in self.layers if l in self.mlps]
        self.used_inputs = {"x", "norm_gains", "ident", "onesD"}
        self.xT = dt("xT_scratch", [D, S], F32, kind="Internal").ap()
        if mlm:
            self.w1 = dt("mlp_w1", [DEPTH, D, DFF], F32, kind="ExternalInput").ap()
            self.w2 = dt("mlp_w2", [DEPTH, DFF, D], F32, kind="ExternalInput").ap()
            self.used_inputs |= {"mlp_w1", "mlp_w2"}
        if sbm or hgm:
            self.identb = dt("identb", [128, 128], F32, kind="ExternalInput").ap()
            self.used_inputs |= {"identb"}
        if sbm:
            self.wqkv = dt("sb_w_qkv", [2, D, 3 * D], F32, kind="ExternalInput").ap()
            self.wo_sb = dt("sb_w_o", [2, D, D], F32, kind="ExternalInput").ap()
            self.qkg = dt("qk_gain", [128, 4], F32, kind="ExternalInput").ap()
            self.blk64 = dt("blk64", [128, 128], F32, kind="ExternalInput").ap()
            self.maskneg = dt("maskneg", [128, 128], F32, kind="ExternalInput").ap()
            self.QT = dt("QT_scratch", [D, S], BF16, kind="Internal").ap()
            self.KT = dt("KT_scratch", [D, S], BF16, kind="Internal").ap()
            self.V = dt("V_scratch", [S, D], BF16, kind="Internal").ap()
            self.used_inputs |= {"sb_w_qkv", "sb_w_o", "qk_gain", "blk64", "maskneg"}
        if hgm:
            self.w_in = dt("hg_w_in", [2, D, 4 * D], F32, kind="ExternalInput").ap()
            self.wo_hg = dt("hg_w_o", [2, D, D], F32, kind="ExternalInput").ap()
            self.lbl = dt("hg_lb_logits", [2, D], F32, kind="ExternalInput").ap()
            self.lblT = dt("hg_lb_logits_T", [128, 2 * C], F32, kind="ExternalInput").ap()
            self.hgg = dt("hg_gain_T", [128, 2], F32, kind="ExternalInput").ap()
            self.M1 = dt("hg_M1", [128, 256], F32, kind="ExternalInput").ap()
            self.M2 = dt("hg_M2", [128, 128], F32, kind="ExternalInput").ap()
            self.Mc = dt("hg_Mc", [128, 1024], F32, kind="ExternalInput").ap()
            self.ones128 = dt("ones128", [128, 128], F32, kind="ExternalInput").ap()
            self.used_inputs |= {"hg_w_in", "hg_w_o", "hg_lb_logits", "hg_lb_logits_T", "hg_gain_T", "hg_M1", "hg_M2", "hg_Mc", "ones128"}
        with contextlib.ExitStack() as st:
            self.st = st
            self.sc = Sched(nc, st)
            self.ph = st
            self._consts()
            self.run_phase(self.phase_in)
            for l in self.layers:
                if l in self.mixers:
                    if l % 2 == 0:
                        self.run_phase(self.phase_qkv, l)
                        self.run_phase(self.phase_attn, l)
                    else:
                        self.run_phase(self.phase_hgrn, l)
                if l in self.mlps:
                    self.run_phase(self.phase_mlp, l)
            self.run_phase(self.phase_out)
        return nc

    def _uniq(self, name):
        self._nalloc = getattr(self, "_nalloc", 0) + 1
        return f"{name}_{self._nalloc}"

    def sb(self, name, shape, dtype):
        return self.ph.enter_context(self.nc.sbuf_tensor(self._uniq(name), shape, dtype))

    def ps(self, name, shape, dtype=F32):
        return self.ph.enter_context(self.nc.psum_tensor(self._uniq(name), shape, dtype))

    def run_phase(self, fn, *a):
        with contextlib.ExitStack() as ph:
            keep, self.ph = self.ph, ph
            fn(*a)
            self.sc.barrier()
            self.sc.emit()
            self.ph = keep

    def _consts(self):
        sc = self.sc
        self.ident_s = self.sb("ident_s", [128, 128], F32)
        self.ones_s = self.sb("ones_s", [128, 128], F32)
        self.ng_s = self.sb("ng_s", [128, DEPTH * 2 * C], F32)
        self.eps_s = self.sb("eps_s", [128, 1], F32)
        self.b_const = Buf("consts")
        sc.dma("sp", lambda q: q.dma_start(out=self.ident_s[:], in_=self.ident[:, :]), writes=[self.b_const], key="c0")
        b1 = Buf("c1"); b2 = Buf("c2"); self.b_eps = Buf("eps")
        sc.dma("sp", lambda q: q.dma_start(out=self.ones_s[:], in_=self.onesD[:, :]), writes=[b1], key="c1")
        sc.dma("sp", lambda q: q.dma_start(out=self.ng_s[:], in_=self.ng[:, :]), writes=[b2], key="c2")
        sc.op("dve", lambda v: v.memset(self.eps_s[:], EPS), writes=[self.b_eps])
        self.b_ones = b1
        self.b_ng = b2
        self.b_xT = Buf("xT_dram")

    def phase_in(self):
        sc, nc = self.sc, self.nc
        xin = [self.sb(f"pin_x{i}", [128, 4, D], F32) for i in range(2)]
        xo = [self.sb(f"pin_o{i}", [128, C, NT], F32) for i in range(2)]
        pst = [self.ps(f"pin_ps{i}", [128, NT]) for i in range(4)]
        b_in = [Buf(f"pin_x{i}") for i in range(2)]
        b_o = [Buf(f"pin_o{i}") for i in range(2)]
        b_ps = [Buf(f"pin_ps{i}") for i in range(4)]
        stores = []
        for t in range(S // NT):
            i = t % 2
            src = self.x_in[t * NT:(t + 1) * NT, :].rearrange("(j p) d -> p j d", p=128)
            sc.dma("sp", lambda q, i=i, src=src: q.dma_start(out=xin[i][:], in_=src), writes=[b_in[i]])
            for c in range(C):
                k = c % 4
                fns = [(lambda e, i=i, c=c, j=j, k=k: e.matmul(pst[k][:, j * 128:(j + 1) * 128],
                                                              lhsT=xin[i][:, j, c * 128:(c + 1) * 128],
                                                              rhs=self.ident_s[:], start=True, stop=True))
                       for j in range(4)]
                sc.op("pe", fns, reads=[b_in[i], self.b_const], writes=[b_ps[k]])
                eng = "dve" if c % 2 == 0 else "act"
                if eng == "dve":
                    sc.op("dve", lambda v, i=i, c=c, k=k: v.tensor_copy(out=xo[i][:, c, :], in_=pst[k][:]),
                          reads=[b_ps[k]], writes=[b_o[i]])
                else:
                    sc.op("act", lambda a, i=i, c=c, k=k: a.copy(out=xo[i][:, c, :], in_=pst[k][:]),
                          reads=[b_ps[k]], writes=[b_o[i]])
            dst = self.xT[:, t * NT:(t + 1) * NT].rearrange("(c p) n -> p c n", p=128)
            stores.append(sc.dma("sp", lambda q, i=i, dst=dst: q.dma_start(out=dst, in_=xo[i][:]),
                                 reads=[b_o[i]], writes=[self.b_xT], key=f"st{i}"))
        self.b_xT.r = []
        self._xT_tokens = stores

    def phase_out(self):
        sc = self.sc
        xi = [self.sb(f"po_x{i}", [128, C, NT], F32) for i in range(2)]
        xo = [self.sb(f"po_o{i}", [128, 4, D], F32) for i in range(2)]
        pst = [self.ps(f"po_ps{i}", [128, NT]) for i in range(4)]
        b_i = [Buf(f"po_x{i}") for i in range(2)]
        b_o = [Buf(f"po_o{i}") for i in range(2)]
        b_ps = [Buf(f"po_ps{i}") for i in range(4)]
        b_out = Buf("out_dram")
        toks = []
        for t in range(S // NT):
            i = t % 2
            src = self.xT[:, t * NT:(t + 1) * NT].rearrange("(c p) n -> p c n", p=128)
            self._wait_xT("sp")
            sc.dma("sp", lambda q, i=i, src=src: q.dma_start(out=xi[i][:], in_=src), writes=[b_i[i]])
            n = 0
            for j in range(4):
                for h in range(2):
                    k = n % 4
                    fns = [(lambda e, i=i, j=j, c=h * 4 + cc, cc=cc, k=k:
                            e.matmul(pst[k][:, cc * 128:(cc + 1) * 128], lhsT=xi[i][:, c, j * 128:(j + 1) * 128],
                                     rhs=self.ident_s[:], start=True, stop=True)) for cc in range(4)]
                    sc.op("pe", fns, reads=[b_i[i], self.b_const], writes=[b_ps[k]])
                    if n % 2 == 0:
                        sc.op("dve", lambda v, i=i, j=j, h=h, k=k: v.tensor_copy(out=xo[i][:, j, h * 512:(h + 1) * 512], in_=pst[k][:]),
                              reads=[b_ps[k]], writes=[b_o[i]])
                    else:
                        sc.op("act", lambda a, i=i, j=j, h=h, k=k: a.copy(out=xo[i][:, j, h * 512:(h + 1) * 512], in_=pst[k][:]),
                              reads=[b_ps[k]], writes=[b_o[i]])
                    n += 1
            dst = self.out[t * NT:(t + 1) * NT, :].rearrange("(j p) d -> p j d", p=128)
            toks.append(sc.dma("sp", lambda q, i=i, dst=dst: q.dma_start(out=dst, in_=xo[i][:]),
                               reads=[b_o[i]], writes=[b_out], key=f"st{i}"))
        self.sc._need("sp", toks)

    def phase_qkv(self, l):
        sc = self.sc
        j = l // 2
        n = NT
        wq = self.sb("wqkv_s", [128, C, 3 * D], BF16)
        bw = Buf("wqkv")
        for c in range(C):
            sc.dma("pool", lambda q, c=c: q.dma_start(out=wq[:, c, :], in_=self.wqkv[j, c * 128:(c + 1) * 128, :]), writes=[bw], key="w1")
        blk = self.sb("blk_s", [128, 128], F32); bblk = Buf("blk")
        qkg = self.sb("qkg_s", [128, 4], F32); bg = Buf("qkg")
        sc.dma("sp", lambda q: q.dma_start(out=blk[:], in_=self.blk64[:, :]), writes=[bblk], key="c0")
        sc.dma("sp", lambda q: q.dma_start(out=qkg[:], in_=self.qkg[:, :]), writes=[bg], key="c1")
        x = [self.sb(f"q_x{i}", [128, C, n], F32) for i in range(2)]
        h = [self.sb(f"q_h{i}", [128, C, n], BF16) for i in range(2)]
        scr = dict(sq=self.sb("q_sq", [128, C, n], F32), bsq=Buf("sq"), pss=self.ps("q_pss", [128, 512]), bpss=Buf("pss"),
                   rt=self.sb("q_rt", [128, n], F32), brt=Buf("rt"), rs=self.sb("q_rs", [128, n], F32), brs=Buf("rs"))
        pp = [self.ps(f"q_pp{i}", [128, 512]) for i in range(2)]
        pm = [self.ps(f"q_pm{i}", [128, 512]) for i in range(2)]
        pv = [self.ps(f"q_pv{i}", [128, 512]) for i in range(2)]
        bpp = [Buf(f"pp{i}") for i in range(2)]; bpm = [Buf(f"pm{i}") for i in range(2)]; bpv = [Buf(f"pv{i}") for i in range(2)]
        sq2 = [self.sb(f"q_sq2{i}", [128, n], F32) for i in range(2)]; bsq2 = [Buf(f"sq2{i}") for i in range(2)]
        rt2 = [self.sb(f"q_rt2{i}", [128, n], F32) for i in range(2)]; brt2 = [Buf(f"rt2{i}") for i in range(2)]
        rs2 = [self.sb(f"q_rs2{i}", [128, n], F32) for i in range(2)]; brs2 = [Buf(f"rs2{i}") for i in range(2)]
        qo = [self.sb(f"q_qo{i}", [128, n], BF16) for i in range(4)]; bqo = [Buf(f"qo{i}") for i in range(4)]
        vt = [self.sb(f"q_vt{i}", [128, 4, D], BF16) for i in range(2)]; bvt = [Buf(f"vt{i}") for i in range(2)]
        bx = [Buf(f"qx{i}") for i in range(2)]; bh = [Buf(f"qh{i}") for i in range(2)]
        b_scr = Buf("qkv_dram")
        gcol = (l * 2) * C
        nq = 0
        for t in range(S // n):
            i = t % 2
            src = self.xT[:, t * n:(t + 1) * n].rearrange("(c p) n -> p c n", p=128)
            self._wait_xT("sp")
            sc.dma("sp", lambda q, i=i, src=src: q.dma_start(out=x[i][:], in_=src), writes=[bx[i]])
            self.norm(x[i], bx[i], h[i], bh[i], n, gcol, scr)
            for m in range(16):
                k = m % 2
                col = m * 128
                fns = [(lambda e, i=i, c=c, k=k, col=col: e.matmul(pp[k][:], lhsT=wq[:, c, col:col + 128], rhs=h[i][:, c, :],
                                                                   start=(c == 0), stop=(c == C - 1))) for c in range(C)]
                sc.op("pe", fns, reads=[bh[i], bw], writes=[bpp[k]])
                sc.op("act", lambda a, k=k: a.activation(out=sq2[k][:], in_=pp[k][:], func=AF.Square), reads=[bpp[k]], writes=[bsq2[k]])
                sc.op("pe", lambda e, k=k: e.matmul(pm[k][:], lhsT=blk[:], rhs=sq2[k][:], start=True, stop=True),
                      reads=[bsq2[k], bblk], writes=[bpm[k]])
                sc.op("act", lambda a, k=k: a.activation(out=rt2[k][:], in_=pm[k][:], func=AF.Sqrt, bias=self.eps_s[:, 0:1], scale=1.0),
                      reads=[bpm[k], self.b_eps], writes=[brt2[k]])
                sc.op("dve", lambda v, k=k: v.reciprocal(out=rs2[k][:], in_=rt2[k][:]), reads=[brt2[k]], writes=[brs2[k]])
                o = nq % 4; nq += 1
                gc = j * 2 + (0 if m < 8 else 1)
                sc.op("dve", lambda v, k=k, o=o, gc=gc: v.scalar_tensor_tensor(out=qo[o][:], in0=pp[k][:], scalar=qkg[:, gc:gc + 1], in1=rs2[k][:],
                                                                                  op0=ALU.mult, op1=ALU.mult),
                      reads=[bpp[k], brs2[k], bg], writes=[bqo[o]])
                dstT = (self.QT if m < 8 else self.KT)[(m % 8) * 128:(m % 8 + 1) * 128, t * n:(t + 1) * n]
                sc.dma("sp", lambda q, o=o, dstT=dstT: q.dma_start(out=dstT, in_=qo[o][:]), reads=[bqo[o]], writes=[], key=f"qst{o}")
            for jj in range(4):
                for half in range(2):
                    k = (jj * 2 + half) % 2
                    fns = [(lambda e, i=i, c=c, k=k, jj=jj, half=half: e.matmul(pv[k][:], lhsT=h[i][:, c, jj * 128:(jj + 1) * 128],
                                                                                 rhs=wq[:, c, 2 * D + half * 512:2 * D + (half + 1) * 512],
                                                                                 start=(c == 0), stop=(c == C - 1))) for c in range(C)]
                    sc.op("pe", fns, reads=[bh[i], bw], writes=[bpv[k]])
                    sc.op("act", lambda a, i=i, k=k, jj=jj, half=half: a.copy(out=vt[i][:, jj, half * 512:(half + 1) * 512], in_=pv[k][:]),
                          reads=[bpv[k]], writes=[bvt[i]])
            dstV = self.V[t * n:(t + 1) * n, :].rearrange("(j p) d -> p j d", p=128)
            sc.dma("sp", lambda q, i=i, dstV=dstV: q.dma_start(out=dstV, in_=vt[i][:]), reads=[bvt[i]], writes=[], key=f"vst{i}")

    def phase_attn(self, l):
        sc = self.sc
        j = l // 2
        idb = self.sb("idb_s", [128, 128], BF16); bidb = Buf("idb")
        mk = self.sb("mk_s", [128, 128], BF16); bmk = Buf("mk")
        sc.dma("pool", lambda q: q.dma_start(out=idb[:], in_=self.identb[:, :]), writes=[bidb], key="c0")
        sc.dma("pool", lambda q: q.dma_start(out=mk[:], in_=self.maskneg[:, :]), writes=[bmk], key="c1")
        wo = self.sb("wo_s", [128, C, D], BF16); bwo = Buf("wo")
        for c0 in range(0, C, 4):
            src = self.wo_sb[j, c0 * 128:(c0 + 4) * 128, :].rearrange("(c p) d -> p c d", p=128)
            sc.dma("pool", lambda q, c0=c0, src=src: q.dma_start(out=wo[:, c0:c0 + 4, :], in_=src), writes=[bwo], key="w1")
        OT = self.sb("OT_s", [128, C, S], BF16)
        bOT = [Buf(f"OT{hp}") for hp in range(C)]
        po = [self.ps(f"a_po{i}", [128, 512]) for i in range(2)]; bpo = [Buf(f"po{i}") for i in range(2)]
        self.run_phase(self._attn_core, l, OT, bOT, po, bpo, idb, bidb, mk, bmk)
        self._attn_wo(l, OT, bOT, po, bpo, wo, bwo)

    def _attn_core(self, l, OT, bOT, po, bpo, idb, bidb, mk, bmk):
        sc = self.sc
        qT = [self.sb(f"a_q{i}", [128, S], BF16) for i in range(1)] * 2; bq = [Buf("aq")] * 2
        kT = [self.sb(f"a_k{i}", [128, S], BF16) for i in range(1)] * 2; bk = [Buf("ak")] * 2
        VA = [self.sb(f"a_va{i}", [128, 32, 128], BF16) for i in range(1)] * 2; bva = [Buf("va")] * 2
        VB = [self.sb(f"a_vb{i}", [128, 32, 128], BF16) for i in range(1)] * 2; bvb = [Buf("vb")] * 2
        for i in range(1):
            sc.op("pool", lambda v, i=i: v.memset(VA[i][:], 0.0), writes=[bva[i]])
            sc.op("pool", lambda v, i=i: v.memset(VB[i][:], 0.0), writes=[bvb[i]])
        CH = 1024
        g = [self.sb(f"a_g{i}", [128, CH], F32) for i in range(2)]; bgt = [Buf(f"ag{i}") for i in range(2)]
        Pf = [self.sb(f"a_P{i}", [128, S + 1], F32) for i in range(2)]; bP = [Buf(f"aP{i}") for i in range(2)]
        zer = self.sb("a_zero", [128, CH], F32); bz = Buf("zero")
        sc.op("pool", lambda v: v.memset(zer[:], 0.0), writes=[bz])
        w = [self.sb(f"a_w{i}", [128, CH], BF16) for i in range(2)]; bwt = [Buf(f"aw{i}") for i in range(2)]
        wT = [self.sb(f"a_wT{i}", [128, 8, 128], BF16) for i in range(2)]; bwT = [Buf(f"awT{i}") for i in range(2)]
        pz = [self.ps(f"a_pz{i}", [128, CH]) for i in range(2)]; bpz = [Buf(f"pz{i}") for i in range(2)]
        pt = self.ps("a_pt", [128, CH]); bpt = Buf("pt")
        chunks = []
        for hp in range(C):
            for Q in range(S // 128):
                t1 = 128 * (Q + 1)
                for hd in range(2):
                    b_ = t1
                    while b_ > 0:
                        a_ = max(0, b_ - CH)
                        chunks.append(dict(hp=hp, Q=Q, hd=hd, a=a_, b=b_, L=b_ - a_, t1=t1, first=(b_ == t1), newhp=(Q == 0 and hd == 0 and b_ == t1),
                                           first_pv=(hd == 0 and b_ == t1), last=(hd == 1 and a_ == 0),
                                           ip=(hp * 64 + Q * 2 + hd) % 2, io=(hp * 32 + Q) % 2))
                        b_ = a_
        for n_, ch in enumerate(chunks):
            ch["iz"] = n_ % 2

        def stage_A(ch):
            hp, Q, hd, a, b, L, iz = ch["hp"], ch["Q"], ch["hd"], ch["a"], ch["b"], ch["L"], ch["iz"]
            i = 0
            if ch["newhp"]:
                rows = slice(hp * 128, (hp + 1) * 128)
                sc.dma("sp", lambda q, rows=rows: q.dma_start(out=qT[i][:], in_=self.QT[rows, :]), writes=[bq[i]], key="ld0")
                sc.dma("sp", lambda q, rows=rows: q.dma_start(out=kT[i][:], in_=self.KT[rows, :]), writes=[bk[i]], key="ldk0")
                srcA = self.V[:, hp * 128:hp * 128 + 64].rearrange("(b p) d -> p b d", p=128)
                srcB = self.V[:, hp * 128 + 64:hp * 128 + 128].rearrange("(b p) d -> p b d", p=128)
                for b0 in range(0, 32, 8):
                    sc.dma("sp", lambda q, srcA=srcA, b0=b0: q.dma_start(out=VA[i][:, b0:b0 + 8, 0:64], in_=srcA[:, b0:b0 + 8, :]), writes=[bva[i]], key="lda")
                    sc.dma("sp", lambda q, srcB=srcB, b0=b0: q.dma_start(out=VB[i][:, b0:b0 + 8, 64:128], in_=srcB[:, b0:b0 + 8, :]), writes=[bvb[i]], key="ldb")
            pr = slice(hd * 64, hd * 64 + 64)
            fns = []
            for p0 in range(0, L, 512):
                pl = min(512, L - p0)
                diag = ch["first"] and (p0 + pl == L)
                fns.append(lambda e, p0=p0, pl=pl, diag=diag: e.matmul(pz[iz][:, p0:p0 + pl], lhsT=qT[i][pr, Q * 128:(Q + 1) * 128],
                                                                       rhs=kT[i][pr, a + p0:a + p0 + pl], start=True, stop=not diag))
                if diag:
                    fns.append(lambda e: e.matmul(pz[iz][:, L - 128:L], lhsT=idb[:], rhs=mk[:], start=False, stop=True))
            sc.op("pe", fns, reads=[bq[i], bk[i], bidb, bmk], writes=[bpz[iz]])
            sc.op("act", lambda a_: a_.activation(out=g[iz][:, :L], in_=pz[iz][:, :L], func=AF.Sigmoid, scale=-0.125), reads=[bpz[iz]], writes=[bgt[iz]])

        def stage_B(ch):
            a, b, L, iz, ip, t1 = ch["a"], ch["b"], ch["L"], ch["iz"], ch["ip"], ch["t1"]
            if ch["first"]:
                sc.op("dve", lambda v: v.memset(Pf[ip][:, t1:t1 + 1], 1.0), writes=[bP[ip]])
            sc.op("dve", lambda v: v.tensor_tensor_scan(out=Pf[ip][:, b - 1:(a - 1 if a > 0 else None):-1], data0=g[iz][:, L - 1::-1], data1=zer[:, :L],
                                                        initial=Pf[ip][:, b:b + 1], op0=ALU.mult, op1=ALU.add), reads=[bgt[iz], bP[ip], bz], writes=[bP[ip]])
            eng = {"pool": "pool", "dve": "dve", "alt": ("pool" if iz == 0 else "dve")}[ATT_SUB]
            sc.op(eng, lambda v: v.tensor_tensor(out=w[iz][:, :L], in0=Pf[ip][:, a + 1:b + 1], in1=Pf[ip][:, a:b], op=ALU.subtract),
                  reads=[bP[ip]], writes=[bwt[iz]])

        def stage_C1(ch):
            L, iz = ch["L"], ch["iz"]
            nb = L // 128
            fns = [(lambda e, bb=bb: e.matmul(pt[:, bb * 128:(bb + 1) * 128], lhsT=w[iz][:, bb * 128:(bb + 1) * 128], rhs=idb[:], start=True, stop=True))
                   for bb in range(nb)]
            sc.op("pe", fns, reads=[bwt[iz], bidb], writes=[bpt])
            sc.op("act", lambda a_: a_.copy(out=wT[iz][:, :nb, :], in_=pt[:, :nb * 128]), reads=[bpt], writes=[bwT[iz]])

        def stage_C2(ch):
            hp, Q, hd, a, L, iz, io = ch["hp"], ch["Q"], ch["hd"], ch["a"], ch["L"], ch["iz"], ch["io"]
            i = 0
            Vh, bV = (VA[i], bva[i]) if hd == 0 else (VB[i], bvb[i])
            nb = L // 128
            fns = []
            for bb in range(nb):
                kb = a // 128 + bb
                fns.append(lambda e, bb=bb, kb=kb, fp=(ch["first_pv"] and bb == 0), last=(ch["last"] and bb == nb - 1):
                           e.matmul(po[io][:, :128], lhsT=Vh[:, kb, :], rhs=wT[iz][:, bb, :], start=fp, stop=last))
            sc.op("pe", fns, reads=[bwT[iz], bV], writes=[bpo[io]])
            if ch["last"]:
                sc.op("dve", lambda v: v.tensor_copy(out=OT[:, hp, Q * 128:(Q + 1) * 128], in_=po[io][:, :128]), reads=[bpo[io]], writes=[bOT[hp]])

        nchk = len(chunks)
        for n_ in range(nchk + 3):
            if n_ < nchk:
                stage_A(chunks[n_])
            if 0 <= n_ - 1 < nchk:
                stage_B(chunks[n_ - 1])
            if 0 <= n_ - 2 < nchk:
                stage_C1(chunks[n_ - 2])
            if 0 <= n_ - 3 < nchk:
                stage_C2(chunks[n_ - 3])

    def _attn_wo(self, l, OT, bOT, po, bpo, wo, bwo):
        sc = self.sc
        n = NT
        x = [self.sb(f"o_x{i}", [128, C, n], F32) for i in range(2)]; bx = [Buf(f"ox{i}") for i in range(2)]
        stores = []
        for t in range(S // n):
            i = t % 2
            src = self.xT[:, t * n:(t + 1) * n].rearrange("(c p) n -> p c n", p=128)
            self._wait_xT("sp")
            sc.dma("sp", lambda q, i=i, src=src: q.dma_start(out=x[i][:], in_=src), writes=[bx[i]], key=f"ldx{i}")
            for o in range(C):
                k = o % 2
                fns = [(lambda e, o=o, c=c, k=k, t=t: e.matmul(po[k][:], lhsT=wo[:, c, o * 128:(o + 1) * 128], rhs=OT[:, c, t * n:(t + 1) * n],
                                                             start=(c == 0), stop=(c == C - 1))) for c in range(C)]
                sc.op("pe", fns, reads=bOT + [bwo], writes=[bpo[k]])
                sc.op("dve", lambda v, i=i, o=o, k=k: v.tensor_tensor(out=x[i][:, o, :], in0=po[k][:], in1=x[i][:, o, :], op=ALU.add),
                      reads=[bpo[k], bx[i]], writes=[bx[i]])
            dst = self.xT[:, t * n:(t + 1) * n].rearrange("(c p) n -> p c n", p=128)
            stores.append(sc.dma("sp", lambda q, i=i, dst=dst: q.dma_start(out=dst, in_=x[i][:]), reads=[bx[i]], writes=[self.b_xT], key=f"st{i}"))
        self.b_xT.r = []
        self._xT_tokens = stores

    def phase_hgrn(self, l):
        sc = self.sc
        j = l // 2
        NB = 128
        HH = 8
        win = self.sb("win_s", [128, C, 4 * D], BF16); bwin = Buf("win")
        for c in range(C):
            for hf in range(2):
                sc.dma("pool", lambda q, c=c, hf=hf: q.dma_start(out=win[:, c, hf * 2048:(hf + 1) * 2048],
                                                                  in_=self.w_in[j, c * 128:(c + 1) * 128, hf * 2048:(hf + 1) * 2048]), writes=[bwin], key="w1")
        wo = self.sb("hwo_s", [128, C, D], BF16); bwo = Buf("hwo")
        for c0 in range(0, C, 4):
            src = self.wo_hg[j, c0 * 128:(c0 + 4) * 128, :].rearrange("(c p) d -> p c d", p=128)
            sc.dma("pool", lambda q, c0=c0, src=src: q.dma_start(out=wo[:, c0:c0 + 4, :], in_=src), writes=[bwo], key="w2")
        M1 = self.sb("M1_s", [128, 256], F32); M2 = self.sb("M2_s", [128, 128], F32); Mc = self.sb("Mc_s", [128, 1024], F32)
        o128 = self.sb("o128_s", [128, 128], F32); idb = self.sb("hidb_s", [128, 128], BF16)
        lrow = self.sb("lrow_s", [128, 2, D], F32)
        lT = self.sb("lT_s", [128, 2 * C], F32); gg = self.sb("gg_s", [128, 2], F32)
        bK = Buf("hconst")
        sc.dma("sp", lambda q: q.dma_start(out=M1[:], in_=self.M1[:, :]), writes=[bK], key="c0")
        sc.dma("sp", lambda q: q.dma_start(out=M2[:], in_=self.M2[:, :]), writes=[bK], key="c0")
        sc.dma("sp", lambda q: q.dma_start(out=Mc[:], in_=self.Mc[:, :]), writes=[bK], key="c0")
        sc.dma("sp", lambda q: q.dma_start(out=o128[:], in_=self.ones128[:, :]), writes=[bK], key="c0")
        sc.dma("sp", lambda q: q.dma_start(out=lT[:], in_=self.lblT[:, :]), writes=[bK], key="c0")
        sc.dma("sp", lambda q: q.dma_start(out=gg[:], in_=self.hgg[:, :]), writes=[bK], key="c0")
        for jj in range(2):
            sc.dma("sp", lambda q, jj=jj: q.dma_start(out=lrow[:, jj, :], in_=self.lbl[jj:jj + 1, :].partition_broadcast(128)), writes=[bK], key="c0")
        omr = self.sb("omr_s", [128, D], F32); omT = self.sb("omT_s", [128, C], F32); bom = Buf("om")
        one1 = self.sb("one1_s", [128, 1], F32)
        sc.op("dve", lambda v: v.memset(one1[:], 1.0), writes=[bom])
        if j == 0:
            sc.op("dve", lambda v: v.memset(omr[:], 1.0), writes=[bom])
            sc.op("dve", lambda v: v.memset(omT[:], 1.0), writes=[bom])
        else:
            dr = self.sb("dr_s", [128, D], F32); dT = self.sb("dT_s", [128, C], F32); bd = Buf("dlt")
            sc.op("dve", lambda v: v.tensor_tensor(out=dr[:], in0=lrow[:, 0, :], in1=lrow[:, 1, :], op=ALU.subtract), reads=[bK], writes=[bd])
            sc.op("dve", lambda v: v.tensor_tensor(out=dT[:], in0=lT[:, 0:C], in1=lT[:, C:2 * C], op=ALU.subtract), reads=[bK], writes=[bd])
            sc.op("act", lambda a: a.activation(out=omr[:], in_=dr[:], func=AF.Sigmoid), reads=[bd], writes=[bom])
            sc.op("act", lambda a: a.activation(out=omT[:], in_=dT[:], func=AF.Sigmoid), reads=[bd], writes=[bom])
        sc.dma("pool", lambda q: q.dma_start(out=idb[:], in_=self.identb[:, :]), writes=[bK], key="c1")
        St = self.sb("S_s", [128, HH, 128], F32); bS = [Buf(f"S{h}") for h in range(HH)]
        Sb = self.sb("Sb_s", [128, HH, 128], BF16); bSb = [Buf(f"Sb{h}") for h in range(HH)]
        sc.op("pool", lambda v: v.memset(St[:], 0.0), writes=bS)
        x = [self.sb(f"g_x{i}", [128, C, NB], F32) for i in range(2)]; bx = [Buf(f"gx{i}") for i in range(2)]
        h = self.sb("g_h", [128, C, NB], BF16); bh = Buf("gh")
        scr = dict(sq=self.sb("g_sq", [128, C, NB], F32), bsq=Buf("sq"), pss=self.ps("g_pss", [128, 512]), bpss=Buf("pss"),
                   rt=self.sb("g_rt", [128, NB], F32), brt=Buf("rt"), rs=self.sb("g_rs", [128, NB], F32), brs=Buf("rs"))
        pA = self.ps("g_pA", [128, 1024]); bpA = [Buf("pA0"), Buf("pA1")]
        pB = self.ps("g_pB", [128, 1024]); bpB = [Buf("pB0"), Buf("pB1")]
        pX = self.ps("g_pX", [128, 1024]); bpX = [Buf("pX0"), Buf("pX1")]
        pE = self.ps("g_pE", [128, 512]); bpE = Buf("pE")
        W = HH * NB

        def t2(name, dtype=F32):
            return self.sb(name, [128, W], dtype), Buf(name)
        sbar, bsb = t2("g_sbar"); ktok, bkt = t2("g_ktok"); lf, blf = t2("g_lf")
        vtok, bvt = t2("g_vtok", BF16); khat, bkh = t2("g_khat", BF16)
        ed, bed = sbar, bsb
        qraw, bqr = t2("g_qraw"); graw, bgr = t2("g_graw"); fraw, bfr = t2("g_fraw")
        sgq, bsgq = t2("g_sgq"); sgg, bsgg = t2("g_sgg"); kf, bkf = t2("g_kf"); qf, bqf = t2("g_qf")
        E1, bE1 = t2("g_E1"); E2, bE2 = t2("g_E2")
        EX = self.sb("g_EX", [128, 4 * HH], F32); bEX = Buf("EX")
        qt, bqt = t2("g_qt", BF16); kt, bkt2 = t2("g_kt", BF16); scT, bsc = t2("g_scT", BF16)
        oT, boT = t2("g_oT"); osq, bosq = t2("g_osq"); ort, bort = t2("g_ort"); on, bon = t2("g_on", BF16)
        omF, bomF = t2("g_omF")
        sc.op("dve", lambda v: v.memset(omF[:], 1.0), writes=[bomF])
        if j != 0:
            for hh in range(HH):
                sc.op("dve", lambda v, hh=hh: v.tensor_scalar_mul(out=omF[:, hh * 128:(hh + 1) * 128], in0=omF[:, hh * 128:(hh + 1) * 128], scalar1=omT[:, hh:hh + 1]),
                      reads=[bom, bomF], writes=[bomF])
        hsl = lambda hh: slice(hh * 128, (hh + 1) * 128)
        gcol = (l * 2) * C
        stores = []
        for t in range(self.hg_blocks or (S // NB)):
            i = t % 2
            src = self.xT[:, t * NB:(t + 1) * NB].rearrange("(c p) n -> p c n", p=128)
            self._wait_xT("sp")
            sc.dma("sp", lambda q, i=i, src=src: q.dma_start(out=x[i][:], in_=src), writes=[bx[i]], key=f"ldx{i}")
            self.norm(x[i], bx[i], h, bh, NB, gcol, scr)
            for (pp, bpp, off) in ((pA, bpA, D), (pB, bpB, 2 * D)):
                fns = []
                for hf in range(2):
                    for c in range(C):
                        fns.append(lambda e, pp=pp, off=off, hf=hf, c=c: e.matmul(pp[:, hf * 512:(hf + 1) * 512], lhsT=h[:, c, :],
                                                                                  rhs=win[:, c, off + hf * 512:off + (hf + 1) * 512],
                                                                                  start=(c == 0), stop=(c == C - 1)))
                sc.op("pe", fns, reads=[bh, bwin], writes=bpp)
            sc.op("act", lambda a: a.activation(out=sbar[:], in_=pA[:], func=AF.Sigmoid, scale=-1.0), reads=bpA, writes=[bsb])
            sc.op("act", lambda a: a.copy(out=vtok[:], in_=pB[:]), reads=bpB, writes=[bvt])
            sc.op("dve", lambda v: v.tensor_tensor(out=ktok[:], in0=sbar[:], in1=omr[:], op=ALU.mult), reads=[bsb, bom], writes=[bkt])
            sc.op("act", lambda a: a.activation(out=lf[:], in_=ktok[:], func=AF.Ln, bias=one1[:, 0:1], scale=-1.0), reads=[bkt, bom], writes=[blf])
            fns = [(lambda e, hf=hf: e.matmul(pA[:, hf * 512:(hf + 1) * 512], lhsT=M2[:], rhs=lf[:, hf * 512:(hf + 1) * 512], start=True, stop=True))
                   for hf in range(2)]
            sc.op("pe", fns, reads=[blf, bK], writes=bpA)
            sc.op("act", lambda a: a.activation(out=ed[:], in_=pA[:], func=AF.Exp), reads=bpA, writes=[bed])
            sc.op("dve", lambda v: v.tensor_tensor(out=khat[:], in0=ed[:], in1=ktok[:], op=ALU.mult), reads=[bed, bkt], writes=[bkh])
            if self.hg_parts >= 2:
                def proj(dst, base):
                    return [(lambda e, hh=hh, c=c: e.matmul(dst[:, hsl(hh)], lhsT=win[:, c, base + hh * 128:base + (hh + 1) * 128], rhs=h[:, c, :],
                                                            start=(c == 0), stop=(c == C - 1))) for hh in range(HH) for c in range(C)]
                for dst, bdst, base, raw_, braw_ in ((pX, bpX, 0, qraw, bqr), (pA, bpA, 3 * D, graw, bgr), (pB, bpB, D, fraw, bfr)):
                    sc.op("pe", proj(dst, base), reads=[bh, bwin], writes=bdst)
                    sc.op("dve", lambda v, dst=dst, raw_=raw_: v.tensor_copy(out=raw_[:], in_=dst[:]), reads=bdst, writes=[braw_])
                sc.op("act", lambda a: a.activation(out=sgq[:], in_=qraw[:], func=AF.Sigmoid), reads=[bqr], writes=[bsgq])
                sc.op("act", lambda a: a.activation(out=sgg[:], in_=graw[:], func=AF.Sigmoid), reads=[bgr], writes=[bsgg])
                sc.op("act", lambda a: a.activation(out=kf[:], in_=fraw[:], func=AF.Sigmoid, scale=-1.0), reads=[bfr], writes=[bkf])
                sc.op("dve", lambda v: v.tensor_tensor(out=qf[:], in0=qraw[:], in1=sgq[:], op=ALU.mult), reads=[bqr, bsgq], writes=[bqf])
                sc.op("pe", [(lambda e, hh=hh: e.matmul(pX[:, hsl(hh)], lhsT=lf[:, hsl(hh)], rhs=M1[:, 0:128], start=True, stop=True)) for hh in range(HH)],
                      reads=[blf, bK], writes=bpX)
                sc.op("pe", [(lambda e, hh=hh: e.matmul(pE[:, hh * 4:(hh + 1) * 4], lhsT=lf[:, hsl(hh)], rhs=M1[:, 128:132], start=True, stop=True)) for hh in range(HH)],
                      reads=[blf, bK], writes=[bpE])
                sc.op("act", lambda a: a.activation(out=E1[:], in_=pX[:], func=AF.Exp), reads=bpX, writes=[bE1])
                sc.op("act", lambda a: a.activation(out=E2[:], in_=pX[:], func=AF.Exp, scale=-1.0), reads=bpX, writes=[bE2])
                sc.op("act", lambda a: a.activation(out=EX[:], in_=pE[:, 0:4 * HH], func=AF.Exp), reads=[bpE], writes=[bEX])
                sc.op("dve", lambda v: v.tensor_tensor(out=qt[:], in0=qf[:], in1=E1[:], op=ALU.mult), reads=[bqf, bE1], writes=[bqt])
                sc.op("dve", lambda v: v.tensor_tensor(out=kf[:], in0=kf[:], in1=E2[:], op=ALU.mult), reads=[bkf, bE2], writes=[bkf])
                sc.op("dve", lambda v: v.tensor_tensor(out=kt[:], in0=kf[:], in1=omF[:], op=ALU.mult), reads=[bkf, bomF], writes=[bkt2])
            if self.hg_parts >= 3:
                sc.op("pe", [(lambda e, hh=hh: e.matmul(pA[:, hsl(hh)], lhsT=kt[:, hsl(hh)], rhs=qt[:, hsl(hh)], start=True, stop=True)) for hh in range(HH)],
                      reads=[bkt2, bqt], writes=bpA)
                sc.op("dve", lambda v: v.tensor_tensor(out=scT[:], in0=pA[:], in1=Mc[:], op=ALU.mult), reads=bpA + [bK], writes=[bsc])
                for cc in range(2):
                    cs = slice(cc * 64, cc * 64 + 64)
                    for hh in range(HH):
                        sc.op("pool", lambda v, hh=hh, cc=cc: v.tensor_scalar_mul(out=Sb[:, hh, :], in0=St[:, hh, :], scalar1=EX[:, hh * 4 + cc:hh * 4 + cc + 1]),
                              reads=[bS[hh], bEX], writes=[bSb[hh]])
                    fns = []
                    for hh in range(HH):
                        col = slice(hh * 128 + cc * 64, hh * 128 + cc * 64 + 64)
                        fns.append(lambda e, hh=hh, col=col, cs=cs: e.matmul(pB[:, col], lhsT=vtok[cs, hsl(hh)], rhs=scT[cs, col], start=True, stop=False))
                        fns.append(lambda e, hh=hh, col=col: e.matmul(pB[:, col], lhsT=Sb[:, hh, :], rhs=qt[:, col], start=False, stop=True))
                    sc.op("pe", fns, reads=[bvt, bsc, bqt] + bSb, writes=bpB)
                    sc.op("pe", [(lambda e, hh=hh, cs=cs: e.matmul(pX[:, hsl(hh)], lhsT=khat[cs, hsl(hh)], rhs=vtok[cs, hsl(hh)], start=True, stop=True)) for hh in range(HH)],
                          reads=[bkh, bvt], writes=bpX)
                    for hh in range(HH):
                        sc.op("dve", lambda v, hh=hh, cc=cc: v.scalar_tensor_tensor(out=St[:, hh, :], in0=St[:, hh, :], scalar=EX[:, hh * 4 + 2 + cc:hh * 4 + 3 + cc],
                                                                                      in1=pX[:, hsl(hh)], op0=ALU.mult, op1=ALU.add),
                              reads=[bS[hh], bEX] + bpX, writes=[bS[hh]])
                sc.op("dve", lambda v: v.tensor_copy(out=oT[:], in_=pB[:]), reads=bpB, writes=[boT])
                sc.op("act", lambda a: a.activation(out=osq[:], in_=oT[:], func=AF.Square), reads=[boT], writes=[bosq])
                sc.op("pe", [(lambda e, hf=hf: e.matmul(pA[:, hf * 512:(hf + 1) * 512], lhsT=o128[:], rhs=osq[:, hf * 512:(hf + 1) * 512], start=True, stop=True))
                             for hf in range(2)], reads=[bosq, bK], writes=bpA)
                sc.op("act", lambda a: a.activation(out=ort[:], in_=pA[:], func=AF.Sqrt, bias=self.eps_s[:, 0:1], scale=1.0), reads=bpA + [self.b_eps], writes=[bort])
                sc.op("dve", lambda v: v.reciprocal(out=osq[:], in_=ort[:]), reads=[bort], writes=[bosq])
                sc.op("dve", lambda v: v.scalar_tensor_tensor(out=ort[:], in0=oT[:], scalar=gg[:, j:j + 1], in1=osq[:], op0=ALU.mult, op1=ALU.mult),
                      reads=[boT, bosq, bK], writes=[bort])
                sc.op("dve", lambda v: v.tensor_tensor(out=on[:], in0=ort[:], in1=sgg[:], op=ALU.mult), reads=[bort, bsgg], writes=[bon])
                sc.op("pe", [(lambda e, o=o, c=c: e.matmul(pX[:, hsl(o)], lhsT=wo[:, c, o * 128:(o + 1) * 128], rhs=on[:, hsl(c)], start=(c == 0), stop=(c == C - 1)))
                             for o in range(C) for c in range(C)], reads=[bon, bwo], writes=bpX)
                for o in range(C):
                    sc.op("dve", lambda v, i=i, o=o: v.tensor_tensor(out=x[i][:, o, :], in0=pX[:, hsl(o)], in1=x[i][:, o, :], op=ALU.add),
                          reads=bpX + [bx[i]], writes=[bx[i]])
            dst = self.xT[:, t * NB:(t + 1) * NB].rearrange("(c p) n -> p c n", p=128)
            stores.append(sc.dma("sp", lambda q, i=i, dst=dst: q.dma_start(out=dst, in_=x[i][:]), reads=[bx[i]], writes=[self.b_xT], key=f"st{i}"))
        self.b_xT.r = []
        self._xT_tokens = stores

    def _wait_xT(self, eng):
        self.sc._need(eng, self._xT_tokens)

    def norm(self, x, bx, h, bh, n, gcol, scr):
        sc = self.sc
        sq, bsq, pss, bpss, rt, brt, rs, brs = (scr[k] for k in ("sq", "bsq", "pss", "bpss", "rt", "brt", "rs", "brs"))
        sc.op("act", lambda a: a.activation(out=sq[:, :, :n], in_=x[:, :, :n], func=AF.Square), reads=[bx], writes=[bsq])
        fns = [(lambda e, c=c: e.matmul(pss[:, :n], lhsT=self.ones_s[:], rhs=sq[:, c, :n], start=(c == 0), stop=(c == C - 1)))
               for c in range(C)]
        sc.op("pe", fns, reads=[bsq, self.b_ones], writes=[bpss])
        sc.op("act", lambda a: a.activation(out=rt[:, :n], in_=pss[:, :n], func=AF.Sqrt, bias=self.eps_s[:, 0:1], scale=1.0),
              reads=[bpss, self.b_eps], writes=[brt])
        sc.op("dve", lambda v: v.reciprocal(out=rs[:, :n], in_=rt[:, :n]), reads=[brt], writes=[brs])
        for c in range(C):
            eng = "dve"
            sc.op(eng, lambda v, c=c: v.scalar_tensor_tensor(out=h[:, c, :n], in0=x[:, c, :n],
                                                             scalar=self.ng_s[:, gcol + c:gcol + c + 1], in1=rs[:, :n],
                                                             op0=ALU.mult, op1=ALU.mult),
                  reads=[bx, brs, self.b_ng], writes=[bh])

    def phase_mlp(self, l):
        sc = self.sc
        n = 256
        FC = DFF // 128
        w1s = self.sb(f"w1s_{l}", [128, C, DFF], BF16)
        w2s = self.sb(f"w2s_{l}", [128, FC, D], BF16)
        bw1 = [Buf(f"w1_{l}")] * 8
        bw2 = [Buf(f"w2_{l}_{i}") for i in range(8)]
        for c in range(C):
            sc.dma("pool", lambda q, c=c: q.dma_start(out=w1s[:, c, :], in_=self.w1[l, c * 128:(c + 1) * 128, :]), writes=[bw1[0]], key="w1")
        for fb in range(8):
            f0 = fb * 4
            src = self.w2[l, f0 * 128:(f0 + 4) * 128, :].rearrange("(f p) d -> p f d", p=128)
            sc.dma("pool", lambda q, f0=f0, src=src: q.dma_start(out=w2s[:, f0:f0 + 4, :], in_=src), writes=[bw2[fb]], key=f"w2_{fb}")
        NX = 3
        x = [self.sb(f"m{l}_x{i}", [128, C, n], F32) for i in range(NX)]
        h = [self.sb(f"m{l}_h{i}", [128, C, n], BF16) for i in range(2)]
        aT = [self.sb(f"m{l}_a{i}", [128, FC, n], BF16) for i in range(2)]
        r32 = [self.sb(f"m{l}_r{i}", [128, n], F32) for i in range(2)]
        scr = dict(sq=self.sb(f"m{l}_sq", [128, C, n], F32), bsq=Buf("sq"), pss=self.ps(f"m{l}_pss", [128, 512]), bpss=Buf("pss"),
                   rt=self.sb(f"m{l}_rt", [128, n], F32), brt=Buf("rt"), rs=self.sb(f"m{l}_rs", [128, n], F32), brs=Buf("rs"))
        pa = [self.ps(f"m{l}_pa{i}", [128, 512]) for i in range(3)]
        py = [self.ps(f"m{l}_py{i}", [128, 512]) for i in range(2)]
        bx = [Buf(f"mx{i}") for i in range(NX)]
        bh = [Buf(f"mh{i}") for i in range(2)]
        ba = [[Buf(f"ma{i}_{f}") for f in range(FC)] for i in range(2)]
        br = [Buf(f"mr{i}") for i in range(2)]
        bpa = [Buf(f"mpa{i}") for i in range(3)]
        bpy = [Buf(f"mpy{i}") for i in range(2)]
        stores = []
        gcol = (l * 2 + 1) * C
        T = S // n

        def load(t):
            src = self.xT[:, t * n:(t + 1) * n].rearrange("(c p) n -> p c n", p=128)
            self._wait_xT("sp")
            sc.dma("sp", lambda q, src=src: q.dma_start(out=x[t % NX][:], in_=src), writes=[bx[t % NX]], key=f"mx{t % NX}")

        def stage1(t):
            hi, ai = t % 2, t % 2
            for f in range(FC):
                k = f % 3
                fns = [(lambda e, f=f, c=c, k=k: e.matmul(pa[k][:, :n], lhsT=w1s[:, c, f * 128:(f + 1) * 128], rhs=h[hi][:, c, :],
                                                         start=(c == 0), stop=(c == C - 1))) for c in range(C)]
                sc.op("pe", fns, reads=[bh[hi], bw1[f // 4]], writes=[bpa[k]])
                j = f % 2
                sc.op("act", lambda a, j=j, k=k: a.activation(out=r32[j][:], in_=pa[k][:, :n], func=AF.Relu), reads=[bpa[k]], writes=[br[j]])
                eng = "dve" if f % 2 == 0 else "pool"
                sc.op(eng, lambda v, j=j, f=f: v.tensor_tensor(out=aT[ai][:, f, :], in0=r32[j][:], in1=r32[j][:], op=ALU.mult),
                      reads=[br[j]], writes=[ba[ai][f]])

        def stage2(t):
            xi, ai = t % NX, t % 2
            for o in range(C):
                k = o % 2
                for fb in range(8):
                    fns = [(lambda e, o=o, f=f, k=k: e.matmul(py[k][:, :n], lhsT=w2s[:, f, o * 128:(o + 1) * 128], rhs=aT[ai][:, f, :],
                                                             start=(f == 0), stop=(f == FC - 1))) for f in range(fb * 4, fb * 4 + 4)]
                    sc.op("pe", fns, reads=ba[ai][fb * 4:fb * 4 + 4] + [bw2[fb]], writes=[bpy[k]])
                sc.op("dve", lambda v, o=o, k=k: v.tensor_tensor(out=x[xi][:, o, :], in0=py[k][:, :n], in1=x[xi][:, o, :], op=ALU.add),
                      reads=[bpy[k], bx[xi]], writes=[bx[xi]])
            dst = self.xT[:, t * n:(t + 1) * n].rearrange("(c p) n -> p c n", p=128)
            stores.append(sc.dma("sp", lambda q, dst=dst: q.dma_start(out=dst, in_=x[xi][:]), reads=[bx[xi]], writes=[self.b_xT], key=f"st{t % 2}"))

        load(0)
        self.norm(x[0], bx[0], h[0], bh[0], n, gcol, scr)
        for t in range(T):
            if t + 1 < T:
                load(t + 1)
            stage1(t)
            if t + 1 < T:
                self.norm(x[(t + 1) % NX], bx[(t + 1) % NX], h[(t + 1) % 2], bh[(t + 1) % 2], n, gcol, scr)
            if t >= 1:
                stage2(t - 1)
        stage2(T - 1)
        self.b_xT.r = []
        self._xT_tokens = stores


_CACHE = {}
HG_PAD = 0
ATT_SUB = "dve"
SELFWAIT_ENGINES = ()
HG_DEBUG = ""


def _consts_np():
    blk = np.zeros((128, 128), np.float32)
    blk[:64, :64] = 1.0 / 64
    blk[64:, 64:] = 1.0 / 64
    q = np.arange(128)[:, None]; s_ = np.arange(128)[None, :]
    mask = np.where(s_ >= q, -240000.0, 0.0).astype(np.float32)
    p = np.arange(128); ch = p // 64; u = p % 64
    same = ch[:, None] == ch[None, :]
    tri = same & (p[:, None] <= p[None, :])
    mid = same & (u[:, None] <= 31)
    M1 = np.zeros((128, 256), np.float32)
    M1[:, :128] = tri.astype(np.float32) - mid.astype(np.float32)
    for c in range(2):
        M1[:, 128 + c] = ((ch == c) & (u <= 31)).astype(np.float32)
        M1[:, 130 + c] = (ch == c).astype(np.float32)
    M2 = (same & (p[:, None] > p[None, :])).astype(np.float32)
    Mc = (same & (p[:, None] <= p[None, :])).astype(np.float32)
    return {"ident": np.eye(128, dtype=np.float32), "onesD": np.full((128, 128), 1.0 / D, np.float32),
            "blk64": blk, "identb": np.eye(128, dtype=np.float32), "maskneg": mask,
            "hg_M1": M1, "hg_M2": M2, "hg_Mc": np.ascontiguousarray(np.tile(Mc, (1, 8))), "ones128": np.full((128, 128), 1.0 / 128, np.float32)}


def kernel(x, norm_gains, sb_w_qkv, sb_q_gain, sb_k_gain, sb_w_o, hg_w_in, hg_lb_logits, hg_norm_gain, hg_w_o,
           mlp_w1, mlp_w2, _layers=(0, 1, 2, 3), _mixers=None, _mlps=None, _hg_blocks=None, _hg_parts=3, _allow_hgrn=False, _cores=None, _return_maps=False):
    key = (tuple(_layers), None if _mixers is None else tuple(_mixers), None if _mlps is None else tuple(_mlps), _hg_blocks, _hg_parts, _allow_hgrn, HG_DEBUG, HG_PAD, SELFWAIT_ENGINES, ATT_SUB)
    if key not in _CACHE:
        pr = Prog(list(_layers), _mixers, _mlps, hg_blocks=_hg_blocks, hg_parts=_hg_parts, allow_hgrn=_allow_hgrn)
        _CACHE[key] = (pr.build(), pr.used_inputs)
    nc, used = _CACHE[key]
    cst = _consts_np()
    ng = np.ascontiguousarray(np.asarray(norm_gains, np.float32).reshape(DEPTH * 2, C, 128).transpose(2, 0, 1).reshape(128, DEPTH * 2 * C))
    qg, kg = np.asarray(sb_q_gain, np.float32), np.asarray(sb_k_gain, np.float32)
    qk = np.stack([np.tile(qg[0], 2), np.tile(kg[0], 2), np.tile(qg[1], 2), np.tile(kg[1], 2)], axis=1)
    lbl = np.ascontiguousarray(hg_lb_logits, np.float32)
    lblT = np.ascontiguousarray(lbl.reshape(2, C, 128).transpose(2, 0, 1).reshape(128, 2 * C))
    shared = {"hg_w_in": np.ascontiguousarray(hg_w_in, np.float32), "hg_w_o": np.ascontiguousarray(hg_w_o, np.float32),
              "hg_lb_logits": lbl, "hg_lb_logits_T": lblT, "hg_gain_T": np.ascontiguousarray(np.asarray(hg_norm_gain, np.float32).T),
              "qk_gain": np.ascontiguousarray(qk), "sb_w_qkv": np.ascontiguousarray(sb_w_qkv, np.float32),
              "sb_w_o": np.ascontiguousarray(sb_w_o, np.float32), "norm_gains": ng, "mlp_w1": np.ascontiguousarray(mlp_w1, np.float32), "mlp_w2": np.ascontiguousarray(mlp_w2, np.float32), **cst}
    shared = {k: v for k, v in shared.items() if k in used}
    nb = _cores or 8
    in_maps = [dict(shared, x=np.ascontiguousarray(x[b], np.float32)) for b in range(nb)]
    if _return_maps:
        return nc, in_maps
    res = run_bass_kernel_spmd(nc, in_maps, core_ids=list(range(nb)))
    return np.stack([r["out"] for r in res.results], axis=0)
```

```python
import contextlib
import numpy as np
import concourse.bass as bass
import concourse.mybir as mybir
from concourse.bass_utils import run_bass_kernel_spmd

F32 = mybir.dt.float32
BF16 = mybir.dt.bfloat16
AF = mybir.ActivationFunctionType
ALU = mybir.AluOpType

D = 1024
S = 4096
DEPTH = 4
NH = 16
DH = 64
DFF = 4096
C = D // 128
NT = 512
EPS = 1e-6


class Buf:
    def __init__(self, name):
        self.name = name
        self.w = None
        self.r = []


class Sched:
    ENG = ("pe", "act", "dve", "pool", "sp")

    def __init__(self, nc, stack):
        self.nc = nc
        self.stack = stack
        self.sem = {e: stack.enter_context(nc.semaphore("s_" + e)) for e in self.ENG if e != "sp"}
        self.cnt = {e: 0 for e in self.sem}
        self.prog = {e: [] for e in self.ENG}
        self.seen = {e: {} for e in self.ENG}
        self.dsem = {}
        self.dcnt = {}

    def _need(self, eng, toks):
        best = {}
        for t in toks:
            if t is None:
                continue
            s, v = t
            if v > best.get(id(s), (s, 0))[1]:
                best[id(s)] = (s, v)
        for s, v in best.values():
            if self.seen[eng].get(id(s), 0) < v:
                self.seen[eng][id(s)] = v
                self.prog[eng].append(("wait", s, v))

    def _deps(self, reads, writes):
        toks = [b.w for b in reads]
        for b in writes:
            toks.append(b.w)
            toks.extend(b.r)
        return toks

    def op(self, eng, fns, reads=(), writes=()):
        if callable(fns):
            fns = [fns]
        n0 = len(self.prog[eng])
        deps = self._deps(reads, writes)
        if eng == "pe":
            deps = [t for t in deps if t is not None and t[0] is not self.sem["pe"]]
        self._need(eng, deps)
        if eng in SELFWAIT_ENGINES and len(self.prog[eng]) == n0 and self.cnt[eng] > 0:
            self.seen[eng][id(self.sem[eng])] = self.cnt[eng]
            self.prog[eng].append(("wait", self.sem[eng], self.cnt[eng]))
        self.cnt[eng] += 1
        tok = (self.sem[eng], self.cnt[eng])
        self.prog[eng].append(("op", fns, self.sem[eng], 1))
        for b in reads:
            b.r.append(tok)
        for b in writes:
            b.w = tok
            b.r = []
        return tok

    def dma(self, q, fn, reads=(), writes=(), key=None):
        key = q + "_" + (key or writes[0].name)
        if key not in self.dsem:
            self.dsem[key] = self.stack.enter_context(self.nc.semaphore("d_" + key))
            self.dcnt[key] = 0
        self._need(q, self._deps(reads, writes))
        self.dcnt[key] += 16
        tok = (self.dsem[key], self.dcnt[key])
        self.prog[q].append(("op", [fn], self.dsem[key], 16))
        for b in reads:
            b.r.append(tok)
        for b in writes:
            b.w = tok
            b.r = []
        return tok

    def barrier(self):
        toks = [(self.sem[e], self.cnt[e]) for e in self.sem if self.cnt[e]]
        toks += [(self.dsem[k], self.dcnt[k]) for k in self.dsem if self.dcnt[k]]
        for e in self.ENG:
            self._need(e, toks)

    def emit(self):
        nc = self.nc
        engobj = {"pe": "tensor", "act": "scalar", "dve": "vector", "pool": "gpsimd", "sp": "sync"}
        with nc.Block() as block:
            for e in self.ENG:
                prog = self.prog[e]

                def body(eng, prog=prog):
                    for item in prog:
                        if item[0] == "wait":
                            eng.wait_ge(item[1], item[2])
                        else:
                            _, fns, sem, inc = item
                            ins = None
                            n_before = nc.n_instructions() if inc == 16 else 0
                            for f in fns:
                                ins = f(eng)
                            if inc == 16 and nc.n_instructions() - n_before != 1:
                                raise RuntimeError(f"dma_start was split into {nc.n_instructions() - n_before} instructions: the tracker's +16 per DMA "
                                                   f"would be wrong (each piece adds 16). Reshape this transfer.")
                            ins.then_inc(sem, inc)
                getattr(block, engobj[e])(body)
        self.prog = {e: [] for e in self.ENG}


class Prog:
    def __init__(self, layers, mixers=None, mlps=None, hg_blocks=None, hg_parts=3, allow_hgrn=False):
        self.allow_hgrn = allow_hgrn
        self.hg_blocks = hg_blocks
        self.hg_parts = hg_parts
        self.layers = layers
        self.mixers = layers if mixers is None else mixers
        self.mlps = layers if mlps is None else mlps

    def build(self):
        nc = bass.Bass("TRN2", target_bir_lowering=False)
        self.nc = nc
        nc.allow_low_precision("bf16 matmul operands with fp32 PSUM accumulation (problem tolerance is set for this)")
        dt = nc.dram_tensor
        self.x_in = dt("x", [S, D], F32, kind="ExternalInput").ap()
        self.out = dt("out", [S, D], F32, kind="ExternalOutput").ap()
        self.ng = dt("norm_gains", [128, DEPTH * 2 * C], F32, kind="ExternalInput").ap()
        self.ident = dt("ident", [128, 128], F32, kind="ExternalInput").ap()
        self.onesD = dt("onesD", [128, 128], F32, kind="ExternalInput").ap()
        sbm = [l for l in self.layers if l in self.mixers and l % 2 == 0]
        hgm = [l for l in self.layers if l in self.mixers and l % 2 == 1]
        mlm = [l for l in self.layers if l in self.mlps]
        self.used_inputs = {"x", "norm_gains", "ident", "onesD"}
        self.xT = dt("xT_scratch", [D, S], F32, kind="Internal").ap()
        if mlm:
            self.w1 = dt("mlp_w1", [DEPTH, D, DFF], F32, kind="ExternalInput").ap()
            self.w2 = dt("mlp_w2", [DEPTH, DFF, D], F32, kind="ExternalInput").ap()
            self.used_inputs |= {"mlp_w1", "mlp_w2"}
        if sbm or hgm:
            self.identb = dt("identb", [128, 128], F32, kind="ExternalInput").ap()
            self.used_inputs |= {"identb"}
        if sbm:
            self.wqkv = dt("sb_w_qkv", [2, D, 3 * D], F32, kind="ExternalInput").ap()
            self.wo_sb = dt("sb_w_o", [2, D, D], F32, kind="ExternalInput").ap()
            self.qkg = dt("qk_gain", [128, 4], F32, kind="ExternalInput").ap()
            self.blk64 = dt("blk64", [128, 128], F32, kind="ExternalInput").ap()
            self.maskneg = dt("maskneg", [128, 128], F32, kind="ExternalInput").ap()
            self.QT = dt("QT_scratch", [D, S], BF16, kind="Internal").ap()
            self.KT = dt("KT_scratch", [D, S], BF16, kind="Internal").ap()
            self.V = dt("V_scratch", [S, D], BF16, kind="Internal").ap()
            self.used_inputs |= {"sb_w_qkv", "sb_w_o", "qk_gain", "blk64", "maskneg"}
        if hgm:
            self.w_in = dt("hg_w_in", [2, D, 4 * D], F32, kind="ExternalInput").ap()
            self.wo_hg = dt("hg_w_o", [2, D, D], F32, kind="ExternalInput").ap()
            self.lbl = dt("hg_lb_logits", [2, D], F32, kind="ExternalInput").ap()
            self.lblT = dt("hg_lb_logits_T", [128, 2 * C], F32, kind="ExternalInput").ap()
            self.hgg = dt("hg_gain_T", [128, 2], F32, kind="ExternalInput").ap()
            self.M1 = dt("hg_M1", [128, 256], F32, kind="ExternalInput").ap()
            self.M2 = dt("hg_M2", [128, 128], F32, kind="ExternalInput").ap()
            self.Mc = dt("hg_Mc", [128, 1024], F32, kind="ExternalInput").ap()
            self.ones128 = dt("ones128", [128, 128], F32, kind="ExternalInput").ap()
            self.used_inputs |= {"hg_w_in", "hg_w_o", "hg_lb_logits", "hg_lb_logits_T", "hg_gain_T", "hg_M1", "hg_M2", "hg_Mc", "ones128"}
        with contextlib.ExitStack() as st:
            self.st = st
            self.sc = Sched(nc, st)
            self.ph = st
            self._consts()
            self.run_phase(self.phase_in)
            for l in self.layers:
                if l in self.mixers:
                    if l % 2 == 0:
                        self.run_phase(self.phase_qkv, l)
                        self.run_phase(self.phase_attn, l)
                    else:
                        self.run_phase(self.phase_hgrn, l)
                if l in self.mlps:
                    self.run_phase(self.phase_mlp, l)
            self.run_phase(self.phase_out)
        return nc

    def _uniq(self, name):
        self._nalloc = getattr(self, "_nalloc", 0) + 1
        return f"{name}_{self._nalloc}"

    def sb(self, name, shape, dtype):
        return self.ph.enter_context(self.nc.sbuf_tensor(self._uniq(name), shape, dtype))

    def ps(self, name, shape, dtype=F32):
        return self.ph.enter_context(self.nc.psum_tensor(self._uniq(name), shape, dtype))

    def run_phase(self, fn, *a):
        with contextlib.ExitStack() as ph:
            keep, self.ph = self.ph, ph
            fn(*a)
            self.sc.barrier()
            self.sc.emit()
            self.ph = keep

    def _consts(self):
        sc = self.sc
        self.ident_s = self.sb("ident_s", [128, 128], F32)
        self.ones_s = self.sb("ones_s", [128, 128], F32)
        self.ng_s = self.sb("ng_s", [128, DEPTH * 2 * C], F32)
        self.eps_s = self.sb("eps_s", [128, 1], F32)
        self.b_const = Buf("consts")
        sc.dma("sp", lambda q: q.dma_start(out=self.ident_s[:], in_=self.ident[:, :]), writes=[self.b_const], key="c0")
        b1 = Buf("c1"); b2 = Buf("c2"); self.b_eps = Buf("eps")
        sc.dma("sp", lambda q: q.dma_start(out=self.ones_s[:], in_=self.onesD[:, :]), writes=[b1], key="c1")
        sc.dma("sp", lambda q: q.dma_start(out=self.ng_s[:], in_=self.ng[:, :]), writes=[b2], key="c2")
        sc.op("dve", lambda v: v.memset(self.eps_s[:], EPS), writes=[self.b_eps])
        self.b_ones = b1
        self.b_ng = b2
        self.b_xT = Buf("xT_dram")

    def phase_in(self):
        sc, nc = self.sc, self.nc
        xin = [self.sb(f"pin_x{i}", [128, 4, D], F32) for i in range(2)]
        xo = [self.sb(f"pin_o{i}", [128, C, NT], F32) for i in range(2)]
        pst = [self.ps(f"pin_ps{i}", [128, NT]) for i in range(4)]
        b_in = [Buf(f"pin_x{i}") for i in range(2)]
        b_o = [Buf(f"pin_o{i}") for i in range(2)]
        b_ps = [Buf(f"pin_ps{i}") for i in range(4)]
        stores = []
        for t in range(S // NT):
            i = t % 2
            src = self.x_in[t * NT:(t + 1) * NT, :].rearrange("(j p) d -> p j d", p=128)
            sc.dma("sp", lambda q, i=i, src=src: q.dma_start(out=xin[i][:], in_=src), writes=[b_in[i]])
            for c in range(C):
                k = c % 4
                fns = [(lambda e, i=i, c=c, j=j, k=k: e.matmul(pst[k][:, j * 128:(j + 1) * 128],
                                                              lhsT=xin[i][:, j, c * 128:(c + 1) * 128],
                                                              rhs=self.ident_s[:], start=True, stop=True))
                       for j in range(4)]
                sc.op("pe", fns, reads=[b_in[i], self.b_const], writes=[b_ps[k]])
                eng = "dve" if c % 2 == 0 else "act"
                if eng == "dve":
                    sc.op("dve", lambda v, i=i, c=c, k=k: v.tensor_copy(out=xo[i][:, c, :], in_=pst[k][:]),
                          reads=[b_ps[k]], writes=[b_o[i]])
                else:
                    sc.op("act", lambda a, i=i, c=c, k=k: a.copy(out=xo[i][:, c, :], in_=pst[k][:]),
                          reads=[b_ps[k]], writes=[b_o[i]])
            dst = self.xT[:, t * NT:(t + 1) * NT].rearrange("(c p) n -> p c n", p=128)
            stores.append(sc.dma("sp", lambda q, i=i, dst=dst: q.dma_start(out=dst, in_=xo[i][:]),
                                 reads=[b_o[i]], writes=[self.b_xT], key=f"st{i}"))
        self.b_xT.r = []
        self._xT_tokens = stores

    def phase_out(self):
        sc = self.sc
        xi = [self.sb(f"po_x{i}", [128, C, NT], F32) for i in range(2)]
        xo = [self.sb(f"po_o{i}", [128, 4, D], F32) for i in range(2)]
        pst = [self.ps(f"po_ps{i}", [128, NT]) for i in range(4)]
        b_i = [Buf(f"po_x{i}") for i in range(2)]
        b_o = [Buf(f"po_o{i}") for i in range(2)]
        b_ps = [Buf(f"po_ps{i}") for i in range(4)]
        b_out = Buf("out_dram")
        toks = []
        for t in range(S // NT):
            i = t % 2
            src = self.xT[:, t * NT:(t + 1) * NT].rearrange("(c p) n -> p c n", p=128)
            self._wait_xT("sp")
            sc.dma("sp", lambda q, i=i, src=src: q.dma_start(out=xi[i][:], in_=src), writes=[b_i[i]])
            n = 0
            for j in range(4):
                for h in range(2):
                    k = n % 4
                    fns = [(lambda e, i=i, j=j, c=h * 4 + cc, cc=cc, k=k:
                            e.matmul(pst[k][:, cc * 128:(cc + 1) * 128], lhsT=xi[i][:, c, j * 128:(j + 1) * 128],
                                     rhs=self.ident_s[:], start=True, stop=True)) for cc in range(4)]
                    sc.op("pe", fns, reads=[b_i[i], self.b_const], writes=[b_ps[k]])
                    if n % 2 == 0:
                        sc.op("dve", lambda v, i=i, j=j, h=h, k=k: v.tensor_copy(out=xo[i][:, j, h * 512:(h + 1) * 512], in_=pst[k][:]),
                              reads=[b_ps[k]], writes=[b_o[i]])
                    else:
                        sc.op("act", lambda a, i=i, j=j, h=h, k=k: a.copy(out=xo[i][:, j, h * 512:(h + 1) * 512], in_=pst[k][:]),
                              reads=[b_ps[k]], writes=[b_o[i]])
                    n += 1
            dst = self.out[t * NT:(t + 1) * NT, :].rearrange("(j p) d -> p j d", p=128)
            toks.append(sc.dma("sp", lambda q, i=i, dst=dst: q.dma_start(out=dst, in_=xo[i][:]),
                               reads=[b_o[i]], writes=[b_out], key=f"st{i}"))
        self.sc._need("sp", toks)

    def phase_qkv(self, l):
        sc = self.sc
        j = l // 2
        n = NT
        wq = self.sb("wqkv_s", [128, C, 3 * D], BF16)
        bw = Buf("wqkv")
        for c in range(C):
            sc.dma("pool", lambda q, c=c: q.dma_start(out=wq[:, c, :], in_=self.wqkv[j, c * 128:(c + 1) * 128, :]), writes=[bw], key="w1")
        blk = self.sb("blk_s", [128, 128], F32); bblk = Buf("blk")
        qkg = self.sb("qkg_s", [128, 4], F32); bg = Buf("qkg")
        sc.dma("sp", lambda q: q.dma_start(out=blk[:], in_=self.blk64[:, :]), writes=[bblk], key="c0")
        sc.dma("sp", lambda q: q.dma_start(out=qkg[:], in_=self.qkg[:, :]), writes=[bg], key="c1")
        x = [self.sb(f"q_x{i}", [128, C, n], F32) for i in range(2)]
        h = [self.sb(f"q_h{i}", [128, C, n], BF16) for i in range(2)]
        scr = dict(sq=self.sb("q_sq", [128, C, n], F32), bsq=Buf("sq"), pss=self.ps("q_pss", [128, 512]), bpss=Buf("pss"),
                   rt=self.sb("q_rt", [128, n], F32), brt=Buf("rt"), rs=self.sb("q_rs", [128, n], F32), brs=Buf("rs"))
        pp = [self.ps(f"q_pp{i}", [128, 512]) for i in range(2)]
        pm = [self.ps(f"q_pm{i}", [128, 512]) for i in range(2)]
        pv = [self.ps(f"q_pv{i}", [128, 512]) for i in range(2)]
        bpp = [Buf(f"pp{i}") for i in range(2)]; bpm = [Buf(f"pm{i}") for i in range(2)]; bpv = [Buf(f"pv{i}") for i in range(2)]
        sq2 = [self.sb(f"q_sq2{i}", [128, n], F32) for i in range(2)]; bsq2 = [Buf(f"sq2{i}") for i in range(2)]
        rt2 = [self.sb(f"q_rt2{i}", [128, n], F32) for i in range(2)]; brt2 = [Buf(f"rt2{i}") for i in range(2)]
        rs2 = [self.sb(f"q_rs2{i}", [128, n], F32) for i in range(2)]; brs2 = [Buf(f"rs2{i}") for i in range(2)]
        qo = [self.sb(f"q_qo{i}", [128, n], BF16) for i in range(4)]; bqo = [Buf(f"qo{i}") for i in range(4)]
        vt = [self.sb(f"q_vt{i}", [128, 4, D], BF16) for i in range(2)]; bvt = [Buf(f"vt{i}") for i in range(2)]
        bx = [Buf(f"qx{i}") for i in range(2)]; bh = [Buf(f"qh{i}") for i in range(2)]
        b_scr = Buf("qkv_dram")
        gcol = (l * 2) * C
        nq = 0
        for t in range(S // n):
            i = t % 2
            src = self.xT[:, t * n:(t + 1) * n].rearrange("(c p) n -> p c n", p=128)
            self._wait_xT("sp")
            sc.dma("sp", lambda q, i=i, src=src: q.dma_start(out=x[i][:], in_=src), writes=[bx[i]])
            self.norm(x[i], bx[i], h[i], bh[i], n, gcol, scr)
            for m in range(16):
                k = m % 2
                col = m * 128
                fns = [(lambda e, i=i, c=c, k=k, col=col: e.matmul(pp[k][:], lhsT=wq[:, c, col:col + 128], rhs=h[i][:, c, :],
                                                                   start=(c == 0), stop=(c == C - 1))) for c in range(C)]
                sc.op("pe", fns, reads=[bh[i], bw], writes=[bpp[k]])
                sc.op("act", lambda a, k=k: a.activation(out=sq2[k][:], in_=pp[k][:], func=AF.Square), reads=[bpp[k]], writes=[bsq2[k]])
                sc.op("pe", lambda e, k=k: e.matmul(pm[k][:], lhsT=blk[:], rhs=sq2[k][:], start=True, stop=True),
                      reads=[bsq2[k], bblk], writes=[bpm[k]])
                sc.op("act", lambda a, k=k: a.activation(out=rt2[k][:], in_=pm[k][:], func=AF.Sqrt, bias=self.eps_s[:, 0:1], scale=1.0),
                      reads=[bpm[k], self.b_eps], writes=[brt2[k]])
                sc.op("dve", lambda v, k=k: v.reciprocal(out=rs2[k][:], in_=rt2[k][:]), reads=[brt2[k]], writes=[brs2[k]])
                o = nq % 4; nq += 1
                gc = j * 2 + (0 if m < 8 else 1)
                sc.op("dve", lambda v, k=k, o=o, gc=gc: v.scalar_tensor_tensor(out=qo[o][:], in0=pp[k][:], scalar=qkg[:, gc:gc + 1], in1=rs2[k][:],
                                                                                  op0=ALU.mult, op1=ALU.mult),
                      reads=[bpp[k], brs2[k], bg], writes=[bqo[o]])
                dstT = (self.QT if m < 8 else self.KT)[(m % 8) * 128:(m % 8 + 1) * 128, t * n:(t + 1) * n]
                sc.dma("sp", lambda q, o=o, dstT=dstT: q.dma_start(out=dstT, in_=qo[o][:]), reads=[bqo[o]], writes=[], key=f"qst{o}")
            for jj in range(4):
                for half in range(2):
                    k = (jj * 2 + half) % 2
                    fns = [(lambda e, i=i, c=c, k=k, jj=jj, half=half: e.matmul(pv[k][:], lhsT=h[i][:, c, jj * 128:(jj + 1) * 128],
                                                                                 rhs=wq[:, c, 2 * D + half * 512:2 * D + (half + 1) * 512],
                                                                                 start=(c == 0), stop=(c == C - 1))) for c in range(C)]
                    sc.op("pe", fns, reads=[bh[i], bw], writes=[bpv[k]])
                    sc.op("act", lambda a, i=i, k=k, jj=jj, half=half: a.copy(out=vt[i][:, jj, half * 512:(half + 1) * 512], in_=pv[k][:]),
                          reads=[bpv[k]], writes=[bvt[i]])
            dstV = self.V[t * n:(t + 1) * n, :].rearrange("(j p) d -> p j d", p=128)
            sc.dma("sp", lambda q, i=i, dstV=dstV: q.dma_start(out=dstV, in_=vt[i][:]), reads=[bvt[i]], writes=[], key=f"vst{i}")

    def phase_attn(self, l):
        sc = self.sc
        j = l // 2
        idb = self.sb("idb_s", [128, 128], BF16); bidb = Buf("idb")
        mk = self.sb("mk_s", [128, 128], BF16); bmk = Buf("mk")
        sc.dma("pool", lambda q: q.dma_start(out=idb[:], in_=self.identb[:, :]), writes=[bidb], key="c0")
        sc.dma("pool", lambda q: q.dma_start(out=mk[:], in_=self.maskneg[:, :]), writes=[bmk], key="c1")
        wo = self.sb("wo_s", [128, C, D], BF16); bwo = Buf("wo")
        for c0 in range(0, C, 4):
            src = self.wo_sb[j, c0 * 128:(c0 + 4) * 128, :].rearrange("(c p) d -> p c d", p=128)
            sc.dma("pool", lambda q, c0=c0, src=src: q.dma_start(out=wo[:, c0:c0 + 4, :], in_=src), writes=[bwo], key="w1")
        OT = self.sb("OT_s", [128, C, S], BF16)
        bOT = [Buf(f"OT{hp}") for hp in range(C)]
        po = [self.ps(f"a_po{i}", [128, 512]) for i in range(2)]; bpo = [Buf(f"po{i}") for i in range(2)]
        self.run_phase(self._attn_core, l, OT, bOT, po, bpo, idb, bidb, mk, bmk)
        self._attn_wo(l, OT, bOT, po, bpo, wo, bwo)

    def _attn_core(self, l, OT, bOT, po, bpo, idb, bidb, mk, bmk):
        sc = self.sc
        qT = [self.sb(f"a_q{i}", [128, S], BF16) for i in range(1)] * 2; bq = [Buf("aq")] * 2
        kT = [self.sb(f"a_k{i}", [128, S], BF16) for i in range(1)] * 2; bk = [Buf("ak")] * 2
        VA = [self.sb(f"a_va{i}", [128, 32, 128], BF16) for i in range(1)] * 2; bva = [Buf("va")] * 2
        VB = [self.sb(f"a_vb{i}", [128, 32, 128], BF16) for i in range(1)] * 2; bvb = [Buf("vb")] * 2
        for i in range(1):
            sc.op("pool", lambda v, i=i: v.memset(VA[i][:], 0.0), writes=[bva[i]])
            sc.op("pool", lambda v, i=i: v.memset(VB[i][:], 0.0), writes=[bvb[i]])
        CH = 1024
        g = [self.sb(f"a_g{i}", [128, CH], F32) for i in range(2)]; bgt = [Buf(f"ag{i}") for i in range(2)]
        Pf = [self.sb(f"a_P{i}", [128, S + 1], F32) for i in range(2)]; bP = [Buf(f"aP{i}") for i in range(2)]
        zer = self.sb("a_zero", [128, CH], F32); bz = Buf("zero")
        sc.op("pool", lambda v: v.memset(zer[:], 0.0), writes=[bz])
        w = [self.sb(f"a_w{i}", [128, CH], BF16) for i in range(2)]; bwt = [Buf(f"aw{i}") for i in range(2)]
        wT = [self.sb(f"a_wT{i}", [128, 8, 128], BF16) for i in range(2)]; bwT = [Buf(f"awT{i}") for i in range(2)]
        pz = [self.ps(f"a_pz{i}", [128, CH]) for i in range(2)]; bpz = [Buf(f"pz{i}") for i in range(2)]
        pt = self.ps("a_pt", [128, CH]); bpt = Buf("pt")
        chunks = []
        for hp in range(C):
            for Q in range(S // 128):
                t1 = 128 * (Q + 1)
                for hd in range(2):
                    b_ = t1
                    while b_ > 0:
                        a_ = max(0, b_ - CH)
                        chunks.append(dict(hp=hp, Q=Q, hd=hd, a=a_, b=b_, L=b_ - a_, t1=t1, first=(b_ == t1), newhp=(Q == 0 and hd == 0 and b_ == t1),
                                           first_pv=(hd == 0 and b_ == t1), last=(hd == 1 and a_ == 0),
                                           ip=(hp * 64 + Q * 2 + hd) % 2, io=(hp * 32 + Q) % 2))
                        b_ = a_
        for n_, ch in enumerate(chunks):
            ch["iz"] = n_ % 2

        def stage_A(ch):
            hp, Q, hd, a, b, L, iz = ch["hp"], ch["Q"], ch["hd"], ch["a"], ch["b"], ch["L"], ch["iz"]
            i = 0
            if ch["newhp"]:
                rows = slice(hp * 128, (hp + 1) * 128)
                sc.dma("sp", lambda q, rows=rows: q.dma_start(out=qT[i][:], in_=self.QT[rows, :]), writes=[bq[i]], key="ld0")
                sc.dma("sp", lambda q, rows=rows: q.dma_start(out=kT[i][:], in_=self.KT[rows, :]), writes=[bk[i]], key="ldk0")
                srcA = self.V[:, hp * 128:hp * 128 + 64].rearrange("(b p) d -> p b d", p=128)
                srcB = self.V[:, hp * 128 + 64:hp * 128 + 128].rearrange("(b p) d -> p b d", p=128)
                for b0 in range(0, 32, 8):
                    sc.dma("sp", lambda q, srcA=srcA, b0=b0: q.dma_start(out=VA[i][:, b0:b0 + 8, 0:64], in_=srcA[:, b0:b0 + 8, :]), writes=[bva[i]], key="lda")
                    sc.dma("sp", lambda q, srcB=srcB, b0=b0: q.dma_start(out=VB[i][:, b0:b0 + 8, 64:128], in_=srcB[:, b0:b0 + 8, :]), writes=[bvb[i]], key="ldb")
            pr = slice(hd * 64, hd * 64 + 64)
            fns = []
            for p0 in range(0, L, 512):
                pl = min(512, L - p0)
                diag = ch["first"] and (p0 + pl == L)
                fns.append(lambda e, p0=p0, pl=pl, diag=diag: e.matmul(pz[iz][:, p0:p0 + pl], lhsT=qT[i][pr, Q * 128:(Q + 1) * 128],
                                                                       rhs=kT[i][pr, a + p0:a + p0 + pl], start=True, stop=not diag))
                if diag:
                    fns.append(lambda e: e.matmul(pz[iz][:, L - 128:L], lhsT=idb[:], rhs=mk[:], start=False, stop=True))
            sc.op("pe", fns, reads=[bq[i], bk[i], bidb, bmk], writes=[bpz[iz]])
            sc.op("act", lambda a_: a_.activation(out=g[iz][:, :L], in_=pz[iz][:, :L], func=AF.Sigmoid, scale=-0.125), reads=[bpz[iz]], writes=[bgt[iz]])

        def stage_B(ch):
            a, b, L, iz, ip, t1 = ch["a"], ch["b"], ch["L"], ch["iz"], ch["ip"], ch["t1"]
            if ch["first"]:
                sc.op("dve", lambda v: v.memset(Pf[ip][:, t1:t1 + 1], 1.0), writes=[bP[ip]])
            sc.op("dve", lambda v: v.tensor_tensor_scan(out=Pf[ip][:, b - 1:(a - 1 if a > 0 else None):-1], data0=g[iz][:, L - 1::-1], data1=zer[:, :L],
                                                        initial=Pf[ip][:, b:b + 1], op0=ALU.mult, op1=ALU.add), reads=[bgt[iz], bP[ip], bz], writes=[bP[ip]])
            eng = {"pool": "pool", "dve": "dve", "alt": ("pool" if iz == 0 else "dve")}[ATT_SUB]
            sc.op(eng, lambda v: v.tensor_tensor(out=w[iz][:, :L], in0=Pf[ip][:, a + 1:b + 1], in1=Pf[ip][:, a:b], op=ALU.subtract),
                  reads=[bP[ip]], writes=[bwt[iz]])

        def stage_C1(ch):
            L, iz = ch["L"], ch["iz"]
            nb = L // 128
            fns = [(lambda e, bb=bb: e.matmul(pt[:, bb * 128:(bb + 1) * 128], lhsT=w[iz][:, bb * 128:(bb + 1) * 128], rhs=idb[:], start=True, stop=True))
                   for bb in range(nb)]
            sc.op("pe", fns, reads=[bwt[iz], bidb], writes=[bpt])
            sc.op("act", lambda a_: a_.copy(out=wT[iz][:, :nb, :], in_=pt[:, :nb * 128]), reads=[bpt], writes=[bwT[iz]])

        def stage_C2(ch):
            hp, Q, hd, a, L, iz, io = ch["hp"], ch["Q"], ch["hd"], ch["a"], ch["L"], ch["iz"], ch["io"]
            i = 0
            Vh, bV = (VA[i], bva[i]) if hd == 0 else (VB[i], bvb[i])
            nb = L // 128
            fns = []
            for bb in range(nb):
                kb = a // 128 + bb
                fns.append(lambda e, bb=bb, kb=kb, fp=(ch["first_pv"] and bb == 0), last=(ch["last"] and bb == nb - 1):
                           e.matmul(po[io][:, :128], lhsT=Vh[:, kb, :], rhs=wT[iz][:, bb, :], start=fp, stop=last))
            sc.op("pe", fns, reads=[bwT[iz], bV], writes=[bpo[io]])
            if ch["last"]:
                sc.op("dve", lambda v: v.tensor_copy(out=OT[:, hp, Q * 128:(Q + 1) * 128], in_=po[io][:, :128]), reads=[bpo[io]], writes=[bOT[hp]])

        nchk = len(chunks)
        for n_ in range(nchk + 3):
            if n_ < nchk:
                stage_A(chunks[n_])
            if 0 <= n_ - 1 < nchk:
                stage_B(chunks[n_ - 1])
            if 0 <= n_ - 2 < nchk:
                stage_C1(chunks[n_ - 2])
            if 0 <= n_ - 3 < nchk:
                stage_C2(chunks[n_ - 3])

    def _attn_wo(self, l, OT, bOT, po, bpo, wo, bwo):
        sc = self.sc
        n = NT
        x = [self.sb(f"o_x{i}", [128, C, n], F32) for i in range(2)]; bx = [Buf(f"ox{i}") for i in range(2)]
        stores = []
        for t in range(S // n):
            i = t % 2
            src = self.xT[:, t * n:(t + 1) * n].rearrange("(c p) n -> p c n", p=128)
            self._wait_xT("sp")
            sc.dma("sp", lambda q, i=i, src=src: q.dma_start(out=x[i][:], in_=src), writes=[bx[i]], key=f"ldx{i}")
            for o in range(C):
                k = o % 2
                fns = [(lambda e, o=o, c=c, k=k, t=t: e.matmul(po[k][:], lhsT=wo[:, c, o * 128:(o + 1) * 128], rhs=OT[:, c, t * n:(t + 1) * n],
                                                             start=(c == 0), stop=(c == C - 1))) for c in range(C)]
                sc.op("pe", fns, reads=bOT + [bwo], writes=[bpo[k]])
                sc.op("dve", lambda v, i=i, o=o, k=k: v.tensor_tensor(out=x[i][:, o, :], in0=po[k][:], in1=x[i][:, o, :], op=ALU.add),
                      reads=[bpo[k], bx[i]], writes=[bx[i]])
            dst = self.xT[:, t * n:(t + 1) * n].rearrange("(c p) n -> p c n", p=128)
            stores.append(sc.dma("sp", lambda q, i=i, dst=dst: q.dma_start(out=dst, in_=x[i][:]), reads=[bx[i]], writes=[self.b_xT], key=f"st{i}"))
        self.b_xT.r = []
        self._xT_tokens = stores

    def phase_hgrn(self, l):
        sc = self.sc
        j = l // 2
        NB = 128
        HH = 8
        win = self.sb("win_s", [128, C, 4 * D], BF16); bwin = Buf("win")
        for c in range(C):
            for hf in range(2):
                sc.dma("pool", lambda q, c=c, hf=hf: q.dma_start(out=win[:, c, hf * 2048:(hf + 1) * 2048],
                                                                  in_=self.w_in[j, c * 128:(c + 1) * 128, hf * 2048:(hf + 1) * 2048]), writes=[bwin], key="w1")
        wo = self.sb("hwo_s", [128, C, D], BF16); bwo = Buf("hwo")
        for c0 in range(0, C, 4):
            src = self.wo_hg[j, c0 * 128:(c0 + 4) * 128, :].rearrange("(c p) d -> p c d", p=128)
            sc.dma("pool", lambda q, c0=c0, src=src: q.dma_start(out=wo[:, c0:c0 + 4, :], in_=src), writes=[bwo], key="w2")
        M1 = self.sb("M1_s", [128, 256], F32); M2 = self.sb("M2_s", [128, 128], F32); Mc = self.sb("Mc_s", [128, 1024], F32)
        o128 = self.sb("o128_s", [128, 128], F32); idb = self.sb("hidb_s", [128, 128], BF16)
        lrow = self.sb("lrow_s", [128, 2, D], F32)
        lT = self.sb("lT_s", [128, 2 * C], F32); gg = self.sb("gg_s", [128, 2], F32)
        bK = Buf("hconst")
        sc.dma("sp", lambda q: q.dma_start(out=M1[:], in_=self.M1[:, :]), writes=[bK], key="c0")
        sc.dma("sp", lambda q: q.dma_start(out=M2[:], in_=self.M2[:, :]), writes=[bK], key="c0")
        sc.dma("sp", lambda q: q.dma_start(out=Mc[:], in_=self.Mc[:, :]), writes=[bK], key="c0")
        sc.dma("sp", lambda q: q.dma_start(out=o128[:], in_=self.ones128[:, :]), writes=[bK], key="c0")
        sc.dma("sp", lambda q: q.dma_start(out=lT[:], in_=self.lblT[:, :]), writes=[bK], key="c0")
        sc.dma("sp", lambda q: q.dma_start(out=gg[:], in_=self.hgg[:, :]), writes=[bK], key="c0")
        for jj in range(2):
            sc.dma("sp", lambda q, jj=jj: q.dma_start(out=lrow[:, jj, :], in_=self.lbl[jj:jj + 1, :].partition_broadcast(128)), writes=[bK], key="c0")
        omr = self.sb("omr_s", [128, D], F32); omT = self.sb("omT_s", [128, C], F32); bom = Buf("om")
        one1 = self.sb("one1_s", [128, 1], F32)
        sc.op("dve", lambda v: v.memset(one1[:], 1.0), writes=[bom])
        if j == 0:
            sc.op("dve", lambda v: v.memset(omr[:], 1.0), writes=[bom])
            sc.op("dve", lambda v: v.memset(omT[:], 1.0), writes=[bom])
        else:
            dr = self.sb("dr_s", [128, D], F32); dT = self.sb("dT_s", [128, C], F32); bd = Buf("dlt")
            sc.op("dve", lambda v: v.tensor_tensor(out=dr[:], in0=lrow[:, 0, :], in1=lrow[:, 1, :], op=ALU.subtract), reads=[bK], writes=[bd])
            sc.op("dve", lambda v: v.tensor_tensor(out=dT[:], in0=lT[:, 0:C], in1=lT[:, C:2 * C], op=ALU.subtract), reads=[bK], writes=[bd])
            sc.op("act", lambda a: a.activation(out=omr[:], in_=dr[:], func=AF.Sigmoid), reads=[bd], writes=[bom])
            sc.op("act", lambda a: a.activation(out=omT[:], in_=dT[:], func=AF.Sigmoid), reads=[bd], writes=[bom])
        sc.dma("pool", lambda q: q.dma_start(out=idb[:], in_=self.identb[:, :]), writes=[bK], key="c1")
        St = self.sb("S_s", [128, HH, 128], F32); bS = [Buf(f"S{h}") for h in range(HH)]
        Sb = self.sb("Sb_s", [128, HH, 128], BF16); bSb = [Buf(f"Sb{h}") for h in range(HH)]
        sc.op("pool", lambda v: v.memset(St[:], 0.0), writes=bS)
        x = [self.sb(f"g_x{i}", [128, C, NB], F32) for i in range(2)]; bx = [Buf(f"gx{i}") for i in range(2)]
        h = self.sb("g_h", [128, C, NB], BF16); bh = Buf("gh")
        scr = dict(sq=self.sb("g_sq", [128, C, NB], F32), bsq=Buf("sq"), pss=self.ps("g_pss", [128, 512]), bpss=Buf("pss"),
                   rt=self.sb("g_rt", [128, NB], F32), brt=Buf("rt"), rs=self.sb("g_rs", [128, NB], F32), brs=Buf("rs"))
        pA = self.ps("g_pA", [128, 1024]); bpA = [Buf("pA0"), Buf("pA1")]
        pB = self.ps("g_pB", [128, 1024]); bpB = [Buf("pB0"), Buf("pB1")]
        pX = self.ps("g_pX", [128, 1024]); bpX = [Buf("pX0"), Buf("pX1")]
        pE = self.ps("g_pE", [128, 512]); bpE = Buf("pE")
        W = HH * NB

        def t2(name, dtype=F32):
            return self.sb(name, [128, W], dtype), Buf(name)
        sbar, bsb = t2("g_sbar"); ktok, bkt = t2("g_ktok"); lf, blf = t2("g_lf")
        vtok, bvt = t2("g_vtok", BF16); khat, bkh = t2("g_khat", BF16)
        ed, bed = sbar, bsb
        qraw, bqr = t2("g_qraw"); graw, bgr = t2("g_graw"); fraw, bfr = t2("g_fraw")
        sgq, bsgq = t2("g_sgq"); sgg, bsgg = t2("g_sgg"); kf, bkf = t2("g_kf"); qf, bqf = t2("g_qf")
        E1, bE1 = t2("g_E1"); E2, bE2 = t2("g_E2")
        EX = self.sb("g_EX", [128, 4 * HH], F32); bEX = Buf("EX")
        qt, bqt = t2("g_qt", BF16); kt, bkt2 = t2("g_kt", BF16); scT, bsc = t2("g_scT", BF16)
        oT, boT = t2("g_oT"); osq, bosq = t2("g_osq"); ort, bort = t2("g_ort"); on, bon = t2("g_on", BF16)
        omF, bomF = t2("g_omF")
        sc.op("dve", lambda v: v.memset(omF[:], 1.0), writes=[bomF])
        if j != 0:
            for hh in range(HH):
                sc.op("dve", lambda v, hh=hh: v.tensor_scalar_mul(out=omF[:, hh * 128:(hh + 1) * 128], in0=omF[:, hh * 128:(hh + 1) * 128], scalar1=omT[:, hh:hh + 1]),
                      reads=[bom, bomF], writes=[bomF])
        hsl = lambda hh: slice(hh * 128, (hh + 1) * 128)
        bchain = Buf("act_chain")
        gcol = (l * 2) * C
        stores = []
        for t in range(self.hg_blocks or (S // NB)):
            i = t % 2
            src = self.xT[:, t * NB:(t + 1) * NB].rearrange("(c p) n -> p c n", p=128)
            self._wait_xT("sp")
            sc.dma("sp", lambda q, i=i, src=src: q.dma_start(out=x[i][:], in_=src), writes=[bx[i]], key=f"ldx{i}")
            self.norm(x[i], bx[i], h, bh, NB, gcol, scr)
            for (pp, bpp, off) in ((pA, bpA, D), (pB, bpB, 2 * D)):
                fns = []
                for hf in range(2):
                    for c in range(C):
                        fns.append(lambda e, pp=pp, off=off, hf=hf, c=c: e.matmul(pp[:, hf * 512:(hf + 1) * 512], lhsT=h[:, c, :],
                                                                                  rhs=win[:, c, off + hf * 512:off + (hf + 1) * 512],
                                                                                  start=(c == 0), stop=(c == C - 1)))
                sc.op("pe", fns, reads=[bh, bwin], writes=bpp)
            sc.op("act", lambda a: a.activation(out=sbar[:], in_=pA[:], func=AF.Sigmoid, scale=-1.0), reads=bpA, writes=[bsb])
            sc.op("act", lambda a: a.copy(out=vtok[:], in_=pB[:]), reads=bpB, writes=[bvt])
            sc.op("dve", lambda v: v.tensor_tensor(out=ktok[:], in0=sbar[:], in1=omr[:], op=ALU.mult), reads=[bsb, bom], writes=[bkt])
            sc.op("act", lambda a: a.activation(out=lf[:], in_=ktok[:], func=AF.Ln, bias=one1[:, 0:1], scale=-1.0), reads=[bkt, bom], writes=[blf])
            fns = [(lambda e, hf=hf: e.matmul(pA[:, hf * 512:(hf + 1) * 512], lhsT=M2[:], rhs=lf[:, hf * 512:(hf + 1) * 512], start=True, stop=True))
                   for hf in range(2)]
            sc.op("pe", fns, reads=[blf, bK], writes=bpA)
            sc.op("act", lambda a: a.activation(out=ed[:], in_=pA[:], func=AF.Exp), reads=bpA, writes=[bed])
            sc.op("dve", lambda v: v.tensor_tensor(out=khat[:], in0=ed[:], in1=ktok[:], op=ALU.mult), reads=[bed, bkt], writes=[bkh])
            if self.hg_parts >= 2:
                def proj(dst, base):
                    return [(lambda e, hh=hh, c=c: e.matmul(dst[:, hsl(hh)], lhsT=win[:, c, base + hh * 128:base + (hh + 1) * 128], rhs=h[:, c, :],
                                                            start=(c == 0), stop=(c == C - 1))) for hh in range(HH) for c in range(C)]
                for dst, bdst, base, raw_, braw_ in ((pX, bpX, 0, qraw, bqr), (pA, bpA, 3 * D, graw, bgr), (pB, bpB, D, fraw, bfr)):
                    sc.op("pe", proj(dst, base), reads=[bh, bwin], writes=bdst)
                    sc.op("dve", lambda v, dst=dst, raw_=raw_: v.tensor_copy(out=raw_[:], in_=dst[:]), reads=bdst, writes=[braw_])
                sc.op("act", lambda a: a.activation(out=sgq[:], in_=qraw[:], func=AF.Sigmoid), reads=[bqr], writes=[bsgq])
                sc.op("act", lambda a: a.activation(out=sgg[:], in_=graw[:], func=AF.Sigmoid), reads=[bgr], writes=[bsgg])
                sc.op("act", lambda a: a.activation(out=kf[:], in_=fraw[:], func=AF.Sigmoid, scale=-1.0), reads=[bfr], writes=[bkf])
                sc.op("dve", lambda v: v.tensor_tensor(out=qf[:], in0=qraw[:], in1=sgq[:], op=ALU.mult), reads=[bqr, bsgq], writes=[bqf])
                sc.op("pe", [(lambda e, hh=hh: e.matmul(pX[:, hsl(hh)], lhsT=lf[:, hsl(hh)], rhs=M1[:, 0:128], start=True, stop=True)) for hh in range(HH)],
                      reads=[blf, bK], writes=bpX)
                sc.op("pe", [(lambda e, hh=hh: e.matmul(pE[:, hh * 4:(hh + 1) * 4], lhsT=lf[:, hsl(hh)], rhs=M1[:, 128:132], start=True, stop=True)) for hh in range(HH)],
                      reads=[blf, bK], writes=[bpE])
                sc.op("act", lambda a: a.activation(out=E1[:], in_=pX[:], func=AF.Exp), reads=bpX, writes=[bE1])
                sc.op("act", lambda a: a.activation(out=E2[:], in_=pX[:], func=AF.Exp, scale=-1.0), reads=bpX, writes=[bE2])
                sc.op("act", lambda a: a.activation(out=EX[:], in_=pE[:, 0:4 * HH], func=AF.Exp), reads=[bpE], writes=[bEX])
                sc.op("dve", lambda v: v.tensor_tensor(out=qt[:], in0=qf[:], in1=E1[:], op=ALU.mult), reads=[bqf, bE1], writes=[bqt])
                sc.op("dve", lambda v: v.tensor_tensor(out=kf[:], in0=kf[:], in1=E2[:], op=ALU.mult), reads=[bkf, bE2], writes=[bkf])
                sc.op("dve", lambda v: v.tensor_tensor(out=kt[:], in0=kf[:], in1=omF[:], op=ALU.mult), reads=[bkf, bomF], writes=[bkt2])
            if self.hg_parts >= 3:
                sc.op("pe", [(lambda e, hh=hh: e.matmul(pA[:, hsl(hh)], lhsT=kt[:, hsl(hh)], rhs=qt[:, hsl(hh)], start=True, stop=True)) for hh in range(HH)],
                      reads=[bkt2, bqt], writes=bpA)
                sc.op("dve", lambda v: v.tensor_tensor(out=scT[:], in0=pA[:], in1=Mc[:], op=ALU.mult), reads=bpA + [bK], writes=[bsc])
                for cc in range(2):
                    cs = slice(cc * 64, cc * 64 + 64)
                    for hh in range(HH):
                        sc.op("act", lambda a, hh=hh, cc=cc: a.activation(out=Sb[:, hh, :], in_=St[:, hh, :], func=AF.Copy, scale=EX[:, hh * 4 + cc:hh * 4 + cc + 1]),
                              reads=[bS[hh], bEX], writes=[bSb[hh], bchain])
                    fns = []
                    for hh in range(HH):
                        col = slice(hh * 128 + cc * 64, hh * 128 + cc * 64 + 64)
                        fns.append(lambda e, hh=hh, col=col, cs=cs: e.matmul(pB[:, col], lhsT=vtok[cs, hsl(hh)], rhs=scT[cs, col], start=True, stop=False))
                        fns.append(lambda e, hh=hh, col=col: e.matmul(pB[:, col], lhsT=Sb[:, hh, :], rhs=qt[:, col], start=False, stop=True))
                    sc.op("pe", fns, reads=[bvt, bsc, bqt] + bSb, writes=bpB)
                    sc.op("pe", [(lambda e, hh=hh, cs=cs: e.matmul(pX[:, hsl(hh)], lhsT=khat[cs, hsl(hh)], rhs=vtok[cs, hsl(hh)], start=True, stop=True)) for hh in range(HH)],
                          reads=[bkh, bvt], writes=bpX)
                    for hh in range(HH):
                        sc.op("dve", lambda v, hh=hh, cc=cc: v.scalar_tensor_tensor(out=St[:, hh, :], in0=St[:, hh, :], scalar=EX[:, hh * 4 + 2 + cc:hh * 4 + 3 + cc],
                                                                                      in1=pX[:, hsl(hh)], op0=ALU.mult, op1=ALU.add),
                              reads=[bS[hh], bEX] + bpX, writes=[bS[hh]])
                sc.op("dve", lambda v: v.tensor_copy(out=oT[:], in_=pB[:]), reads=bpB, writes=[boT])
                sc.op("act", lambda a: a.activation(out=osq[:], in_=oT[:], func=AF.Square), reads=[boT], writes=[bosq])
                sc.op("pe", [(lambda e, hf=hf: e.matmul(pA[:, hf * 512:(hf + 1) * 512], lhsT=o128[:], rhs=osq[:, hf * 512:(hf + 1) * 512], start=True, stop=True))
                             for hf in range(2)], reads=[bosq, bK], writes=bpA)
                sc.op("act", lambda a: a.activation(out=ort[:], in_=pA[:], func=AF.Sqrt, bias=self.eps_s[:, 0:1], scale=1.0), reads=bpA + [self.b_eps], writes=[bort])
                sc.op("dve", lambda v: v.reciprocal(out=osq[:], in_=ort[:]), reads=[bort], writes=[bosq])
                sc.op("dve", lambda v: v.scalar_tensor_tensor(out=ort[:], in0=oT[:], scalar=gg[:, j:j + 1], in1=osq[:], op0=ALU.mult, op1=ALU.mult),
                      reads=[boT, bosq, bK], writes=[bort])
                sc.op("dve", lambda v: v.tensor_tensor(out=on[:], in0=ort[:], in1=sgg[:], op=ALU.mult), reads=[bort, bsgg], writes=[bon])
                sc.op("pe", [(lambda e, o=o, c=c: e.matmul(pX[:, hsl(o)], lhsT=wo[:, c, o * 128:(o + 1) * 128], rhs=on[:, hsl(c)], start=(c == 0), stop=(c == C - 1)))
                             for o in range(C) for c in range(C)], reads=[bon, bwo], writes=bpX)
                for o in range(C):
                    sc.op("dve", lambda v, i=i, o=o: v.tensor_tensor(out=x[i][:, o, :], in0=pX[:, hsl(o)], in1=x[i][:, o, :], op=ALU.add),
                          reads=bpX + [bx[i]], writes=[bx[i]])
            dst = self.xT[:, t * NB:(t + 1) * NB].rearrange("(c p) n -> p c n", p=128)
            stores.append(sc.dma("sp", lambda q, i=i, dst=dst: q.dma_start(out=dst, in_=x[i][:]), reads=[bx[i]], writes=[self.b_xT], key=f"st{i}"))
        self.b_xT.r = []
        self._xT_tokens = stores

    def _wait_xT(self, eng):
        self.sc._need(eng, self._xT_tokens)

    def norm(self, x, bx, h, bh, n, gcol, scr):
        sc = self.sc
        sq, bsq, pss, bpss, rt, brt, rs, brs = (scr[k] for k in ("sq", "bsq", "pss", "bpss", "rt", "brt", "rs", "brs"))
        sc.op("act", lambda a: a.activation(out=sq[:, :, :n], in_=x[:, :, :n], func=AF.Square), reads=[bx], writes=[bsq])
        fns = [(lambda e, c=c: e.matmul(pss[:, :n], lhsT=self.ones_s[:], rhs=sq[:, c, :n], start=(c == 0), stop=(c == C - 1)))
               for c in range(C)]
        sc.op("pe", fns, reads=[bsq, self.b_ones], writes=[bpss])
        sc.op("act", lambda a: a.activation(out=rt[:, :n], in_=pss[:, :n], func=AF.Sqrt, bias=self.eps_s[:, 0:1], scale=1.0),
              reads=[bpss, self.b_eps], writes=[brt])
        sc.op("dve", lambda v: v.reciprocal(out=rs[:, :n], in_=rt[:, :n]), reads=[brt], writes=[brs])
        for c in range(C):
            eng = "dve"
            sc.op(eng, lambda v, c=c: v.scalar_tensor_tensor(out=h[:, c, :n], in0=x[:, c, :n],
                                                             scalar=self.ng_s[:, gcol + c:gcol + c + 1], in1=rs[:, :n],
                                                             op0=ALU.mult, op1=ALU.mult),
                  reads=[bx, brs, self.b_ng], writes=[bh])

    def phase_mlp(self, l):
        sc = self.sc
        n = 256
        FC = DFF // 128
        w1s = self.sb(f"w1s_{l}", [128, C, DFF], BF16)
        w2s = self.sb(f"w2s_{l}", [128, FC, D], BF16)
        bw1 = [Buf(f"w1_{l}")] * 8
        bw2 = [Buf(f"w2_{l}_{i}") for i in range(8)]
        for c in range(C):
            sc.dma("pool", lambda q, c=c: q.dma_start(out=w1s[:, c, :], in_=self.w1[l, c * 128:(c + 1) * 128, :]), writes=[bw1[0]], key="w1")
        for fb in range(8):
            f0 = fb * 4
            src = self.w2[l, f0 * 128:(f0 + 4) * 128, :].rearrange("(f p) d -> p f d", p=128)
            sc.dma("pool", lambda q, f0=f0, src=src: q.dma_start(out=w2s[:, f0:f0 + 4, :], in_=src), writes=[bw2[fb]], key=f"w2_{fb}")
        NX = 3
        x = [self.sb(f"m{l}_x{i}", [128, C, n], F32) for i in range(NX)]
        h = [self.sb(f"m{l}_h{i}", [128, C, n], BF16) for i in range(2)]
        aT = [self.sb(f"m{l}_a{i}", [128, FC, n], BF16) for i in range(2)]
        r32 = [self.sb(f"m{l}_r{i}", [128, n], F32) for i in range(2)]
        scr = dict(sq=self.sb(f"m{l}_sq", [128, C, n], F32), bsq=Buf("sq"), pss=self.ps(f"m{l}_pss", [128, 512]), bpss=Buf("pss"),
                   rt=self.sb(f"m{l}_rt", [128, n], F32), brt=Buf("rt"), rs=self.sb(f"m{l}_rs", [128, n], F32), brs=Buf("rs"))
        pa = [self.ps(f"m{l}_pa{i}", [128, 512]) for i in range(3)]
        py = [self.ps(f"m{l}_py{i}", [128, 512]) for i in range(2)]
        bx = [Buf(f"mx{i}") for i in range(NX)]
        bh = [Buf(f"mh{i}") for i in range(2)]
        ba = [[Buf(f"ma{i}_{f}") for f in range(FC)] for i in range(2)]
        br = [Buf(f"mr{i}") for i in range(2)]
        bpa = [Buf(f"mpa{i}") for i in range(3)]
        bpy = [Buf(f"mpy{i}") for i in range(2)]
        stores = []
        gcol = (l * 2 + 1) * C
        T = S // n

        def load(t):
            src = self.xT[:, t * n:(t + 1) * n].rearrange("(c p) n -> p c n", p=128)
            self._wait_xT("sp")
            sc.dma("sp", lambda q, src=src: q.dma_start(out=x[t % NX][:], in_=src), writes=[bx[t % NX]], key=f"mx{t % NX}")

        def stage1(t):
            hi, ai = t % 2, t % 2
            for f in range(FC):
                k = f % 3
                fns = [(lambda e, f=f, c=c, k=k: e.matmul(pa[k][:, :n], lhsT=w1s[:, c, f * 128:(f + 1) * 128], rhs=h[hi][:, c, :],
                                                         start=(c == 0), stop=(c == C - 1))) for c in range(C)]
                sc.op("pe", fns, reads=[bh[hi], bw1[f // 4]], writes=[bpa[k]])
                j = f % 2
                sc.op("act", lambda a, j=j, k=k: a.activation(out=r32[j][:], in_=pa[k][:, :n], func=AF.Relu), reads=[bpa[k]], writes=[br[j]])
                eng = "dve" if f % 2 == 0 else "pool"
                sc.op(eng, lambda v, j=j, f=f: v.tensor_tensor(out=aT[ai][:, f, :], in0=r32[j][:], in1=r32[j][:], op=ALU.mult),
                      reads=[br[j]], writes=[ba[ai][f]])

        def stage2(t):
            xi, ai = t % NX, t % 2
            for o in range(C):
                k = o % 2
                for fb in range(8):
                    fns = [(lambda e, o=o, f=f, k=k: e.matmul(py[k][:, :n], lhsT=w2s[:, f, o * 128:(o + 1) * 128], rhs=aT[ai][:, f, :],
                                                             start=(f == 0), stop=(f == FC - 1))) for f in range(fb * 4, fb * 4 + 4)]
                    sc.op("pe", fns, reads=ba[ai][fb * 4:fb * 4 + 4] + [bw2[fb]], writes=[bpy[k]])
                sc.op("dve", lambda v, o=o, k=k: v.tensor_tensor(out=x[xi][:, o, :], in0=py[k][:, :n], in1=x[xi][:, o, :], op=ALU.add),
                      reads=[bpy[k], bx[xi]], writes=[bx[xi]])
            dst = self.xT[:, t * n:(t + 1) * n].rearrange("(c p) n -> p c n", p=128)
            stores.append(sc.dma("sp", lambda q, dst=dst: q.dma_start(out=dst, in_=x[xi][:]), reads=[bx[xi]], writes=[self.b_xT], key=f"st{t % 2}"))

        load(0)
        self.norm(x[0], bx[0], h[0], bh[0], n, gcol, scr)
        for t in range(T):
            if t + 1 < T:
                load(t + 1)
            stage1(t)
            if t + 1 < T:
                self.norm(x[(t + 1) % NX], bx[(t + 1) % NX], h[(t + 1) % 2], bh[(t + 1) % 2], n, gcol, scr)
            if t >= 1:
                stage2(t - 1)
        stage2(T - 1)
        self.b_xT.r = []
        self._xT_tokens = stores


_CACHE = {}
HG_PAD = 0
ATT_SUB = "dve"
SELFWAIT_ENGINES = ()
HG_DEBUG = ""


def _consts_np():
    blk = np.zeros((128, 128), np.float32)
    blk[:64, :64] = 1.0 / 64
    blk[64:, 64:] = 1.0 / 64
    q = np.arange(128)[:, None]; s_ = np.arange(128)[None, :]
    mask = np.where(s_ >= q, -240000.0, 0.0).astype(np.float32)
    p = np.arange(128); ch = p // 64; u = p % 64
    same = ch[:, None] == ch[None, :]
    tri = same & (p[:, None] <= p[None, :])
    mid = same & (u[:, None] <= 31)
    M1 = np.zeros((128, 256), np.float32)
    M1[:, :128] = tri.astype(np.float32) - mid.astype(np.float32)
    for c in range(2):
        M1[:, 128 + c] = ((ch == c) & (u <= 31)).astype(np.float32)
        M1[:, 130 + c] = (ch == c).astype(np.float32)
    M2 = (same & (p[:, None] > p[None, :])).astype(np.float32)
    Mc = (same & (p[:, None] <= p[None, :])).astype(np.float32)
    return {"ident": np.eye(128, dtype=np.float32), "onesD": np.full((128, 128), 1.0 / D, np.float32),
            "blk64": blk, "identb": np.eye(128, dtype=np.float32), "maskneg": mask,
            "hg_M1": M1, "hg_M2": M2, "hg_Mc": np.ascontiguousarray(np.tile(Mc, (1, 8))), "ones128": np.full((128, 128), 1.0 / 128, np.float32)}


def kernel(x, norm_gains, sb_w_qkv, sb_q_gain, sb_k_gain, sb_w_o, hg_w_in, hg_lb_logits, hg_norm_gain, hg_w_o,
           mlp_w1, mlp_w2, _layers=(0, 1, 2, 3), _mixers=None, _mlps=None, _hg_blocks=None, _hg_parts=3, _allow_hgrn=False, _cores=None, _return_maps=False):
    key = (tuple(_layers), None if _mixers is None else tuple(_mixers), None if _mlps is None else tuple(_mlps), _hg_blocks, _hg_parts, _allow_hgrn, HG_DEBUG, HG_PAD, SELFWAIT_ENGINES, ATT_SUB)
    if key not in _CACHE:
        pr = Prog(list(_layers), _mixers, _mlps, hg_blocks=_hg_blocks, hg_parts=_hg_parts, allow_hgrn=_allow_hgrn)
        _CACHE[key] = (pr.build(), pr.used_inputs)
    nc, used = _CACHE[key]
    cst = _consts_np()
    ng = np.ascontiguousarray(np.asarray(norm_gains, np.float32).reshape(DEPTH * 2, C, 128).transpose(2, 0, 1).reshape(128, DEPTH * 2 * C))
    qg, kg = np.asarray(sb_q_gain, np.float32), np.asarray(sb_k_gain, np.float32)
    qk = np.stack([np.tile(qg[0], 2), np.tile(kg[0], 2), np.tile(qg[1], 2), np.tile(kg[1], 2)], axis=1)
    lbl = np.ascontiguousarray(hg_lb_logits, np.float32)
    lblT = np.ascontiguousarray(lbl.reshape(2, C, 128).transpose(2, 0, 1).reshape(128, 2 * C))
    shared = {"hg_w_in": np.ascontiguousarray(hg_w_in, np.float32), "hg_w_o": np.ascontiguousarray(hg_w_o, np.float32),
              "hg_lb_logits": lbl, "hg_lb_logits_T": lblT, "hg_gain_T": np.ascontiguousarray(np.asarray(hg_norm_gain, np.float32).T),
              "qk_gain": np.ascontiguousarray(qk), "sb_w_qkv": np.ascontiguousarray(sb_w_qkv, np.float32),
              "sb_w_o": np.ascontiguousarray(sb_w_o, np.float32), "norm_gains": ng, "mlp_w1": np.ascontiguousarray(mlp_w1, np.float32), "mlp_w2": np.ascontiguousarray(mlp_w2, np.float32), **cst}
    shared = {k: v for k, v in shared.items() if k in used}
    nb = _cores or 8
    in_maps = [dict(shared, x=np.ascontiguousarray(x[b], np.float32)) for b in range(nb)]
    if _return_maps:
        return nc, in_maps
    res = run_bass_kernel_spmd(nc, in_maps, core_ids=list(range(nb)))
    return np.stack([r["out"] for r in res.results], axis=0)
```

```python
import contextlib
import numpy as np
import concourse.bass as bass
import concourse.mybir as mybir
from concourse.bass_utils import run_bass_kernel_spmd

F32 = mybir.dt.float32
BF16 = mybir.dt.bfloat16
AF = mybir.ActivationFunctionType
ALU = mybir.AluOpType

D = 1024
S = 4096
DEPTH = 4
NH = 16
DH = 64
DFF = 4096
C = D // 128
NT = 512
EPS = 1e-6


class Buf:
    def __init__(self, name):
        self.name = name
        self.w = None
        self.r = []


class Sched:
    ENG = ("pe", "act", "dve", "pool", "sp")

    def __init__(self, nc, stack):
        self.nc = nc
        self.stack = stack
        self.sem = {e: stack.enter_context(nc.semaphore("s_" + e)) for e in self.ENG if e != "sp"}
        self.cnt = {e: 0 for e in self.sem}
        self.prog = {e: [] for e in self.ENG}
        self.seen = {e: {} for e in self.ENG}
        self.dsem = {}
        self.dcnt = {}

    def _need(self, eng, toks):
        best = {}
        for t in toks:
            if t is None:
                continue
            s, v = t
            if v > best.get(id(s), (s, 0))[1]:
                best[id(s)] = (s, v)
        for s, v in best.values():
            if self.seen[eng].get(id(s), 0) < v:
                self.seen[eng][id(s)] = v
                self.prog[eng].append(("wait", s, v))

    def _deps(self, reads, writes):
        toks = [b.w for b in reads]
        for b in writes:
            toks.append(b.w)
            toks.extend(b.r)
        return toks

    def op(self, eng, fns, reads=(), writes=()):
        if callable(fns):
            fns = [fns]
        n0 = len(self.prog[eng])
        deps = self._deps(reads, writes)
        if eng == "pe":
            deps = [t for t in deps if t is not None and t[0] is not self.sem["pe"]]
        self._need(eng, deps)
        if eng in SELFWAIT_ENGINES and len(self.prog[eng]) == n0 and self.cnt[eng] > 0:
            self.seen[eng][id(self.sem[eng])] = self.cnt[eng]
            self.prog[eng].append(("wait", self.sem[eng], self.cnt[eng]))
        self.cnt[eng] += 1
        tok = (self.sem[eng], self.cnt[eng])
        self.prog[eng].append(("op", fns, self.sem[eng], 1))
        for b in reads:
            b.r.append(tok)
        for b in writes:
            b.w = tok
            b.r = []
        return tok

    def dma(self, q, fn, reads=(), writes=(), key=None):
        key = q + "_" + (key or writes[0].name)
        if key not in self.dsem:
            self.dsem[key] = self.stack.enter_context(self.nc.semaphore("d_" + key))
            self.dcnt[key] = 0
        self._need(q, self._deps(reads, writes))
        self.dcnt[key] += 16
        tok = (self.dsem[key], self.dcnt[key])
        self.prog[q].append(("op", [fn], self.dsem[key], 16))
        for b in reads:
            b.r.append(tok)
        for b in writes:
            b.w = tok
            b.r = []
        return tok

    def barrier(self):
        toks = [(self.sem[e], self.cnt[e]) for e in self.sem if self.cnt[e]]
        toks += [(self.dsem[k], self.dcnt[k]) for k in self.dsem if self.dcnt[k]]
        for e in self.ENG:
            self._need(e, toks)

    def emit(self):
        nc = self.nc
        engobj = {"pe": "tensor", "act": "scalar", "dve": "vector", "pool": "gpsimd", "sp": "sync"}
        with nc.Block() as block:
            for e in self.ENG:
                prog = self.prog[e]

                def body(eng, prog=prog):
                    for item in prog:
                        if item[0] == "wait":
                            eng.wait_ge(item[1], item[2])
                        else:
                            _, fns, sem, inc = item
                            ins = None
                            n_before = nc.n_instructions() if inc == 16 else 0
                            for f in fns:
                                ins = f(eng)
                            if inc == 16 and nc.n_instructions() - n_before != 1:
                                raise RuntimeError(f"dma_start was split into {nc.n_instructions() - n_before} instructions: the tracker's +16 per DMA "
                                                   f"would be wrong (each piece adds 16). Reshape this transfer.")
                            ins.then_inc(sem, inc)
                getattr(block, engobj[e])(body)
        self.prog = {e: [] for e in self.ENG}


class Prog:
    def __init__(self, layers, mixers=None, mlps=None, hg_blocks=None, hg_parts=3, allow_hgrn=False):
        self.allow_hgrn = allow_hgrn
        self.hg_blocks = hg_blocks
        self.hg_parts = hg_parts
        self.layers = layers
        self.mixers = layers if mixers is None else mixers
        self.mlps = layers if mlps is None else mlps

    def build(self):
        nc = bass.Bass("TRN2", target_bir_lowering=False)
        self.nc = nc
        nc.allow_low_precision("bf16 matmul operands with fp32 PSUM accumulation (problem tolerance is set for this)")
        dt = nc.dram_tensor
        self.x_in = dt("x", [S, D], F32, kind="ExternalInput").ap()
        self.out = dt("out", [S, D], F32, kind="ExternalOutput").ap()
        self.ng = dt("norm_gains", [128, DEPTH * 2 * C], F32, kind="ExternalInput").ap()
        self.ident = dt("ident", [128, 128], F32, kind="ExternalInput").ap()
        self.onesD = dt("onesD", [128, 128], F32, kind="ExternalInput").ap()
        sbm = [l for l in self.layers if l in self.mixers and l % 2 == 0]
        hgm = [l for l in self.layers if l in self.mixers and l % 2 == 1]
        mlm = [l for l in self.layers if l in self.mlps]
        self.used_inputs = {"x", "norm_gains", "ident", "onesD"}
        self.xT = dt("xT_scratch", [D, S], F32, kind="Internal").ap()
        if mlm:
            self.w1 = dt("mlp_w1", [DEPTH, D, DFF], F32, kind="ExternalInput").ap()
            self.w2 = dt("mlp_w2", [DEPTH, DFF, D], F32, kind="ExternalInput").ap()
            self.used_inputs |= {"mlp_w1", "mlp_w2"}
        if sbm or hgm:
            self.identb = dt("identb", [128, 128], F32, kind="ExternalInput").ap()
            self.used_inputs |= {"identb"}
        if sbm:
            self.wqkv = dt("sb_w_qkv", [2, D, 3 * D], F32, kind="ExternalInput").ap()
            self.wo_sb = dt("sb_w_o", [2, D, D], F32, kind="ExternalInput").ap()
            self.qkg = dt("qk_gain", [128, 4], F32, kind="ExternalInput").ap()
            self.blk64 = dt("blk64", [128, 128], F32, kind="ExternalInput").ap()
            self.maskneg = dt("maskneg", [128, 128], F32, kind="ExternalInput").ap()
            self.QT = dt("QT_scratch", [D, S], BF16, kind="Internal").ap()
            self.KT = dt("KT_scratch", [D, S], BF16, kind="Internal").ap()
            self.V = dt("V_scratch", [S, D], BF16, kind="Internal").ap()
            self.used_inputs |= {"sb_w_qkv", "sb_w_o", "qk_gain", "blk64", "maskneg"}
        if hgm:
            self.w_in = dt("hg_w_in", [2, D, 4 * D], F32, kind="ExternalInput").ap()
            self.wo_hg = dt("hg_w_o", [2, D, D], F32, kind="ExternalInput").ap()
            self.lbl = dt("hg_lb_logits", [2, D], F32, kind="ExternalInput").ap()
            self.lblT = dt("hg_lb_logits_T", [128, 2 * C], F32, kind="ExternalInput").ap()
            self.hgg = dt("hg_gain_T", [128, 2], F32, kind="ExternalInput").ap()
            self.M1 = dt("hg_M1", [128, 256], F32, kind="ExternalInput").ap()
            self.M2 = dt("hg_M2", [128, 128], F32, kind="ExternalInput").ap()
            self.Mc = dt("hg_Mc", [128, 1024], F32, kind="ExternalInput").ap()
            self.ones128 = dt("ones128", [128, 128], F32, kind="ExternalInput").ap()
            self.used_inputs |= {"hg_w_in", "hg_w_o", "hg_lb_logits", "hg_lb_logits_T", "hg_gain_T", "hg_M1", "hg_M2", "hg_Mc", "ones128"}
        with contextlib.ExitStack() as st:
            self.st = st
            self.sc = Sched(nc, st)
            self.ph = st
            self._consts()
            self.run_phase(self.phase_in)
            for l in self.layers:
                if l in self.mixers:
                    if l % 2 == 0:
                        self.run_phase(self.phase_qkv, l)
                        if not SKIP_ATTN:
                            self.run_phase(self.phase_attn, l)
                    else:
                        self.run_phase(self.phase_hgrn, l)
                if l in self.mlps:
                    self.run_phase(self.phase_mlp, l)
            self.run_phase(self.phase_out)
        return nc

    def _uniq(self, name):
        self._nalloc = getattr(self, "_nalloc", 0) + 1
        return f"{name}_{self._nalloc}"

    def sb(self, name, shape, dtype):
        return self.ph.enter_context(self.nc.sbuf_tensor(self._uniq(name), shape, dtype))

    def ps(self, name, shape, dtype=F32):
        return self.ph.enter_context(self.nc.psum_tensor(self._uniq(name), shape, dtype))

    def run_phase(self, fn, *a):
        with contextlib.ExitStack() as ph:
            keep, self.ph = self.ph, ph
            fn(*a)
            self.sc.barrier()
            self.sc.emit()
            self.ph = keep

    def _consts(self):
        sc = self.sc
        self.ident_s = self.sb("ident_s", [128, 128], F32)
        self.ones_s = self.sb("ones_s", [128, 128], F32)
        self.ng_s = self.sb("ng_s", [128, DEPTH * 2 * C], F32)
        self.eps_s = self.sb("eps_s", [128, 1], F32)
        self.b_const = Buf("consts")
        sc.dma("sp", lambda q: q.dma_start(out=self.ident_s[:], in_=self.ident[:, :]), writes=[self.b_const], key="c0")
        b1 = Buf("c1"); b2 = Buf("c2"); self.b_eps = Buf("eps")
        sc.dma("sp", lambda q: q.dma_start(out=self.ones_s[:], in_=self.onesD[:, :]), writes=[b1], key="c1")
        sc.dma("sp", lambda q: q.dma_start(out=self.ng_s[:], in_=self.ng[:, :]), writes=[b2], key="c2")
        sc.op("dve", lambda v: v.memset(self.eps_s[:], EPS), writes=[self.b_eps])
        self.b_ones = b1
        self.b_ng = b2
        self.b_xT = Buf("xT_dram")

    def phase_in(self):
        sc, nc = self.sc, self.nc
        xin = [self.sb(f"pin_x{i}", [128, 4, D], F32) for i in range(2)]
        xo = [self.sb(f"pin_o{i}", [128, C, NT], F32) for i in range(2)]
        pst = [self.ps(f"pin_ps{i}", [128, NT]) for i in range(4)]
        b_in = [Buf(f"pin_x{i}") for i in range(2)]
        b_o = [Buf(f"pin_o{i}") for i in range(2)]
        b_ps = [Buf(f"pin_ps{i}") for i in range(4)]
        stores = []
        for t in range(S // NT):
            i = t % 2
            src = self.x_in[t * NT:(t + 1) * NT, :].rearrange("(j p) d -> p j d", p=128)
            sc.dma("sp", lambda q, i=i, src=src: q.dma_start(out=xin[i][:], in_=src), writes=[b_in[i]])
            for c in range(C):
                k = c % 4
                fns = [(lambda e, i=i, c=c, j=j, k=k: e.matmul(pst[k][:, j * 128:(j + 1) * 128],
                                                              lhsT=xin[i][:, j, c * 128:(c + 1) * 128],
                                                              rhs=self.ident_s[:], start=True, stop=True))
                       for j in range(4)]
                sc.op("pe", fns, reads=[b_in[i], self.b_const], writes=[b_ps[k]])
                eng = "dve" if c % 2 == 0 else "act"
                if eng == "dve":
                    sc.op("dve", lambda v, i=i, c=c, k=k: v.tensor_copy(out=xo[i][:, c, :], in_=pst[k][:]),
                          reads=[b_ps[k]], writes=[b_o[i]])
                else:
                    sc.op("act", lambda a, i=i, c=c, k=k: a.copy(out=xo[i][:, c, :], in_=pst[k][:]),
                          reads=[b_ps[k]], writes=[b_o[i]])
            dst = self.xT[:, t * NT:(t + 1) * NT].rearrange("(c p) n -> p c n", p=128)
            stores.append(sc.dma("sp", lambda q, i=i, dst=dst: q.dma_start(out=dst, in_=xo[i][:]),
                                 reads=[b_o[i]], writes=[self.b_xT], key=f"st{i}"))
        self.b_xT.r = []
        self._xT_tokens = stores

    def phase_out(self):
        sc = self.sc
        xi = [self.sb(f"po_x{i}", [128, C, NT], F32) for i in range(2)]
        xo = [self.sb(f"po_o{i}", [128, 4, D], F32) for i in range(2)]
        pst = [self.ps(f"po_ps{i}", [128, NT]) for i in range(4)]
        b_i = [Buf(f"po_x{i}") for i in range(2)]
        b_o = [Buf(f"po_o{i}") for i in range(2)]
        b_ps = [Buf(f"po_ps{i}") for i in range(4)]
        b_out = Buf("out_dram")
        toks = []
        for t in range(S // NT):
            i = t % 2
            src = self.xT[:, t * NT:(t + 1) * NT].rearrange("(c p) n -> p c n", p=128)
            self._wait_xT("sp")
            sc.dma("sp", lambda q, i=i, src=src: q.dma_start(out=xi[i][:], in_=src), writes=[b_i[i]])
            n = 0
            for j in range(4):
                for h in range(2):
                    k = n % 4
                    fns = [(lambda e, i=i, j=j, c=h * 4 + cc, cc=cc, k=k:
                            e.matmul(pst[k][:, cc * 128:(cc + 1) * 128], lhsT=xi[i][:, c, j * 128:(j + 1) * 128],
                                     rhs=self.ident_s[:], start=True, stop=True)) for cc in range(4)]
                    sc.op("pe", fns, reads=[b_i[i], self.b_const], writes=[b_ps[k]])
                    if n % 2 == 0:
                        sc.op("dve", lambda v, i=i, j=j, h=h, k=k: v.tensor_copy(out=xo[i][:, j, h * 512:(h + 1) * 512], in_=pst[k][:]),
                              reads=[b_ps[k]], writes=[b_o[i]])
                    else:
                        sc.op("act", lambda a, i=i, j=j, h=h, k=k: a.copy(out=xo[i][:, j, h * 512:(h + 1) * 512], in_=pst[k][:]),
                              reads=[b_ps[k]], writes=[b_o[i]])
                    n += 1
            dst = self.out[t * NT:(t + 1) * NT, :].rearrange("(j p) d -> p j d", p=128)
            toks.append(sc.dma("sp", lambda q, i=i, dst=dst: q.dma_start(out=dst, in_=xo[i][:]),
                               reads=[b_o[i]], writes=[b_out], key=f"st{i}"))
        self.sc._need("sp", toks)

    def phase_qkv(self, l):
        sc = self.sc
        j = l // 2
        n = NT
        wq = self.sb("wqkv_s", [128, C, 3 * D], BF16)
        bw = Buf("wqkv")
        for c in range(C):
            sc.dma("pool", lambda q, c=c: q.dma_start(out=wq[:, c, :], in_=self.wqkv[j, c * 128:(c + 1) * 128, :]), writes=[bw], key="w1")
        blk = self.sb("blk_s", [128, 128], F32); bblk = Buf("blk")
        qkg = self.sb("qkg_s", [128, 4], F32); bg = Buf("qkg")
        sc.dma("sp", lambda q: q.dma_start(out=blk[:], in_=self.blk64[:, :]), writes=[bblk], key="c0")
        sc.dma("sp", lambda q: q.dma_start(out=qkg[:], in_=self.qkg[:, :]), writes=[bg], key="c1")
        x = [self.sb(f"q_x{i}", [128, C, n], F32) for i in range(2)]
        h = [self.sb(f"q_h{i}", [128, C, n], BF16) for i in range(2)]
        scr = dict(sq=self.sb("q_sq", [128, C, n], F32), bsq=Buf("sq"), pss=self.ps("q_pss", [128, 512]), bpss=Buf("pss"),
                   rt=self.sb("q_rt", [128, n], F32), brt=Buf("rt"), rs=self.sb("q_rs", [128, n], F32), brs=Buf("rs"))
        pp = [self.ps(f"q_pp{i}", [128, 512]) for i in range(3)]
        pm = [self.ps(f"q_pm{i}", [128, 512]) for i in range(2)]
        pv = [self.ps(f"q_pv{i}", [128, 512]) for i in range(2)]
        bpp = [Buf(f"pp{i}") for i in range(3)]; bpm = [Buf(f"pm{i}") for i in range(2)]; bpv = [Buf(f"pv{i}") for i in range(2)]
        sq2 = [self.sb(f"q_sq2{i}", [128, n], F32) for i in range(2)]; bsq2 = [Buf(f"sq2{i}") for i in range(2)]
        rt2 = [self.sb(f"q_rt2{i}", [128, n], F32) for i in range(2)]; brt2 = [Buf(f"rt2{i}") for i in range(2)]
        rs2 = [self.sb(f"q_rs2{i}", [128, n], F32) for i in range(2)]; brs2 = [Buf(f"rs2{i}") for i in range(2)]
        qo = [self.sb(f"q_qo{i}", [128, n], BF16) for i in range(4)]; bqo = [Buf(f"qo{i}") for i in range(4)]
        vt = [self.sb(f"q_vt{i}", [128, 4, D], BF16) for i in range(2)]; bvt = [Buf(f"vt{i}") for i in range(2)]
        bx = [Buf(f"qx{i}") for i in range(2)]; bh = [Buf(f"qh{i}") for i in range(2)]
        b_scr = Buf("qkv_dram")
        gcol = (l * 2) * C
        nq = 0
        T = S // n

        def load(t):
            src = self.xT[:, t * n:(t + 1) * n].rearrange("(c p) n -> p c n", p=128)
            self._wait_xT("sp")
            sc.dma("sp", lambda q, src=src: q.dma_start(out=x[t % 2][:], in_=src), writes=[bx[t % 2]], key=f"qx{t % 2}")

        load(0)
        for t in range(T):
            i = t % 2
            self.norm(x[i], bx[i], h[i], bh[i], n, gcol, scr)
            if t + 1 < T:
                load(t + 1)

            def qk_proj(m, i=i):
                k, kp = m % 2, m % 3
                col = m * 128
                fns = [(lambda e, c=c: e.matmul(pp[kp][:], lhsT=wq[:, c, col:col + 128], rhs=h[i][:, c, :], start=(c == 0), stop=(c == C - 1)))
                       for c in range(C)]
                sc.op("pe", fns, reads=[bh[i], bw], writes=[bpp[kp]])
                sc.op("act", lambda a: a.activation(out=sq2[k][:], in_=pp[kp][:], func=AF.Square), reads=[bpp[kp]], writes=[bsq2[k]])

            def qk_fin(m, o, t=t):
                k, kp = m % 2, m % 3
                sc.op("pe", lambda e: e.matmul(pm[k][:], lhsT=blk[:], rhs=sq2[k][:], start=True, stop=True), reads=[bsq2[k], bblk], writes=[bpm[k]])
                sc.op("act", lambda a: a.activation(out=rt2[k][:], in_=pm[k][:], func=AF.Sqrt, bias=self.eps_s[:, 0:1], scale=1.0),
                      reads=[bpm[k], self.b_eps], writes=[brt2[k]])
                sc.op("dve", lambda v: v.reciprocal(out=rs2[k][:], in_=rt2[k][:]), reads=[brt2[k]], writes=[brs2[k]])
                gc = j * 2 + (0 if m < 8 else 1)
                sc.op("dve", lambda v: v.scalar_tensor_tensor(out=qo[o][:], in0=pp[kp][:], scalar=qkg[:, gc:gc + 1], in1=rs2[k][:], op0=ALU.mult, op1=ALU.mult),
                      reads=[bpp[kp], brs2[k], bg], writes=[bqo[o]])
                dstT = (self.QT if m < 8 else self.KT)[(m % 8) * 128:(m % 8 + 1) * 128, t * n:(t + 1) * n]
                sc.dma("sp", lambda q: q.dma_start(out=dstT, in_=qo[o][:]), reads=[bqo[o]], writes=[], key=f"qst{o}")

            for m in range(17):
                if m < 16:
                    qk_proj(m)
                if m >= 1:
                    qk_fin(m - 1, nq % 4)
                    nq += 1
            for jj in range(4):
                for half in range(2):
                    k = (jj * 2 + half) % 2
                    fns = [(lambda e, i=i, c=c, k=k, jj=jj, half=half: e.matmul(pv[k][:], lhsT=h[i][:, c, jj * 128:(jj + 1) * 128],
                                                                                 rhs=wq[:, c, 2 * D + half * 512:2 * D + (half + 1) * 512],
                                                                                 start=(c == 0), stop=(c == C - 1))) for c in range(C)]
                    sc.op("pe", fns, reads=[bh[i], bw], writes=[bpv[k]])
                    sc.op("act", lambda a, i=i, k=k, jj=jj, half=half: a.copy(out=vt[i][:, jj, half * 512:(half + 1) * 512], in_=pv[k][:]),
                          reads=[bpv[k]], writes=[bvt[i]])
            dstV = self.V[t * n:(t + 1) * n, :].rearrange("(j p) d -> p j d", p=128)
            sc.dma("sp", lambda q, i=i, dstV=dstV: q.dma_start(out=dstV, in_=vt[i][:]), reads=[bvt[i]], writes=[], key=f"vst{i}")

    def phase_attn(self, l):
        sc = self.sc
        j = l // 2
        idb = self.sb("idb_s", [128, 128], BF16); bidb = Buf("idb")
        mk = self.sb("mk_s", [128, 128], BF16); bmk = Buf("mk")
        sc.dma("pool", lambda q: q.dma_start(out=idb[:], in_=self.identb[:, :]), writes=[bidb], key="c0")
        sc.dma("pool", lambda q: q.dma_start(out=mk[:], in_=self.maskneg[:, :]), writes=[bmk], key="c1")
        wo = self.sb("wo_s", [128, C, D], BF16); bwo = Buf("wo")
        for c0 in range(0, C, 4):
            src = self.wo_sb[j, c0 * 128:(c0 + 4) * 128, :].rearrange("(c p) d -> p c d", p=128)
            sc.dma("pool", lambda q, c0=c0, src=src: q.dma_start(out=wo[:, c0:c0 + 4, :], in_=src), writes=[bwo], key="w1")
        OT = self.sb("OT_s", [128, C, S], BF16)
        bOT = [Buf(f"OT{hp}") for hp in range(C)]
        po = [self.ps(f"a_po{i}", [128, 512]) for i in range(2)]; bpo = [Buf(f"po{i}") for i in range(2)]
        self.run_phase(self._attn_core, l, OT, bOT, po, bpo, idb, bidb, mk, bmk)
        self._attn_wo(l, OT, bOT, po, bpo, wo, bwo)

    def _attn_core(self, l, OT, bOT, po, bpo, idb, bidb, mk, bmk):
        sc = self.sc
        qT = [self.sb(f"a_q{i}", [128, S], BF16) for i in range(1)] * 2; bq = [Buf("aq")] * 2
        kT = [self.sb(f"a_k{i}", [128, S], BF16) for i in range(1)] * 2; bk = [Buf("ak")] * 2
        VA = [self.sb(f"a_va{i}", [128, 32, 128], BF16) for i in range(1)] * 2; bva = [Buf("va")] * 2
        VB = [self.sb(f"a_vb{i}", [128, 32, 128], BF16) for i in range(1)] * 2; bvb = [Buf("vb")] * 2
        for i in range(1):
            sc.op("pool", lambda v, i=i: v.memset(VA[i][:], 0.0), writes=[bva[i]])
            sc.op("pool", lambda v, i=i: v.memset(VB[i][:], 0.0), writes=[bvb[i]])
        CH = 1024
        g = [self.sb(f"a_g{i}", [128, CH], F32) for i in range(2)]; bgt = [Buf(f"ag{i}") for i in range(2)]
        Pf = [self.sb(f"a_P{i}", [128, S + 1], F32) for i in range(2)]; bP = [Buf(f"aP{i}") for i in range(2)]
        zer = self.sb("a_zero", [128, CH], F32); bz = Buf("zero")
        sc.op("pool", lambda v: v.memset(zer[:], 0.0), writes=[bz])
        w = [self.sb(f"a_w{i}", [128, CH], BF16) for i in range(2)]; bwt = [Buf(f"aw{i}") for i in range(2)]
        wT = [self.sb(f"a_wT{i}", [128, 8, 128], BF16) for i in range(2)]; bwT = [Buf(f"awT{i}") for i in range(2)]
        pz = [self.ps(f"a_pz{i}", [128, CH]) for i in range(2)]; bpz = [Buf(f"pz{i}") for i in range(2)]
        pt = self.ps("a_pt", [128, CH]); bpt = Buf("pt")
        chunks = []
        for hp in range(C):
            for Q in range(S // 128):
                t1 = 128 * (Q + 1)
                for hd in range(2):
                    b_ = t1
                    while b_ > 0:
                        a_ = max(0, b_ - CH)
                        chunks.append(dict(hp=hp, Q=Q, hd=hd, a=a_, b=b_, L=b_ - a_, t1=t1, first=(b_ == t1), newhp=(Q == 0 and hd == 0 and b_ == t1),
                                           first_pv=(hd == 0 and b_ == t1), last=(hd == 1 and a_ == 0),
                                           ip=(hp * 64 + Q * 2 + hd) % 2, io=(hp * 32 + Q) % 2))
                        b_ = a_
        for n_, ch in enumerate(chunks):
            ch["iz"] = n_ % 2

        def stage_A(ch):
            hp, Q, hd, a, b, L, iz = ch["hp"], ch["Q"], ch["hd"], ch["a"], ch["b"], ch["L"], ch["iz"]
            i = 0
            if ch["newhp"]:
                rows = slice(hp * 128, (hp + 1) * 128)
                sc.dma("sp", lambda q, rows=rows: q.dma_start(out=qT[i][:], in_=self.QT[rows, :]), writes=[bq[i]], key="ld0")
                sc.dma("sp", lambda q, rows=rows: q.dma_start(out=kT[i][:], in_=self.KT[rows, :]), writes=[bk[i]], key="ldk0")
                srcA = self.V[:, hp * 128:hp * 128 + 64].rearrange("(b p) d -> p b d", p=128)
                srcB = self.V[:, hp * 128 + 64:hp * 128 + 128].rearrange("(b p) d -> p b d", p=128)
                for b0 in range(0, 32, 8):
                    sc.dma("sp", lambda q, srcA=srcA, b0=b0: q.dma_start(out=VA[i][:, b0:b0 + 8, 0:64], in_=srcA[:, b0:b0 + 8, :]), writes=[bva[i]], key="lda")
                    sc.dma("sp", lambda q, srcB=srcB, b0=b0: q.dma_start(out=VB[i][:, b0:b0 + 8, 64:128], in_=srcB[:, b0:b0 + 8, :]), writes=[bvb[i]], key="ldb")
            pr = slice(hd * 64, hd * 64 + 64)
            fns = []
            for p0 in range(0, L, 512):
                pl = min(512, L - p0)
                diag = ch["first"] and (p0 + pl == L)
                fns.append(lambda e, p0=p0, pl=pl, diag=diag: e.matmul(pz[iz][:, p0:p0 + pl], lhsT=qT[i][pr, Q * 128:(Q + 1) * 128],
                                                                       rhs=kT[i][pr, a + p0:a + p0 + pl], start=True, stop=not diag))
                if diag:
                    fns.append(lambda e: e.matmul(pz[iz][:, L - 128:L], lhsT=idb[:], rhs=mk[:], start=False, stop=True))
            sc.op("pe", fns, reads=[bq[i], bk[i], bidb, bmk], writes=[bpz[iz]])
            sc.op("act", lambda a_: a_.activation(out=g[iz][:, :L], in_=pz[iz][:, :L], func=AF.Sigmoid, scale=-0.125), reads=[bpz[iz]], writes=[bgt[iz]])

        def stage_B(ch):
            a, b, L, iz, ip, t1 = ch["a"], ch["b"], ch["L"], ch["iz"], ch["ip"], ch["t1"]
            if ch["first"]:
                sc.op("dve", lambda v: v.memset(Pf[ip][:, t1:t1 + 1], 1.0), writes=[bP[ip]])
            sc.op("dve", lambda v: v.tensor_tensor_scan(out=Pf[ip][:, b - 1:(a - 1 if a > 0 else None):-1], data0=g[iz][:, L - 1::-1], data1=zer[:, :L],
                                                        initial=Pf[ip][:, b:b + 1], op0=ALU.mult, op1=ALU.add), reads=[bgt[iz], bP[ip], bz], writes=[bP[ip]])
            eng = {"pool": "pool", "dve": "dve", "alt": ("pool" if iz == 0 else "dve")}[ATT_SUB]
            sc.op(eng, lambda v: v.tensor_tensor(out=w[iz][:, :L], in0=Pf[ip][:, a + 1:b + 1], in1=Pf[ip][:, a:b], op=ALU.subtract),
                  reads=[bP[ip]], writes=[bwt[iz]])

        def stage_C1(ch):
            L, iz = ch["L"], ch["iz"]
            nb = L // 128
            fns = [(lambda e, bb=bb: e.matmul(pt[:, bb * 128:(bb + 1) * 128], lhsT=w[iz][:, bb * 128:(bb + 1) * 128], rhs=idb[:], start=True, stop=True))
                   for bb in range(nb)]
            sc.op("pe", fns, reads=[bwt[iz], bidb], writes=[bpt])
            sc.op("act", lambda a_: a_.copy(out=wT[iz][:, :nb, :], in_=pt[:, :nb * 128]), reads=[bpt], writes=[bwT[iz]])

        def stage_C2(ch):
            hp, Q, hd, a, L, iz, io = ch["hp"], ch["Q"], ch["hd"], ch["a"], ch["L"], ch["iz"], ch["io"]
            i = 0
            Vh, bV = (VA[i], bva[i]) if hd == 0 else (VB[i], bvb[i])
            nb = L // 128
            fns = []
            for bb in range(nb):
                kb = a // 128 + bb
                fns.append(lambda e, bb=bb, kb=kb, fp=(ch["first_pv"] and bb == 0), last=(ch["last"] and bb == nb - 1):
                           e.matmul(po[io][:, :128], lhsT=Vh[:, kb, :], rhs=wT[iz][:, bb, :], start=fp, stop=last))
            sc.op("pe", fns, reads=[bwT[iz], bV], writes=[bpo[io]])
            if ch["last"]:
                sc.op("dve", lambda v: v.tensor_copy(out=OT[:, hp, Q * 128:(Q + 1) * 128], in_=po[io][:, :128]), reads=[bpo[io]], writes=[bOT[hp]])

        nchk = len(chunks)
        for n_ in range(nchk + 3):
            if n_ < nchk:
                stage_A(chunks[n_])
            if 0 <= n_ - 1 < nchk:
                stage_B(chunks[n_ - 1])
            if 0 <= n_ - 2 < nchk:
                stage_C1(chunks[n_ - 2])
            if 0 <= n_ - 3 < nchk:
                stage_C2(chunks[n_ - 3])

    def _attn_wo(self, l, OT, bOT, po, bpo, wo, bwo):
        sc = self.sc
        n = NT
        x = [self.sb(f"o_x{i}", [128, C, n], F32) for i in range(2)]; bx = [Buf(f"ox{i}") for i in range(2)]
        stores = []
        for t in range(S // n):
            i = t % 2
            src = self.xT[:, t * n:(t + 1) * n].rearrange("(c p) n -> p c n", p=128)
            self._wait_xT("sp")
            sc.dma("sp", lambda q, i=i, src=src: q.dma_start(out=x[i][:], in_=src), writes=[bx[i]], key=f"ldx{i}")
            for o in range(C):
                k = o % 2
                fns = [(lambda e, o=o, c=c, k=k, t=t: e.matmul(po[k][:], lhsT=wo[:, c, o * 128:(o + 1) * 128], rhs=OT[:, c, t * n:(t + 1) * n],
                                                             start=(c == 0), stop=(c == C - 1))) for c in range(C)]
                sc.op("pe", fns, reads=bOT + [bwo], writes=[bpo[k]])
                sc.op("dve", lambda v, i=i, o=o, k=k: v.tensor_tensor(out=x[i][:, o, :], in0=po[k][:], in1=x[i][:, o, :], op=ALU.add),
                      reads=[bpo[k], bx[i]], writes=[bx[i]])
            dst = self.xT[:, t * n:(t + 1) * n].rearrange("(c p) n -> p c n", p=128)
            stores.append(sc.dma("sp", lambda q, i=i, dst=dst: q.dma_start(out=dst, in_=x[i][:]), reads=[bx[i]], writes=[self.b_xT], key=f"st{i}"))
        self.b_xT.r = []
        self._xT_tokens = stores

    def phase_hgrn(self, l):
        sc = self.sc
        j = l // 2
        NB = 128
        HH = 8
        win = self.sb("win_s", [128, C, 4 * D], BF16); bwin = Buf("win")
        for c in range(C):
            for hf in range(2):
                sc.dma("pool", lambda q, c=c, hf=hf: q.dma_start(out=win[:, c, hf * 2048:(hf + 1) * 2048],
                                                                  in_=self.w_in[j, c * 128:(c + 1) * 128, hf * 2048:(hf + 1) * 2048]), writes=[bwin], key="w1")
        wo = self.sb("hwo_s", [128, C, D], BF16); bwo = Buf("hwo")
        for c0 in range(0, C, 4):
            src = self.wo_hg[j, c0 * 128:(c0 + 4) * 128, :].rearrange("(c p) d -> p c d", p=128)
            sc.dma("pool", lambda q, c0=c0, src=src: q.dma_start(out=wo[:, c0:c0 + 4, :], in_=src), writes=[bwo], key="w2")
        M1 = self.sb("M1_s", [128, 256], F32); M2 = self.sb("M2_s", [128, 128], F32); Mc = self.sb("Mc_s", [128, 1024], F32)
        o128 = self.sb("o128_s", [128, 128], F32); idb = self.sb("hidb_s", [128, 128], BF16)
        lrow = self.sb("lrow_s", [128, 2, D], F32)
        lT = self.sb("lT_s", [128, 2 * C], F32); gg = self.sb("gg_s", [128, 2], F32)
        bK = Buf("hconst")
        sc.dma("sp", lambda q: q.dma_start(out=M1[:], in_=self.M1[:, :]), writes=[bK], key="c0")
        sc.dma("sp", lambda q: q.dma_start(out=M2[:], in_=self.M2[:, :]), writes=[bK], key="c0")
        sc.dma("sp", lambda q: q.dma_start(out=Mc[:], in_=self.Mc[:, :]), writes=[bK], key="c0")
        sc.dma("sp", lambda q: q.dma_start(out=o128[:], in_=self.ones128[:, :]), writes=[bK], key="c0")
        sc.dma("sp", lambda q: q.dma_start(out=lT[:], in_=self.lblT[:, :]), writes=[bK], key="c0")
        sc.dma("sp", lambda q: q.dma_start(out=gg[:], in_=self.hgg[:, :]), writes=[bK], key="c0")
        for jj in range(2):
            sc.dma("sp", lambda q, jj=jj: q.dma_start(out=lrow[:, jj, :], in_=self.lbl[jj:jj + 1, :].partition_broadcast(128)), writes=[bK], key="c0")
        omr = self.sb("omr_s", [128, D], F32); omT = self.sb("omT_s", [128, C], F32); bom = Buf("om")
        one1 = self.sb("one1_s", [128, 1], F32)
        sc.op("dve", lambda v: v.memset(one1[:], 1.0), writes=[bom])
        if j == 0:
            sc.op("dve", lambda v: v.memset(omr[:], 1.0), writes=[bom])
            sc.op("dve", lambda v: v.memset(omT[:], 1.0), writes=[bom])
        else:
            dr = self.sb("dr_s", [128, D], F32); dT = self.sb("dT_s", [128, C], F32); bd = Buf("dlt")
            sc.op("dve", lambda v: v.tensor_tensor(out=dr[:], in0=lrow[:, 0, :], in1=lrow[:, 1, :], op=ALU.subtract), reads=[bK], writes=[bd])
            sc.op("dve", lambda v: v.tensor_tensor(out=dT[:], in0=lT[:, 0:C], in1=lT[:, C:2 * C], op=ALU.subtract), reads=[bK], writes=[bd])
            sc.op("act", lambda a: a.activation(out=omr[:], in_=dr[:], func=AF.Sigmoid), reads=[bd], writes=[bom])
            sc.op("act", lambda a: a.activation(out=omT[:], in_=dT[:], func=AF.Sigmoid), reads=[bd], writes=[bom])
        sc.dma("pool", lambda q: q.dma_start(out=idb[:], in_=self.identb[:, :]), writes=[bK], key="c1")
        St = self.sb("S_s", [128, HH, 128], F32); bS = [Buf(f"S{h}") for h in range(HH)]
        Sb = self.sb("Sb_s", [128, HH, 128], BF16); bSb = [Buf(f"Sb{h}") for h in range(HH)]
        sc.op("pool", lambda v: v.memset(St[:], 0.0), writes=bS)
        x = [self.sb(f"g_x{i}", [128, C, NB], F32) for i in range(2)]; bx = [Buf(f"gx{i}") for i in range(2)]
        h = self.sb("g_h", [128, C, NB], BF16); bh = Buf("gh")
        scr = dict(sq=self.sb("g_sq", [128, C, NB], F32), bsq=Buf("sq"), pss=self.ps("g_pss", [128, 512]), bpss=Buf("pss"),
                   rt=self.sb("g_rt", [128, NB], F32), brt=Buf("rt"), rs=self.sb("g_rs", [128, NB], F32), brs=Buf("rs"))
        pA = self.ps("g_pA", [128, 1024]); bpA = [Buf("pA0"), Buf("pA1")]
        pB = self.ps("g_pB", [128, 1024]); bpB = [Buf("pB0"), Buf("pB1")]
        pX = self.ps("g_pX", [128, 1024]); bpX = [Buf("pX0"), Buf("pX1")]
        pE = self.ps("g_pE", [128, 512]); bpE = Buf("pE")
        W = HH * NB

        def t2(name, dtype=F32):
            return self.sb(name, [128, W], dtype), Buf(name)
        sbar, bsb = t2("g_sbar"); ktok, bkt = t2("g_ktok"); lf, blf = t2("g_lf")
        vtok, bvt = t2("g_vtok", BF16); khat, bkh = t2("g_khat", BF16)
        ed, bed = sbar, bsb
        qraw, bqr = t2("g_qraw"); graw, bgr = t2("g_graw"); fraw, bfr = t2("g_fraw")
        sgq, bsgq = t2("g_sgq"); sgg, bsgg = t2("g_sgg"); kf, bkf = t2("g_kf"); qf, bqf = t2("g_qf")
        E1, bE1 = t2("g_E1"); E2, bE2 = t2("g_E2")
        EX = self.sb("g_EX", [128, 4 * HH], F32); bEX = Buf("EX")
        qt, bqt = t2("g_qt", BF16); kt, bkt2 = t2("g_kt", BF16); scT, bsc = t2("g_scT", BF16)
        oT, boT = t2("g_oT"); osq, bosq = t2("g_osq"); ort, bort = t2("g_ort"); on, bon = t2("g_on", BF16)
        omF, bomF = t2("g_omF")
        sc.op("dve", lambda v: v.memset(omF[:], 1.0), writes=[bomF])
        if j != 0:
            for hh in range(HH):
                sc.op("dve", lambda v, hh=hh: v.tensor_scalar_mul(out=omF[:, hh * 128:(hh + 1) * 128], in0=omF[:, hh * 128:(hh + 1) * 128], scalar1=omT[:, hh:hh + 1]),
                      reads=[bom, bomF], writes=[bomF])
        hsl = lambda hh: slice(hh * 128, (hh + 1) * 128)
        bchain = Buf("act_chain")
        gcol = (l * 2) * C
        stores = []
        for t in range(self.hg_blocks or (S // NB)):
            i = t % 2
            src = self.xT[:, t * NB:(t + 1) * NB].rearrange("(c p) n -> p c n", p=128)
            self._wait_xT("sp")
            sc.dma("sp", lambda q, i=i, src=src: q.dma_start(out=x[i][:], in_=src), writes=[bx[i]], key=f"ldx{i}")
            self.norm(x[i], bx[i], h, bh, NB, gcol, scr)
            for (pp, bpp, off) in ((pA, bpA, D), (pB, bpB, 2 * D)):
                fns = []
                for hf in range(2):
                    for c in range(C):
                        fns.append(lambda e, pp=pp, off=off, hf=hf, c=c: e.matmul(pp[:, hf * 512:(hf + 1) * 512], lhsT=h[:, c, :],
                                                                                  rhs=win[:, c, off + hf * 512:off + (hf + 1) * 512],
                                                                                  start=(c == 0), stop=(c == C - 1)))
                sc.op("pe", fns, reads=[bh, bwin], writes=bpp)
            sc.op("act", lambda a: a.activation(out=sbar[:], in_=pA[:], func=AF.Sigmoid, scale=-1.0), reads=bpA, writes=[bsb])
            sc.op("act", lambda a: a.copy(out=vtok[:], in_=pB[:]), reads=bpB, writes=[bvt])
            sc.op("dve", lambda v: v.tensor_tensor(out=ktok[:], in0=sbar[:], in1=omr[:], op=ALU.mult), reads=[bsb, bom], writes=[bkt])
            sc.op("act", lambda a: a.activation(out=lf[:], in_=ktok[:], func=AF.Ln, bias=one1[:, 0:1], scale=-1.0), reads=[bkt, bom], writes=[blf])
            fns = [(lambda e, hf=hf: e.matmul(pA[:, hf * 512:(hf + 1) * 512], lhsT=M2[:], rhs=lf[:, hf * 512:(hf + 1) * 512], start=True, stop=True))
                   for hf in range(2)]
            sc.op("pe", fns, reads=[blf, bK], writes=bpA)
            sc.op("act", lambda a: a.activation(out=ed[:], in_=pA[:], func=AF.Exp), reads=bpA, writes=[bed])
            sc.op("dve", lambda v: v.tensor_tensor(out=khat[:], in0=ed[:], in1=ktok[:], op=ALU.mult), reads=[bed, bkt], writes=[bkh])
            if self.hg_parts >= 2:
                def proj(dst, base):
                    return [(lambda e, hh=hh, c=c: e.matmul(dst[:, hsl(hh)], lhsT=win[:, c, base + hh * 128:base + (hh + 1) * 128], rhs=h[:, c, :],
                                                            start=(c == 0), stop=(c == C - 1))) for hh in range(HH) for c in range(C)]
                for dst, bdst, base, raw_, braw_ in ((pX, bpX, 0, qraw, bqr), (pA, bpA, 3 * D, graw, bgr), (pB, bpB, D, fraw, bfr)):
                    sc.op("pe", proj(dst, base), reads=[bh, bwin], writes=bdst)
                    sc.op("dve", lambda v, dst=dst, raw_=raw_: v.tensor_copy(out=raw_[:], in_=dst[:]), reads=bdst, writes=[braw_])
                sc.op("act", lambda a: a.activation(out=sgq[:], in_=qraw[:], func=AF.Sigmoid), reads=[bqr], writes=[bsgq])
                sc.op("act", lambda a: a.activation(out=sgg[:], in_=graw[:], func=AF.Sigmoid), reads=[bgr], writes=[bsgg])
                sc.op("act", lambda a: a.activation(out=kf[:], in_=fraw[:], func=AF.Sigmoid, scale=-1.0), reads=[bfr], writes=[bkf])
                sc.op("dve", lambda v: v.tensor_tensor(out=qf[:], in0=qraw[:], in1=sgq[:], op=ALU.mult), reads=[bqr, bsgq], writes=[bqf])
                sc.op("pe", [(lambda e, hh=hh: e.matmul(pX[:, hsl(hh)], lhsT=lf[:, hsl(hh)], rhs=M1[:, 0:128], start=True, stop=True)) for hh in range(HH)],
                      reads=[blf, bK], writes=bpX)
                sc.op("pe", [(lambda e, hh=hh: e.matmul(pE[:, hh * 4:(hh + 1) * 4], lhsT=lf[:, hsl(hh)], rhs=M1[:, 128:132], start=True, stop=True)) for hh in range(HH)],
                      reads=[blf, bK], writes=[bpE])
                sc.op("act", lambda a: a.activation(out=E1[:], in_=pX[:], func=AF.Exp), reads=bpX, writes=[bE1])
                sc.op("act", lambda a: a.activation(out=E2[:], in_=pX[:], func=AF.Exp, scale=-1.0), reads=bpX, writes=[bE2])
                sc.op("act", lambda a: a.activation(out=EX[:], in_=pE[:, 0:4 * HH], func=AF.Exp), reads=[bpE], writes=[bEX])
                sc.op("dve", lambda v: v.tensor_tensor(out=qt[:], in0=qf[:], in1=E1[:], op=ALU.mult), reads=[bqf, bE1], writes=[bqt])
                sc.op("dve", lambda v: v.tensor_tensor(out=kf[:], in0=kf[:], in1=E2[:], op=ALU.mult), reads=[bkf, bE2], writes=[bkf])
                sc.op("dve", lambda v: v.tensor_tensor(out=kt[:], in0=kf[:], in1=omF[:], op=ALU.mult), reads=[bkf, bomF], writes=[bkt2])
            if self.hg_parts >= 3:
                sc.op("pe", [(lambda e, hh=hh: e.matmul(pA[:, hsl(hh)], lhsT=kt[:, hsl(hh)], rhs=qt[:, hsl(hh)], start=True, stop=True)) for hh in range(HH)],
                      reads=[bkt2, bqt], writes=bpA)
                sc.op("dve", lambda v: v.tensor_tensor(out=scT[:], in0=pA[:], in1=Mc[:], op=ALU.mult), reads=bpA + [bK], writes=[bsc])
                for cc in range(2):
                    cs = slice(cc * 64, cc * 64 + 64)
                    for hh in range(HH):
                        sc.op("act", lambda a, hh=hh, cc=cc: a.activation(out=Sb[:, hh, :], in_=St[:, hh, :], func=AF.Copy, scale=EX[:, hh * 4 + cc:hh * 4 + cc + 1]),
                              reads=[bS[hh], bEX], writes=[bSb[hh], bchain])
                    fns = []
                    for hh in range(HH):
                        col = slice(hh * 128 + cc * 64, hh * 128 + cc * 64 + 64)
                        fns.append(lambda e, hh=hh, col=col, cs=cs: e.matmul(pB[:, col], lhsT=vtok[cs, hsl(hh)], rhs=scT[cs, col], start=True, stop=False))
                        fns.append(lambda e, hh=hh, col=col: e.matmul(pB[:, col], lhsT=Sb[:, hh, :], rhs=qt[:, col], start=False, stop=True))
                    sc.op("pe", fns, reads=[bvt, bsc, bqt] + bSb, writes=bpB)
                    sc.op("pe", [(lambda e, hh=hh, cs=cs: e.matmul(pX[:, hsl(hh)], lhsT=khat[cs, hsl(hh)], rhs=vtok[cs, hsl(hh)], start=True, stop=True)) for hh in range(HH)],
                          reads=[bkh, bvt], writes=bpX)
                    for hh in range(HH):
                        sc.op("dve", lambda v, hh=hh, cc=cc: v.scalar_tensor_tensor(out=St[:, hh, :], in0=St[:, hh, :], scalar=EX[:, hh * 4 + 2 + cc:hh * 4 + 3 + cc],
                                                                                      in1=pX[:, hsl(hh)], op0=ALU.mult, op1=ALU.add),
                              reads=[bS[hh], bEX] + bpX, writes=[bS[hh]])
                sc.op("dve", lambda v: v.tensor_copy(out=oT[:], in_=pB[:]), reads=bpB, writes=[boT])
                sc.op("act", lambda a: a.activation(out=osq[:], in_=oT[:], func=AF.Square), reads=[boT], writes=[bosq])
                sc.op("pe", [(lambda e, hf=hf: e.matmul(pA[:, hf * 512:(hf + 1) * 512], lhsT=o128[:], rhs=osq[:, hf * 512:(hf + 1) * 512], start=True, stop=True))
                             for hf in range(2)], reads=[bosq, bK], writes=bpA)
                sc.op("act", lambda a: a.activation(out=ort[:], in_=pA[:], func=AF.Sqrt, bias=self.eps_s[:, 0:1], scale=1.0), reads=bpA + [self.b_eps], writes=[bort])
                sc.op("dve", lambda v: v.reciprocal(out=osq[:], in_=ort[:]), reads=[bort], writes=[bosq])
                sc.op("dve", lambda v: v.scalar_tensor_tensor(out=ort[:], in0=oT[:], scalar=gg[:, j:j + 1], in1=osq[:], op0=ALU.mult, op1=ALU.mult),
                      reads=[boT, bosq, bK], writes=[bort])
                sc.op("dve", lambda v: v.tensor_tensor(out=on[:], in0=ort[:], in1=sgg[:], op=ALU.mult), reads=[bort, bsgg], writes=[bon])
                sc.op("pe", [(lambda e, o=o, c=c: e.matmul(pX[:, hsl(o)], lhsT=wo[:, c, o * 128:(o + 1) * 128], rhs=on[:, hsl(c)], start=(c == 0), stop=(c == C - 1)))
                             for o in range(C) for c in range(C)], reads=[bon, bwo], writes=bpX)
                for o in range(C):
                    sc.op("dve", lambda v, i=i, o=o: v.tensor_tensor(out=x[i][:, o, :], in0=pX[:, hsl(o)], in1=x[i][:, o, :], op=ALU.add),
                          reads=bpX + [bx[i]], writes=[bx[i]])
            dst = self.xT[:, t * NB:(t + 1) * NB].rearrange("(c p) n -> p c n", p=128)
            stores.append(sc.dma("sp", lambda q, i=i, dst=dst: q.dma_start(out=dst, in_=x[i][:]), reads=[bx[i]], writes=[self.b_xT], key=f"st{i}"))
        self.b_xT.r = []
        self._xT_tokens = stores

    def _wait_xT(self, eng):
        self.sc._need(eng, self._xT_tokens)

    def norm(self, x, bx, h, bh, n, gcol, scr):
        sc = self.sc
        sq, bsq, pss, bpss, rt, brt, rs, brs = (scr[k] for k in ("sq", "bsq", "pss", "bpss", "rt", "brt", "rs", "brs"))
        sc.op("act", lambda a: a.activation(out=sq[:, :, :n], in_=x[:, :, :n], func=AF.Square), reads=[bx], writes=[bsq])
        fns = [(lambda e, c=c: e.matmul(pss[:, :n], lhsT=self.ones_s[:], rhs=sq[:, c, :n], start=(c == 0), stop=(c == C - 1)))
               for c in range(C)]
        sc.op("pe", fns, reads=[bsq, self.b_ones], writes=[bpss])
        sc.op("act", lambda a: a.activation(out=rt[:, :n], in_=pss[:, :n], func=AF.Sqrt, bias=self.eps_s[:, 0:1], scale=1.0),
              reads=[bpss, self.b_eps], writes=[brt])
        sc.op("dve", lambda v: v.reciprocal(out=rs[:, :n], in_=rt[:, :n]), reads=[brt], writes=[brs])
        for c in range(C):
            eng = "dve"
            sc.op(eng, lambda v, c=c: v.scalar_tensor_tensor(out=h[:, c, :n], in0=x[:, c, :n],
                                                             scalar=self.ng_s[:, gcol + c:gcol + c + 1], in1=rs[:, :n],
                                                             op0=ALU.mult, op1=ALU.mult),
                  reads=[bx, brs, self.b_ng], writes=[bh])

    def phase_mlp(self, l):
        sc = self.sc
        n = 256
        FC = DFF // 128
        w1s = self.sb(f"w1s_{l}", [128, C, DFF], BF16)
        w2s = self.sb(f"w2s_{l}", [128, FC, D], BF16)
        bw1 = [Buf(f"w1_{l}")] * 8
        bw2 = [Buf(f"w2_{l}_{i}") for i in range(8)]
        for c in range(C):
            sc.dma("pool", lambda q, c=c: q.dma_start(out=w1s[:, c, :], in_=self.w1[l, c * 128:(c + 1) * 128, :]), writes=[bw1[0]], key="w1")
        for fb in range(8):
            f0 = fb * 4
            src = self.w2[l, f0 * 128:(f0 + 4) * 128, :].rearrange("(f p) d -> p f d", p=128)
            sc.dma("pool", lambda q, f0=f0, src=src: q.dma_start(out=w2s[:, f0:f0 + 4, :], in_=src), writes=[bw2[fb]], key=f"w2_{fb}")
        NX = 3
        x = [self.sb(f"m{l}_x{i}", [128, C, n], F32) for i in range(NX)]
        h = [self.sb(f"m{l}_h{i}", [128, C, n], BF16) for i in range(2)]
        aT = [self.sb(f"m{l}_a{i}", [128, FC, n], BF16) for i in range(2)]
        r32 = [self.sb(f"m{l}_r{i}", [128, n], F32) for i in range(2)]
        scr = dict(sq=self.sb(f"m{l}_sq", [128, C, n], F32), bsq=Buf("sq"), pss=self.ps(f"m{l}_pss", [128, 512]), bpss=Buf("pss"),
                   rt=self.sb(f"m{l}_rt", [128, n], F32), brt=Buf("rt"), rs=self.sb(f"m{l}_rs", [128, n], F32), brs=Buf("rs"))
        pa = [self.ps(f"m{l}_pa{i}", [128, 512]) for i in range(3)]
        py = [self.ps(f"m{l}_py{i}", [128, 512]) for i in range(2)]
        bx = [Buf(f"mx{i}") for i in range(NX)]
        bh = [Buf(f"mh{i}") for i in range(2)]
        ba = [[Buf(f"ma{i}_{f}") for f in range(FC)] for i in range(2)]
        br = [Buf(f"mr{i}") for i in range(2)]
        bpa = [Buf(f"mpa{i}") for i in range(3)]
        bpy = [Buf(f"mpy{i}") for i in range(2)]
        stores = []
        gcol = (l * 2 + 1) * C
        T = S // n

        def load(t):
            src = self.xT[:, t * n:(t + 1) * n].rearrange("(c p) n -> p c n", p=128)
            self._wait_xT("sp")
            sc.dma("sp", lambda q, src=src: q.dma_start(out=x[t % NX][:], in_=src), writes=[bx[t % NX]], key=f"mx{t % NX}")

        def stage1(t):
            hi, ai = t % 2, t % 2
            for f in range(FC):
                k = f % 3
                fns = [(lambda e, f=f, c=c, k=k: e.matmul(pa[k][:, :n], lhsT=w1s[:, c, f * 128:(f + 1) * 128], rhs=h[hi][:, c, :],
                                                         start=(c == 0), stop=(c == C - 1))) for c in range(C)]
                sc.op("pe", fns, reads=[bh[hi], bw1[f // 4]], writes=[bpa[k]])
                j = f % 2
                sc.op("act", lambda a, j=j, k=k: a.activation(out=r32[j][:], in_=pa[k][:, :n], func=AF.Relu), reads=[bpa[k]], writes=[br[j]])
                eng = "dve" if f % 2 == 0 else "pool"
                sc.op(eng, lambda v, j=j, f=f: v.tensor_tensor(out=aT[ai][:, f, :], in0=r32[j][:], in1=r32[j][:], op=ALU.mult),
                      reads=[br[j]], writes=[ba[ai][f]])

        def stage2(t):
            xi, ai = t % NX, t % 2
            for o in range(C):
                k = o % 2
                for fb in range(8):
                    fns = [(lambda e, o=o, f=f, k=k: e.matmul(py[k][:, :n], lhsT=w2s[:, f, o * 128:(o + 1) * 128], rhs=aT[ai][:, f, :],
                                                             start=(f == 0), stop=(f == FC - 1))) for f in range(fb * 4, fb * 4 + 4)]
                    sc.op("pe", fns, reads=ba[ai][fb * 4:fb * 4 + 4] + [bw2[fb]], writes=[bpy[k]])
                sc.op("dve", lambda v, o=o, k=k: v.tensor_tensor(out=x[xi][:, o, :], in0=py[k][:, :n], in1=x[xi][:, o, :], op=ALU.add),
                      reads=[bpy[k], bx[xi]], writes=[bx[xi]])
            dst = self.xT[:, t * n:(t + 1) * n].rearrange("(c p) n -> p c n", p=128)
            stores.append(sc.dma("sp", lambda q, dst=dst: q.dma_start(out=dst, in_=x[xi][:]), reads=[bx[xi]], writes=[self.b_xT], key=f"st{t % 2}"))

        load(0)
        self.norm(x[0], bx[0], h[0], bh[0], n, gcol, scr)
        for t in range(T):
            if t + 1 < T:
                load(t + 1)
            stage1(t)
            if t + 1 < T:
                self.norm(x[(t + 1) % NX], bx[(t + 1) % NX], h[(t + 1) % 2], bh[(t + 1) % 2], n, gcol, scr)
            if t >= 1:
                stage2(t - 1)
        stage2(T - 1)
        self.b_xT.r = []
        self._xT_tokens = stores


_CACHE = {}
HG_PAD = 0
SKIP_ATTN = False
ATT_SUB = "dve"
SELFWAIT_ENGINES = ()
HG_DEBUG = ""


def _consts_np():
    blk = np.zeros((128, 128), np.float32)
    blk[:64, :64] = 1.0 / 64
    blk[64:, 64:] = 1.0 / 64
    q = np.arange(128)[:, None]; s_ = np.arange(128)[None, :]
    mask = np.where(s_ >= q, -240000.0, 0.0).astype(np.float32)
    p = np.arange(128); ch = p // 64; u = p % 64
    same = ch[:, None] == ch[None, :]
    tri = same & (p[:, None] <= p[None, :])
    mid = same & (u[:, None] <= 31)
    M1 = np.zeros((128, 256), np.float32)
    M1[:, :128] = tri.astype(np.float32) - mid.astype(np.float32)
    for c in range(2):
        M1[:, 128 + c] = ((ch == c) & (u <= 31)).astype(np.float32)
        M1[:, 130 + c] = (ch == c).astype(np.float32)
    M2 = (same & (p[:, None] > p[None, :])).astype(np.float32)
    Mc = (same & (p[:, None] <= p[None, :])).astype(np.float32)
    return {"ident": np.eye(128, dtype=np.float32), "onesD": np.full((128, 128), 1.0 / D, np.float32),
            "blk64": blk, "identb": np.eye(128, dtype=np.float32), "maskneg": mask,
            "hg_M1": M1, "hg_M2": M2, "hg_Mc": np.ascontiguousarray(np.tile(Mc, (1, 8))), "ones128": np.full((128, 128), 1.0 / 128, np.float32)}


def kernel(x, norm_gains, sb_w_qkv, sb_q_gain, sb_k_gain, sb_w_o, hg_w_in, hg_lb_logits, hg_norm_gain, hg_w_o,
           mlp_w1, mlp_w2, _layers=(0, 1, 2, 3), _mixers=None, _mlps=None, _hg_blocks=None, _hg_parts=3, _allow_hgrn=False, _cores=None, _return_maps=False):
    key = (tuple(_layers), None if _mixers is None else tuple(_mixers), None if _mlps is None else tuple(_mlps), _hg_blocks, _hg_parts, _allow_hgrn, HG_DEBUG, HG_PAD, SELFWAIT_ENGINES, ATT_SUB, SKIP_ATTN)
    if key not in _CACHE:
        pr = Prog(list(_layers), _mixers, _mlps, hg_blocks=_hg_blocks, hg_parts=_hg_parts, allow_hgrn=_allow_hgrn)
        _CACHE[key] = (pr.build(), pr.used_inputs)
    nc, used = _CACHE[key]
    cst = _consts_np()
    ng = np.ascontiguousarray(np.asarray(norm_gains, np.float32).reshape(DEPTH * 2, C, 128).transpose(2, 0, 1).reshape(128, DEPTH * 2 * C))
    qg, kg = np.asarray(sb_q_gain, np.float32), np.asarray(sb_k_gain, np.float32)
    qk = np.stack([np.tile(qg[0], 2), np.tile(kg[0], 2), np.tile(qg[1], 2), np.tile(kg[1], 2)], axis=1)
    lbl = np.ascontiguousarray(hg_lb_logits, np.float32)
    lblT = np.ascontiguousarray(lbl.reshape(2, C, 128).transpose(2, 0, 1).reshape(128, 2 * C))
    shared = {"hg_w_in": np.ascontiguousarray(hg_w_in, np.float32), "hg_w_o": np.ascontiguousarray(hg_w_o, np.float32),
              "hg_lb_logits": lbl, "hg_lb_logits_T": lblT, "hg_gain_T": np.ascontiguousarray(np.asarray(hg_norm_gain, np.float32).T),
              "qk_gain": np.ascontiguousarray(qk), "sb_w_qkv": np.ascontiguousarray(sb_w_qkv, np.float32),
              "sb_w_o": np.ascontiguousarray(sb_w_o, np.float32), "norm_gains": ng, "mlp_w1": np.ascontiguousarray(mlp_w1, np.float32), "mlp_w2": np.ascontiguousarray(mlp_w2, np.float32), **cst}
    shared = {k: v for k, v in shared.items() if k in used}
    nb = _cores or 8
    in_maps = [dict(shared, x=np.ascontiguousarray(x[b], np.float32)) for b in range(nb)]
    if _return_maps:
        return nc, in_maps
    res = run_bass_kernel_spmd(nc, in_maps, core_ids=list(range(nb)))
    return np.stack([r["out"] for r in res.results], axis=0)
```

```python
import contextlib
import numpy as np
import concourse.bass as bass
import concourse.mybir as mybir
from concourse.bass_utils import run_bass_kernel_spmd

F32 = mybir.dt.float32
BF16 = mybir.dt.bfloat16
AF = mybir.ActivationFunctionType
ALU = mybir.AluOpType

D = 1024
S = 4096
DEPTH = 4
NH = 16
DH = 64
DFF = 4096
C = D // 128
NT = 512
EPS = 1e-6


class Buf:
    def __init__(self, name):
        self.name = name
        self.w = None
        self.r = []


class Sched:
    ENG = ("pe", "act", "dve", "pool", "sp")

    def __init__(self, nc, stack):
        self.nc = nc
        self.stack = stack
        self.sem = {e: stack.enter_context(nc.semaphore("s_" + e)) for e in self.ENG if e != "sp"}
        self.cnt = {e: 0 for e in self.sem}
        self.prog = {e: [] for e in self.ENG}
        self.seen = {e: {} for e in self.ENG}
        self.dsem = {}
        self.dcnt = {}

    def _need(self, eng, toks):
        best = {}
        for t in toks:
            if t is None:
                continue
            s, v = t
            if v > best.get(id(s), (s, 0))[1]:
                best[id(s)] = (s, v)
        for s, v in best.values():
            if self.seen[eng].get(id(s), 0) < v:
                self.seen[eng][id(s)] = v
                self.prog[eng].append(("wait", s, v))

    def _deps(self, reads, writes):
        toks = [b.w for b in reads]
        for b in writes:
            toks.append(b.w)
            toks.extend(b.r)
        return toks

    def op(self, eng, fns, reads=(), writes=()):
        if callable(fns):
            fns = [fns]
        n0 = len(self.prog[eng])
        deps = self._deps(reads, writes)
        if eng == "pe":
            deps = [t for t in deps if t is not None and t[0] is not self.sem["pe"]]
        self._need(eng, deps)
        if eng in SELFWAIT_ENGINES and len(self.prog[eng]) == n0 and self.cnt[eng] > 0:
            self.seen[eng][id(self.sem[eng])] = self.cnt[eng]
            self.prog[eng].append(("wait", self.sem[eng], self.cnt[eng]))
        self.cnt[eng] += 1
        tok = (self.sem[eng], self.cnt[eng])
        self.prog[eng].append(("op", fns, self.sem[eng], 1))
        for b in reads:
            b.r.append(tok)
        for b in writes:
            b.w = tok
            b.r = []
        return tok

    def dma(self, q, fn, reads=(), writes=(), key=None):
        key = q + "_" + (key or writes[0].name)
        if key not in self.dsem:
            self.dsem[key] = self.stack.enter_context(self.nc.semaphore("d_" + key))
            self.dcnt[key] = 0
        self._need(q, self._deps(reads, writes))
        self.dcnt[key] += 16
        tok = (self.dsem[key], self.dcnt[key])
        self.prog[q].append(("op", [fn], self.dsem[key], 16))
        for b in reads:
            b.r.append(tok)
        for b in writes:
            b.w = tok
            b.r = []
        return tok

    def barrier(self):
        toks = [(self.sem[e], self.cnt[e]) for e in self.sem if self.cnt[e]]
        toks += [(self.dsem[k], self.dcnt[k]) for k in self.dsem if self.dcnt[k]]
        for e in self.ENG:
            self._need(e, toks)

    def emit(self):
        nc = self.nc
        engobj = {"pe": "tensor", "act": "scalar", "dve": "vector", "pool": "gpsimd", "sp": "sync"}
        with nc.Block() as block:
            for e in self.ENG:
                prog = self.prog[e]

                def body(eng, prog=prog):
                    for item in prog:
                        if item[0] == "wait":
                            eng.wait_ge(item[1], item[2])
                        else:
                            _, fns, sem, inc = item
                            ins = None
                            n_before = nc.n_instructions() if inc == 16 else 0
                            for f in fns:
                                ins = f(eng)
                            if inc == 16 and nc.n_instructions() - n_before != 1:
                                raise RuntimeError(f"dma_start was split into {nc.n_instructions() - n_before} instructions: the tracker's +16 per DMA "
                                                   f"would be wrong (each piece adds 16). Reshape this transfer.")
                            ins.then_inc(sem, inc)
                getattr(block, engobj[e])(body)
        self.prog = {e: [] for e in self.ENG}


class Prog:
    def __init__(self, layers, mixers=None, mlps=None, hg_blocks=None, hg_parts=3, allow_hgrn=False):
        self.allow_hgrn = allow_hgrn
        self.hg_blocks = hg_blocks
        self.hg_parts = hg_parts
        self.layers = layers
        self.mixers = layers if mixers is None else mixers
        self.mlps = layers if mlps is None else mlps

    def build(self):
        nc = bass.Bass("TRN2", target_bir_lowering=False)
        self.nc = nc
        nc.allow_low_precision("bf16 matmul operands with fp32 PSUM accumulation (problem tolerance is set for this)")
        dt = nc.dram_tensor
        self.x_in = dt("x", [S, D], F32, kind="ExternalInput").ap()
        self.out = dt("out", [S, D], F32, kind="ExternalOutput").ap()
        self.ng = dt("norm_gains", [128, DEPTH * 2 * C], F32, kind="ExternalInput").ap()
        self.ident = dt("ident", [128, 128], F32, kind="ExternalInput").ap()
        self.onesD = dt("onesD", [128, 128], F32, kind="ExternalInput").ap()
        sbm = [l for l in self.layers if l in self.mixers and l % 2 == 0]
        hgm = [l for l in self.layers if l in self.mixers and l % 2 == 1]
        mlm = [l for l in self.layers if l in self.mlps]
        self.used_inputs = {"x", "norm_gains", "ident", "onesD"}
        self.xT = dt("xT_scratch", [D, S], F32, kind="Internal").ap()
        if mlm:
            self.w1 = dt("mlp_w1", [DEPTH, D, DFF], F32, kind="ExternalInput").ap()
            self.w2 = dt("mlp_w2", [DEPTH, DFF, D], F32, kind="ExternalInput").ap()
            self.used_inputs |= {"mlp_w1", "mlp_w2"}
        if sbm or hgm:
            self.identb = dt("identb", [128, 128], F32, kind="ExternalInput").ap()
            self.used_inputs |= {"identb"}
        if sbm:
            self.wqkv = dt("sb_w_qkv", [2, D, 3 * D], F32, kind="ExternalInput").ap()
            self.wo_sb = dt("sb_w_o", [2, D, D], F32, kind="ExternalInput").ap()
            self.qkg = dt("qk_gain", [128, 4], F32, kind="ExternalInput").ap()
            self.blk64 = dt("blk64", [128, 128], F32, kind="ExternalInput").ap()
            self.maskneg = dt("maskneg", [128, 128], F32, kind="ExternalInput").ap()
            self.QT = dt("QT_scratch", [D, S], BF16, kind="Internal").ap()
            self.KT = dt("KT_scratch", [D, S], BF16, kind="Internal").ap()
            self.V = dt("V_scratch", [S, D], BF16, kind="Internal").ap()
            self.used_inputs |= {"sb_w_qkv", "sb_w_o", "qk_gain", "blk64", "maskneg"}
        if hgm:
            self.w_in = dt("hg_w_in", [2, D, 4 * D], F32, kind="ExternalInput").ap()
            self.wo_hg = dt("hg_w_o", [2, D, D], F32, kind="ExternalInput").ap()
            self.lbl = dt("hg_lb_logits", [2, D], F32, kind="ExternalInput").ap()
            self.lblT = dt("hg_lb_logits_T", [128, 2 * C], F32, kind="ExternalInput").ap()
            self.hgg = dt("hg_gain_T", [128, 2], F32, kind="ExternalInput").ap()
            self.M1 = dt("hg_M1", [128, 256], F32, kind="ExternalInput").ap()
            self.M2 = dt("hg_M2", [128, 128], F32, kind="ExternalInput").ap()
            self.Mc = dt("hg_Mc", [128, 1024], F32, kind="ExternalInput").ap()
            self.ones128 = dt("ones128", [128, 128], F32, kind="ExternalInput").ap()
            self.used_inputs |= {"hg_w_in", "hg_w_o", "hg_lb_logits", "hg_lb_logits_T", "hg_gain_T", "hg_M1", "hg_M2", "hg_Mc", "ones128"}
        with contextlib.ExitStack() as st:
            self.st = st
            self.sc = Sched(nc, st)
            self.ph = st
            self._consts()
            self.run_phase(self.phase_in)
            for l in self.layers:
                if l in self.mixers:
                    if l % 2 == 0:
                        self.run_phase(self.phase_qkv, l)
                        if not SKIP_ATTN:
                            self.run_phase(self.phase_attn, l)
                    else:
                        self.run_phase(self.phase_hgrn, l)
                if l in self.mlps:
                    self.run_phase(self.phase_mlp, l)
            self.run_phase(self.phase_out)
        return nc

    def _uniq(self, name):
        self._nalloc = getattr(self, "_nalloc", 0) + 1
        return f"{name}_{self._nalloc}"

    def sb(self, name, shape, dtype):
        return self.ph.enter_context(self.nc.sbuf_tensor(self._uniq(name), shape, dtype))

    def ps(self, name, shape, dtype=F32):
        return self.ph.enter_context(self.nc.psum_tensor(self._uniq(name), shape, dtype))

    def run_phase(self, fn, *a):
        with contextlib.ExitStack() as ph:
            keep, self.ph = self.ph, ph
            fn(*a)
            self.sc.barrier()
            self.sc.emit()
            self.ph = keep

    def _consts(self):
        sc = self.sc
        self.ident_s = self.sb("ident_s", [128, 128], F32)
        self.ones_s = self.sb("ones_s", [128, 128], F32)
        self.ng_s = self.sb("ng_s", [128, DEPTH * 2 * C], F32)
        self.eps_s = self.sb("eps_s", [128, 1], F32)
        self.b_const = Buf("consts")
        sc.dma("sp", lambda q: q.dma_start(out=self.ident_s[:], in_=self.ident[:, :]), writes=[self.b_const], key="c0")
        b1 = Buf("c1"); b2 = Buf("c2"); self.b_eps = Buf("eps")
        sc.dma("sp", lambda q: q.dma_start(out=self.ones_s[:], in_=self.onesD[:, :]), writes=[b1], key="c1")
        sc.dma("sp", lambda q: q.dma_start(out=self.ng_s[:], in_=self.ng[:, :]), writes=[b2], key="c2")
        sc.op("dve", lambda v: v.memset(self.eps_s[:], EPS), writes=[self.b_eps])
        self.b_ones = b1
        self.b_ng = b2
        self.b_xT = Buf("xT_dram")

    def phase_in(self):
        sc, nc = self.sc, self.nc
        xin = [self.sb(f"pin_x{i}", [128, 4, D], F32) for i in range(2)]
        xo = [self.sb(f"pin_o{i}", [128, C, NT], F32) for i in range(2)]
        pst = [self.ps(f"pin_ps{i}", [128, NT]) for i in range(4)]
        b_in = [Buf(f"pin_x{i}") for i in range(2)]
        b_o = [Buf(f"pin_o{i}") for i in range(2)]
        b_ps = [Buf(f"pin_ps{i}") for i in range(4)]
        stores = []
        for t in range(S // NT):
            i = t % 2
            src = self.x_in[t * NT:(t + 1) * NT, :].rearrange("(j p) d -> p j d", p=128)
            sc.dma("sp", lambda q, i=i, src=src: q.dma_start(out=xin[i][:], in_=src), writes=[b_in[i]])
            for c in range(C):
                k = c % 4
                fns = [(lambda e, i=i, c=c, j=j, k=k: e.matmul(pst[k][:, j * 128:(j + 1) * 128],
                                                              lhsT=xin[i][:, j, c * 128:(c + 1) * 128],
                                                              rhs=self.ident_s[:], start=True, stop=True))
                       for j in range(4)]
                sc.op("pe", fns, reads=[b_in[i], self.b_const], writes=[b_ps[k]])
                eng = "dve" if c % 2 == 0 else "act"
                if eng == "dve":
                    sc.op("dve", lambda v, i=i, c=c, k=k: v.tensor_copy(out=xo[i][:, c, :], in_=pst[k][:]),
                          reads=[b_ps[k]], writes=[b_o[i]])
                else:
                    sc.op("act", lambda a, i=i, c=c, k=k: a.copy(out=xo[i][:, c, :], in_=pst[k][:]),
                          reads=[b_ps[k]], writes=[b_o[i]])
            dst = self.xT[:, t * NT:(t + 1) * NT].rearrange("(c p) n -> p c n", p=128)
            stores.append(sc.dma("sp", lambda q, i=i, dst=dst: q.dma_start(out=dst, in_=xo[i][:]),
                                 reads=[b_o[i]], writes=[self.b_xT], key=f"st{i}"))
        self.b_xT.r = []
        self._xT_tokens = stores

    def phase_out(self):
        sc = self.sc
        xi = [self.sb(f"po_x{i}", [128, C, NT], F32) for i in range(2)]
        xo = [self.sb(f"po_o{i}", [128, 4, D], F32) for i in range(2)]
        pst = [self.ps(f"po_ps{i}", [128, NT]) for i in range(4)]
        b_i = [Buf(f"po_x{i}") for i in range(2)]
        b_o = [Buf(f"po_o{i}") for i in range(2)]
        b_ps = [Buf(f"po_ps{i}") for i in range(4)]
        b_out = Buf("out_dram")
        toks = []
        for t in range(S // NT):
            i = t % 2
            src = self.xT[:, t * NT:(t + 1) * NT].rearrange("(c p) n -> p c n", p=128)
            self._wait_xT("sp")
            sc.dma("sp", lambda q, i=i, src=src: q.dma_start(out=xi[i][:], in_=src), writes=[b_i[i]])
            n = 0
            for j in range(4):
                for h in range(2):
                    k = n % 4
                    fns = [(lambda e, i=i, j=j, c=h * 4 + cc, cc=cc, k=k:
                            e.matmul(pst[k][:, cc * 128:(cc + 1) * 128], lhsT=xi[i][:, c, j * 128:(j + 1) * 128],
                                     rhs=self.ident_s[:], start=True, stop=True)) for cc in range(4)]
                    sc.op("pe", fns, reads=[b_i[i], self.b_const], writes=[b_ps[k]])
                    if n % 2 == 0:
                        sc.op("dve", lambda v, i=i, j=j, h=h, k=k: v.tensor_copy(out=xo[i][:, j, h * 512:(h + 1) * 512], in_=pst[k][:]),
                              reads=[b_ps[k]], writes=[b_o[i]])
                    else:
                        sc.op("act", lambda a, i=i, j=j, h=h, k=k: a.copy(out=xo[i][:, j, h * 512:(h + 1) * 512], in_=pst[k][:]),
                              reads=[b_ps[k]], writes=[b_o[i]])
                    n += 1
            dst = self.out[t * NT:(t + 1) * NT, :].rearrange("(j p) d -> p j d", p=128)
            toks.append(sc.dma("sp", lambda q, i=i, dst=dst: q.dma_start(out=dst, in_=xo[i][:]),
                               reads=[b_o[i]], writes=[b_out], key=f"st{i}"))
        self.sc._need("sp", toks)

    def phase_qkv(self, l):
        sc = self.sc
        j = l // 2
        n = NT
        wq = self.sb("wqkv_s", [128, C, 3 * D], BF16)
        bw = Buf("wqkv")
        for c in range(C):
            sc.dma("pool", lambda q, c=c: q.dma_start(out=wq[:, c, :], in_=self.wqkv[j, c * 128:(c + 1) * 128, :]), writes=[bw], key="w1")
        blk = self.sb("blk_s", [128, 128], F32); bblk = Buf("blk")
        qkg = self.sb("qkg_s", [128, 4], F32); bg = Buf("qkg")
        sc.dma("sp", lambda q: q.dma_start(out=blk[:], in_=self.blk64[:, :]), writes=[bblk], key="c0")
        sc.dma("sp", lambda q: q.dma_start(out=qkg[:], in_=self.qkg[:, :]), writes=[bg], key="c1")
        x = [self.sb(f"q_x{i}", [128, C, n], F32) for i in range(2)]
        h = [self.sb(f"q_h{i}", [128, C, n], BF16) for i in range(2)]
        scr = dict(sq=self.sb("q_sq", [128, C, n], F32), bsq=Buf("sq"), pss=self.ps("q_pss", [128, 512]), bpss=Buf("pss"),
                   rt=self.sb("q_rt", [128, n], F32), brt=Buf("rt"), rs=self.sb("q_rs", [128, n], F32), brs=Buf("rs"))
        pp = [self.ps(f"q_pp{i}", [128, 512]) for i in range(3)]
        pm = [self.ps(f"q_pm{i}", [128, 512]) for i in range(2)]
        pv = [self.ps(f"q_pv{i}", [128, 512]) for i in range(2)]
        bpp = [Buf(f"pp{i}") for i in range(3)]; bpm = [Buf(f"pm{i}") for i in range(2)]; bpv = [Buf(f"pv{i}") for i in range(2)]
        sq2 = [self.sb(f"q_sq2{i}", [128, n], F32) for i in range(2)]; bsq2 = [Buf(f"sq2{i}") for i in range(2)]
        rt2 = [self.sb(f"q_rt2{i}", [128, n], F32) for i in range(2)]; brt2 = [Buf(f"rt2{i}") for i in range(2)]
        rs2 = [self.sb(f"q_rs2{i}", [128, n], F32) for i in range(2)]; brs2 = [Buf(f"rs2{i}") for i in range(2)]
        qo = [self.sb(f"q_qo{i}", [128, n], BF16) for i in range(4)]; bqo = [Buf(f"qo{i}") for i in range(4)]
        vt = [self.sb(f"q_vt{i}", [128, 4, D], BF16) for i in range(2)]; bvt = [Buf(f"vt{i}") for i in range(2)]
        bx = [Buf(f"qx{i}") for i in range(2)]; bh = [Buf(f"qh{i}") for i in range(2)]
        b_scr = Buf("qkv_dram")
        gcol = (l * 2) * C
        nq = 0
        T = S // n

        def load(t):
            src = self.xT[:, t * n:(t + 1) * n].rearrange("(c p) n -> p c n", p=128)
            self._wait_xT("sp")
            sc.dma("sp", lambda q, src=src: q.dma_start(out=x[t % 2][:], in_=src), writes=[bx[t % 2]], key=f"qx{t % 2}")

        load(0)
        for t in range(T):
            i = t % 2
            self.norm(x[i], bx[i], h[i], bh[i], n, gcol, scr)
            if t + 1 < T:
                load(t + 1)

            def qk_proj(m, i=i):
                k, kp = m % 2, m % 3
                col = m * 128
                fns = [(lambda e, c=c: e.matmul(pp[kp][:], lhsT=wq[:, c, col:col + 128], rhs=h[i][:, c, :], start=(c == 0), stop=(c == C - 1)))
                       for c in range(C)]
                sc.op("pe", fns, reads=[bh[i], bw], writes=[bpp[kp]])
                sc.op("act", lambda a: a.activation(out=sq2[k][:], in_=pp[kp][:], func=AF.Square), reads=[bpp[kp]], writes=[bsq2[k]])

            def qk_fin(m, o, t=t):
                k, kp = m % 2, m % 3
                sc.op("pe", lambda e: e.matmul(pm[k][:], lhsT=blk[:], rhs=sq2[k][:], start=True, stop=True), reads=[bsq2[k], bblk], writes=[bpm[k]])
                sc.op("act", lambda a: a.activation(out=rt2[k][:], in_=pm[k][:], func=AF.Sqrt, bias=self.eps_s[:, 0:1], scale=1.0),
                      reads=[bpm[k], self.b_eps], writes=[brt2[k]])
                sc.op("dve", lambda v: v.reciprocal(out=rs2[k][:], in_=rt2[k][:]), reads=[brt2[k]], writes=[brs2[k]])
                gc = j * 2 + (0 if m < 8 else 1)
                sc.op("dve", lambda v: v.scalar_tensor_tensor(out=qo[o][:], in0=pp[kp][:], scalar=qkg[:, gc:gc + 1], in1=rs2[k][:], op0=ALU.mult, op1=ALU.mult),
                      reads=[bpp[kp], brs2[k], bg], writes=[bqo[o]])
                dstT = (self.QT if m < 8 else self.KT)[(m % 8) * 128:(m % 8 + 1) * 128, t * n:(t + 1) * n]
                sc.dma("sp", lambda q: q.dma_start(out=dstT, in_=qo[o][:]), reads=[bqo[o]], writes=[], key=f"qst{o}")

            for m in range(17):
                if m < 16:
                    qk_proj(m)
                if m >= 1:
                    qk_fin(m - 1, nq % 4)
                    nq += 1
            for jj in range(4):
                for half in range(2):
                    k = (jj * 2 + half) % 2
                    fns = [(lambda e, i=i, c=c, k=k, jj=jj, half=half: e.matmul(pv[k][:], lhsT=h[i][:, c, jj * 128:(jj + 1) * 128],
                                                                                 rhs=wq[:, c, 2 * D + half * 512:2 * D + (half + 1) * 512],
                                                                                 start=(c == 0), stop=(c == C - 1))) for c in range(C)]
                    sc.op("pe", fns, reads=[bh[i], bw], writes=[bpv[k]])
                    sc.op("act", lambda a, i=i, k=k, jj=jj, half=half: a.copy(out=vt[i][:, jj, half * 512:(half + 1) * 512], in_=pv[k][:]),
                          reads=[bpv[k]], writes=[bvt[i]])
            dstV = self.V[t * n:(t + 1) * n, :].rearrange("(j p) d -> p j d", p=128)
            sc.dma("sp", lambda q, i=i, dstV=dstV: q.dma_start(out=dstV, in_=vt[i][:]), reads=[bvt[i]], writes=[], key=f"vst{i}")

    def phase_attn(self, l):
        sc = self.sc
        j = l // 2
        idb = self.sb("idb_s", [128, 128], BF16); bidb = Buf("idb")
        mk = self.sb("mk_s", [128, 128], BF16); bmk = Buf("mk")
        sc.dma("pool", lambda q: q.dma_start(out=idb[:], in_=self.identb[:, :]), writes=[bidb], key="c0")
        sc.dma("pool", lambda q: q.dma_start(out=mk[:], in_=self.maskneg[:, :]), writes=[bmk], key="c1")
        wo = self.sb("wo_s", [128, C, D], BF16); bwo = Buf("wo")
        for c0 in range(0, C, 4):
            src = self.wo_sb[j, c0 * 128:(c0 + 4) * 128, :].rearrange("(c p) d -> p c d", p=128)
            sc.dma("pool", lambda q, c0=c0, src=src: q.dma_start(out=wo[:, c0:c0 + 4, :], in_=src), writes=[bwo], key="w1")
        OT = self.sb("OT_s", [128, C, S], BF16)
        bOT = [Buf(f"OT{hp}") for hp in range(C)]
        po = [self.ps(f"a_po{i}", [128, 512]) for i in range(2)]; bpo = [Buf(f"po{i}") for i in range(2)]
        self.run_phase(self._attn_core, l, OT, bOT, po, bpo, idb, bidb, mk, bmk)
        self._attn_wo(l, OT, bOT, po, bpo, wo, bwo)

    def _attn_core(self, l, OT, bOT, po, bpo, idb, bidb, mk, bmk):
        sc = self.sc
        qT = [self.sb(f"a_q{i}", [128, S], BF16) for i in range(1)] * 2; bq = [Buf("aq")] * 2
        kT = [self.sb(f"a_k{i}", [128, S], BF16) for i in range(1)] * 2; bk = [Buf("ak")] * 2
        VA = [self.sb(f"a_va{i}", [128, 32, 128], BF16) for i in range(1)] * 2; bva = [Buf("va")] * 2
        VB = [self.sb(f"a_vb{i}", [128, 32, 128], BF16) for i in range(1)] * 2; bvb = [Buf("vb")] * 2
        for i in range(1):
            sc.op("pool", lambda v, i=i: v.memset(VA[i][:], 0.0), writes=[bva[i]])
            sc.op("pool", lambda v, i=i: v.memset(VB[i][:], 0.0), writes=[bvb[i]])
        CH = 1024
        g = [self.sb(f"a_g{i}", [128, CH], F32) for i in range(2)]; bgt = [Buf(f"ag{i}") for i in range(2)]
        Pf = [self.sb(f"a_P{i}", [128, S + 1], F32) for i in range(2)]; bP = [Buf(f"aP{i}") for i in range(2)]
        zer = self.sb("a_zero", [128, CH], F32); bz = Buf("zero")
        sc.op("pool", lambda v: v.memset(zer[:], 0.0), writes=[bz])
        w = [self.sb(f"a_w{i}", [128, CH], BF16) for i in range(2)]; bwt = [Buf(f"aw{i}") for i in range(2)]
        wT = [self.sb(f"a_wT{i}", [128, 8, 128], BF16) for i in range(2)]; bwT = [Buf(f"awT{i}") for i in range(2)]
        pz = [self.ps(f"a_pz{i}", [128, CH]) for i in range(2)]; bpz = [Buf(f"pz{i}") for i in range(2)]
        pt = self.ps("a_pt", [128, CH]); bpt = Buf("pt")
        chunks = []
        for hp in range(C):
            for Q in range(S // 128):
                t1 = 128 * (Q + 1)
                for hd in range(2):
                    b_ = t1
                    while b_ > 0:
                        a_ = max(0, b_ - CH)
                        chunks.append(dict(hp=hp, Q=Q, hd=hd, a=a_, b=b_, L=b_ - a_, t1=t1, first=(b_ == t1), newhp=(Q == 0 and hd == 0 and b_ == t1),
                                           first_pv=(hd == 0 and b_ == t1), last=(hd == 1 and a_ == 0),
                                           ip=(hp * 64 + Q * 2 + hd) % 2, io=(hp * 32 + Q) % 2))
                        b_ = a_
        for n_, ch in enumerate(chunks):
            ch["iz"] = n_ % 2

        def stage_A(ch):
            hp, Q, hd, a, b, L, iz = ch["hp"], ch["Q"], ch["hd"], ch["a"], ch["b"], ch["L"], ch["iz"]
            i = 0
            if ch["newhp"]:
                rows = slice(hp * 128, (hp + 1) * 128)
                sc.dma("sp", lambda q, rows=rows: q.dma_start(out=qT[i][:], in_=self.QT[rows, :]), writes=[bq[i]], key="ld0")
                sc.dma("sp", lambda q, rows=rows: q.dma_start(out=kT[i][:], in_=self.KT[rows, :]), writes=[bk[i]], key="ldk0")
                srcA = self.V[:, hp * 128:hp * 128 + 64].rearrange("(b p) d -> p b d", p=128)
                srcB = self.V[:, hp * 128 + 64:hp * 128 + 128].rearrange("(b p) d -> p b d", p=128)
                for b0 in range(0, 32, 8):
                    sc.dma("sp", lambda q, srcA=srcA, b0=b0: q.dma_start(out=VA[i][:, b0:b0 + 8, 0:64], in_=srcA[:, b0:b0 + 8, :]), writes=[bva[i]], key="lda")
                    sc.dma("sp", lambda q, srcB=srcB, b0=b0: q.dma_start(out=VB[i][:, b0:b0 + 8, 64:128], in_=srcB[:, b0:b0 + 8, :]), writes=[bvb[i]], key="ldb")
            pr = slice(hd * 64, hd * 64 + 64)
            fns = []
            for p0 in range(0, L, 512):
                pl = min(512, L - p0)
                diag = ch["first"] and (p0 + pl == L)
                fns.append(lambda e, p0=p0, pl=pl, diag=diag: e.matmul(pz[iz][:, p0:p0 + pl], lhsT=qT[i][pr, Q * 128:(Q + 1) * 128],
                                                                       rhs=kT[i][pr, a + p0:a + p0 + pl], start=True, stop=not diag))
                if diag:
                    fns.append(lambda e: e.matmul(pz[iz][:, L - 128:L], lhsT=idb[:], rhs=mk[:], start=False, stop=True))
            sc.op("pe", fns, reads=[bq[i], bk[i], bidb, bmk], writes=[bpz[iz]])
            sc.op("act", lambda a_: a_.activation(out=g[iz][:, :L], in_=pz[iz][:, :L], func=AF.Sigmoid, scale=-0.125), reads=[bpz[iz]], writes=[bgt[iz]])

        def stage_B(ch):
            a, b, L, iz, ip, t1 = ch["a"], ch["b"], ch["L"], ch["iz"], ch["ip"], ch["t1"]
            if ch["newhp"]:
                for ipx in range(2):
                    sc.op("dve", lambda v, ipx=ipx: v.memset(Pf[ipx][:, 128:S + 1:128], 1.0), writes=[bP[ipx]])
            sc.op("dve", lambda v: v.tensor_tensor_scan(out=Pf[ip][:, b - 1:(a - 1 if a > 0 else None):-1], data0=g[iz][:, L - 1::-1], data1=zer[:, :L],
                                                        initial=Pf[ip][:, b:b + 1], op0=ALU.mult, op1=ALU.add), reads=[bgt[iz], bP[ip], bz], writes=[bP[ip]])
            eng = {"pool": "pool", "dve": "dve", "alt": ("pool" if iz == 0 else "dve")}[ATT_SUB]
            sc.op(eng, lambda v: v.tensor_tensor(out=w[iz][:, :L], in0=Pf[ip][:, a + 1:b + 1], in1=Pf[ip][:, a:b], op=ALU.subtract),
                  reads=[bP[ip]], writes=[bwt[iz]])

        def stage_C1(ch):
            L, iz = ch["L"], ch["iz"]
            nb = L // 128
            fns = [(lambda e, bb=bb: e.matmul(pt[:, bb * 128:(bb + 1) * 128], lhsT=w[iz][:, bb * 128:(bb + 1) * 128], rhs=idb[:], start=True, stop=True))
                   for bb in range(nb)]
            sc.op("pe", fns, reads=[bwt[iz], bidb], writes=[bpt])
            sc.op("act", lambda a_: a_.copy(out=wT[iz][:, :nb, :], in_=pt[:, :nb * 128]), reads=[bpt], writes=[bwT[iz]])

        def stage_C2(ch):
            hp, Q, hd, a, L, iz, io = ch["hp"], ch["Q"], ch["hd"], ch["a"], ch["L"], ch["iz"], ch["io"]
            i = 0
            Vh, bV = (VA[i], bva[i]) if hd == 0 else (VB[i], bvb[i])
            nb = L // 128
            fns = []
            for bb in range(nb):
                kb = a // 128 + bb
                fns.append(lambda e, bb=bb, kb=kb, fp=(ch["first_pv"] and bb == 0), last=(ch["last"] and bb == nb - 1):
                           e.matmul(po[io][:, :128], lhsT=Vh[:, kb, :], rhs=wT[iz][:, bb, :], start=fp, stop=last))
            sc.op("pe", fns, reads=[bwT[iz], bV], writes=[bpo[io]])
            if ch["last"]:
                sc.op("dve", lambda v: v.tensor_copy(out=OT[:, hp, Q * 128:(Q + 1) * 128], in_=po[io][:, :128]), reads=[bpo[io]], writes=[bOT[hp]])

        nchk = len(chunks)
        for n_ in range(nchk + 3):
            if n_ < nchk:
                stage_A(chunks[n_])
            if 0 <= n_ - 1 < nchk:
                stage_B(chunks[n_ - 1])
            if 0 <= n_ - 2 < nchk:
                stage_C1(chunks[n_ - 2])
            if 0 <= n_ - 3 < nchk:
                stage_C2(chunks[n_ - 3])

    def _attn_wo(self, l, OT, bOT, po, bpo, wo, bwo):
        sc = self.sc
        n = NT
        x = [self.sb(f"o_x{i}", [128, C, n], F32) for i in range(2)]; bx = [Buf(f"ox{i}") for i in range(2)]
        stores = []
        for t in range(S // n):
            i = t % 2
            src = self.xT[:, t * n:(t + 1) * n].rearrange("(c p) n -> p c n", p=128)
            self._wait_xT("sp")
            sc.dma("sp", lambda q, i=i, src=src: q.dma_start(out=x[i][:], in_=src), writes=[bx[i]], key=f"ldx{i}")
            for o in range(C):
                k = o % 2
                fns = [(lambda e, o=o, c=c, k=k, t=t: e.matmul(po[k][:], lhsT=wo[:, c, o * 128:(o + 1) * 128], rhs=OT[:, c, t * n:(t + 1) * n],
                                                             start=(c == 0), stop=(c == C - 1))) for c in range(C)]
                sc.op("pe", fns, reads=bOT + [bwo], writes=[bpo[k]])
                sc.op("dve", lambda v, i=i, o=o, k=k: v.tensor_tensor(out=x[i][:, o, :], in0=po[k][:], in1=x[i][:, o, :], op=ALU.add),
                      reads=[bpo[k], bx[i]], writes=[bx[i]])
            dst = self.xT[:, t * n:(t + 1) * n].rearrange("(c p) n -> p c n", p=128)
            stores.append(sc.dma("sp", lambda q, i=i, dst=dst: q.dma_start(out=dst, in_=x[i][:]), reads=[bx[i]], writes=[self.b_xT], key=f"st{i}"))
        self.b_xT.r = []
        self._xT_tokens = stores

    def phase_hgrn(self, l):
        sc = self.sc
        j = l // 2
        NB = 128
        HH = 8
        win = self.sb("win_s", [128, C, 4 * D], BF16); bwin = Buf("win")
        for c in range(C):
            for hf in range(2):
                sc.dma("pool", lambda q, c=c, hf=hf: q.dma_start(out=win[:, c, hf * 2048:(hf + 1) * 2048],
                                                                  in_=self.w_in[j, c * 128:(c + 1) * 128, hf * 2048:(hf + 1) * 2048]), writes=[bwin], key="w1")
        wo = self.sb("hwo_s", [128, C, D], BF16); bwo = Buf("hwo")
        for c0 in range(0, C, 4):
            src = self.wo_hg[j, c0 * 128:(c0 + 4) * 128, :].rearrange("(c p) d -> p c d", p=128)
            sc.dma("pool", lambda q, c0=c0, src=src: q.dma_start(out=wo[:, c0:c0 + 4, :], in_=src), writes=[bwo], key="w2")
        M1 = self.sb("M1_s", [128, 256], F32); M2 = self.sb("M2_s", [128, 128], F32); Mc = self.sb("Mc_s", [128, 1024], F32)
        o128 = self.sb("o128_s", [128, 128], F32); idb = self.sb("hidb_s", [128, 128], BF16)
        lrow = self.sb("lrow_s", [128, 2, D], F32)
        lT = self.sb("lT_s", [128, 2 * C], F32); gg = self.sb("gg_s", [128, 2], F32)
        bK = Buf("hconst")
        sc.dma("sp", lambda q: q.dma_start(out=M1[:], in_=self.M1[:, :]), writes=[bK], key="c0")
        sc.dma("sp", lambda q: q.dma_start(out=M2[:], in_=self.M2[:, :]), writes=[bK], key="c0")
        sc.dma("sp", lambda q: q.dma_start(out=Mc[:], in_=self.Mc[:, :]), writes=[bK], key="c0")
        sc.dma("sp", lambda q: q.dma_start(out=o128[:], in_=self.ones128[:, :]), writes=[bK], key="c0")
        sc.dma("sp", lambda q: q.dma_start(out=lT[:], in_=self.lblT[:, :]), writes=[bK], key="c0")
        sc.dma("sp", lambda q: q.dma_start(out=gg[:], in_=self.hgg[:, :]), writes=[bK], key="c0")
        for jj in range(2):
            sc.dma("sp", lambda q, jj=jj: q.dma_start(out=lrow[:, jj, :], in_=self.lbl[jj:jj + 1, :].partition_broadcast(128)), writes=[bK], key="c0")
        omr = self.sb("omr_s", [128, D], F32); omT = self.sb("omT_s", [128, C], F32); bom = Buf("om")
        one1 = self.sb("one1_s", [128, 1], F32)
        sc.op("dve", lambda v: v.memset(one1[:], 1.0), writes=[bom])
        if j == 0:
            sc.op("dve", lambda v: v.memset(omr[:], 1.0), writes=[bom])
            sc.op("dve", lambda v: v.memset(omT[:], 1.0), writes=[bom])
        else:
            dr = self.sb("dr_s", [128, D], F32); dT = self.sb("dT_s", [128, C], F32); bd = Buf("dlt")
            sc.op("dve", lambda v: v.tensor_tensor(out=dr[:], in0=lrow[:, 0, :], in1=lrow[:, 1, :], op=ALU.subtract), reads=[bK], writes=[bd])
            sc.op("dve", lambda v: v.tensor_tensor(out=dT[:], in0=lT[:, 0:C], in1=lT[:, C:2 * C], op=ALU.subtract), reads=[bK], writes=[bd])
            sc.op("act", lambda a: a.activation(out=omr[:], in_=dr[:], func=AF.Sigmoid), reads=[bd], writes=[bom])
            sc.op("act", lambda a: a.activation(out=omT[:], in_=dT[:], func=AF.Sigmoid), reads=[bd], writes=[bom])
        sc.dma("pool", lambda q: q.dma_start(out=idb[:], in_=self.identb[:, :]), writes=[bK], key="c1")
        St = self.sb("S_s", [128, HH, 128], F32); bS = [Buf(f"S{h}") for h in range(HH)]
        Sb = self.sb("Sb_s", [128, HH, 128], BF16); bSb = [Buf(f"Sb{h}") for h in range(HH)]
        sc.op("pool", lambda v: v.memset(St[:], 0.0), writes=bS)
        x = [self.sb(f"g_x{i}", [128, C, NB], F32) for i in range(2)]; bx = [Buf(f"gx{i}") for i in range(2)]
        h2 = [self.sb(f"g_h{i}", [128, C, NB], BF16) for i in range(2)]; bh2 = [Buf(f"gh{i}") for i in range(2)]
        scr = dict(sq=self.sb("g_sq", [128, C, NB], F32), bsq=Buf("sq"), pss=self.ps("g_pss", [128, 512]), bpss=Buf("pss"),
                   rt=self.sb("g_rt", [128, NB], F32), brt=Buf("rt"), rs=self.sb("g_rs", [128, NB], F32), brs=Buf("rs"))
        pA = self.ps("g_pA", [128, 1024]); bpA = [Buf("pA0"), Buf("pA1")]
        pB = self.ps("g_pB", [128, 1024]); bpB = [Buf("pB0"), Buf("pB1")]
        pX = self.ps("g_pX", [128, 1024]); bpX = [Buf("pX0"), Buf("pX1")]
        pE = self.ps("g_pE", [128, 512]); bpE = Buf("pE")
        W = HH * NB

        def t2(name, dtype=F32):
            return self.sb(name, [128, W], dtype), Buf(name)
        sbar, bsb = t2("g_sbar"); ktok, bkt = t2("g_ktok"); lf, blf = t2("g_lf")
        vtok, bvt = t2("g_vtok", BF16); khat, bkh = t2("g_khat", BF16)
        ed, bed = sbar, bsb
        qraw, bqr = t2("g_qraw"); graw, bgr = t2("g_graw"); fraw, bfr = t2("g_fraw")
        sgq, bsgq = t2("g_sgq"); sgg, bsgg = t2("g_sgg"); kf, bkf = t2("g_kf"); qf, bqf = t2("g_qf")
        E1, bE1 = t2("g_E1"); E2, bE2 = t2("g_E2")
        EX = self.sb("g_EX", [128, 4 * HH], F32); bEX = Buf("EX")
        qt, bqt = t2("g_qt", BF16); kt, bkt2 = t2("g_kt", BF16); scT, bsc = t2("g_scT", BF16)
        oT, boT = t2("g_oT"); osq, bosq = t2("g_osq"); ort, bort = t2("g_ort"); on, bon = t2("g_on", BF16)
        omF, bomF = t2("g_omF")
        sc.op("dve", lambda v: v.memset(omF[:], 1.0), writes=[bomF])
        if j != 0:
            for hh in range(HH):
                sc.op("dve", lambda v, hh=hh: v.tensor_scalar_mul(out=omF[:, hh * 128:(hh + 1) * 128], in0=omF[:, hh * 128:(hh + 1) * 128], scalar1=omT[:, hh:hh + 1]),
                      reads=[bom, bomF], writes=[bomF])
        hsl = lambda hh: slice(hh * 128, (hh + 1) * 128)
        bchain = Buf("act_chain")
        gcol = (l * 2) * C
        stores = []
        nblk = self.hg_blocks or (S // NB)

        def load(t):
            src = self.xT[:, t * NB:(t + 1) * NB].rearrange("(c p) n -> p c n", p=128)
            self._wait_xT("sp")
            sc.dma("sp", lambda q, src=src: q.dma_start(out=x[t % 2][:], in_=src), writes=[bx[t % 2]], key=f"ldx{t % 2}")

        def stage1a(t):
            h, bh = h2[t % 2], bh2[t % 2]
            for (pp, bpp, off) in ((pA, bpA, D), (pB, bpB, 2 * D)):
                fns = []
                for hf in range(2):
                    for c in range(C):
                        fns.append(lambda e, pp=pp, off=off, hf=hf, c=c, h=h: e.matmul(pp[:, hf * 512:(hf + 1) * 512], lhsT=h[:, c, :],
                                                                                  rhs=win[:, c, off + hf * 512:off + (hf + 1) * 512],
                                                                                  start=(c == 0), stop=(c == C - 1)))
                sc.op("pe", fns, reads=[bh, bwin], writes=bpp)
            sc.op("act", lambda a: a.activation(out=sbar[:], in_=pA[:], func=AF.Sigmoid, scale=-1.0), reads=bpA, writes=[bsb])
            sc.op("act", lambda a: a.copy(out=vtok[:], in_=pB[:]), reads=bpB, writes=[bvt])
            sc.op("dve", lambda v: v.tensor_tensor(out=ktok[:], in0=sbar[:], in1=omr[:], op=ALU.mult), reads=[bsb, bom], writes=[bkt])
            sc.op("act", lambda a: a.activation(out=lf[:], in_=ktok[:], func=AF.Ln, bias=one1[:, 0:1], scale=-1.0), reads=[bkt, bom], writes=[blf])

        def stage1b(t):
            fns = [(lambda e, hf=hf: e.matmul(pA[:, hf * 512:(hf + 1) * 512], lhsT=M2[:], rhs=lf[:, hf * 512:(hf + 1) * 512], start=True, stop=True))
                   for hf in range(2)]
            sc.op("pe", fns, reads=[blf, bK], writes=bpA)
            sc.op("act", lambda a: a.activation(out=ed[:], in_=pA[:], func=AF.Exp), reads=bpA, writes=[bed])
            sc.op("dve", lambda v: v.tensor_tensor(out=khat[:], in0=ed[:], in1=ktok[:], op=ALU.mult), reads=[bed, bkt], writes=[bkh])

        def stage2(t):
            h, bh = h2[t % 2], bh2[t % 2]
            if self.hg_parts >= 2:
                def proj(dst, base):
                    return [(lambda e, hh=hh, c=c, h=h: e.matmul(dst[:, hsl(hh)], lhsT=win[:, c, base + hh * 128:base + (hh + 1) * 128], rhs=h[:, c, :],
                                                            start=(c == 0), stop=(c == C - 1))) for hh in range(HH) for c in range(C)]
                for dst, bdst, base, raw_, braw_ in ((pX, bpX, 0, qraw, bqr), (pA, bpA, 3 * D, graw, bgr), (pB, bpB, D, fraw, bfr)):
                    sc.op("pe", proj(dst, base), reads=[bh, bwin], writes=bdst)
                    sc.op("dve", lambda v, dst=dst, raw_=raw_: v.tensor_copy(out=raw_[:], in_=dst[:]), reads=bdst, writes=[braw_])
                sc.op("act", lambda a: a.activation(out=sgq[:], in_=qraw[:], func=AF.Sigmoid), reads=[bqr], writes=[bsgq])
                sc.op("act", lambda a: a.activation(out=sgg[:], in_=graw[:], func=AF.Sigmoid), reads=[bgr], writes=[bsgg])
                sc.op("act", lambda a: a.activation(out=kf[:], in_=fraw[:], func=AF.Sigmoid, scale=-1.0), reads=[bfr], writes=[bkf])
                sc.op("dve", lambda v: v.tensor_tensor(out=qf[:], in0=qraw[:], in1=sgq[:], op=ALU.mult), reads=[bqr, bsgq], writes=[bqf])
                sc.op("pe", [(lambda e, hh=hh: e.matmul(pX[:, hsl(hh)], lhsT=lf[:, hsl(hh)], rhs=M1[:, 0:128], start=True, stop=True)) for hh in range(HH)],
                      reads=[blf, bK], writes=bpX)
                sc.op("pe", [(lambda e, hh=hh: e.matmul(pE[:, hh * 4:(hh + 1) * 4], lhsT=lf[:, hsl(hh)], rhs=M1[:, 128:132], start=True, stop=True)) for hh in range(HH)],
                      reads=[blf, bK], writes=[bpE])
                sc.op("act", lambda a: a.activation(out=E1[:], in_=pX[:], func=AF.Exp), reads=bpX, writes=[bE1])
                sc.op("act", lambda a: a.activation(out=E2[:], in_=pX[:], func=AF.Exp, scale=-1.0), reads=bpX, writes=[bE2])
                sc.op("act", lambda a: a.activation(out=EX[:], in_=pE[:, 0:4 * HH], func=AF.Exp), reads=[bpE], writes=[bEX])
                sc.op("dve", lambda v: v.tensor_tensor(out=qt[:], in0=qf[:], in1=E1[:], op=ALU.mult), reads=[bqf, bE1], writes=[bqt])
                sc.op("dve", lambda v: v.tensor_tensor(out=kf[:], in0=kf[:], in1=E2[:], op=ALU.mult), reads=[bkf, bE2], writes=[bkf])
                sc.op("dve", lambda v: v.tensor_tensor(out=kt[:], in0=kf[:], in1=omF[:], op=ALU.mult), reads=[bkf, bomF], writes=[bkt2])

        def rec(t):
            if self.hg_parts >= 3:
                sc.op("pe", [(lambda e, hh=hh: e.matmul(pA[:, hsl(hh)], lhsT=kt[:, hsl(hh)], rhs=qt[:, hsl(hh)], start=True, stop=True)) for hh in range(HH)],
                      reads=[bkt2, bqt], writes=bpA)
                sc.op("dve", lambda v: v.tensor_tensor(out=scT[:], in0=pA[:], in1=Mc[:], op=ALU.mult), reads=bpA + [bK], writes=[bsc])
                for cc in range(2):
                    cs = slice(cc * 64, cc * 64 + 64)
                    for hh in range(HH):
                        sc.op("act", lambda a, hh=hh, cc=cc: a.activation(out=Sb[:, hh, :], in_=St[:, hh, :], func=AF.Copy, scale=EX[:, hh * 4 + cc:hh * 4 + cc + 1]),
                              reads=[bS[hh], bEX], writes=[bSb[hh], bchain])
                    fns = []
                    for hh in range(HH):
                        col = slice(hh * 128 + cc * 64, hh * 128 + cc * 64 + 64)
                        fns.append(lambda e, hh=hh, col=col, cs=cs: e.matmul(pB[:, col], lhsT=vtok[cs, hsl(hh)], rhs=scT[cs, col], start=True, stop=False))
                        fns.append(lambda e, hh=hh, col=col: e.matmul(pB[:, col], lhsT=Sb[:, hh, :], rhs=qt[:, col], start=False, stop=True))
                    sc.op("pe", fns, reads=[bvt, bsc, bqt] + bSb, writes=bpB)
                    sc.op("pe", [(lambda e, hh=hh, cs=cs: e.matmul(pX[:, hsl(hh)], lhsT=khat[cs, hsl(hh)], rhs=vtok[cs, hsl(hh)], start=True, stop=True)) for hh in range(HH)],
                          reads=[bkh, bvt], writes=bpX)
                    for hh in range(HH):
                        sc.op("dve", lambda v, hh=hh, cc=cc: v.scalar_tensor_tensor(out=St[:, hh, :], in0=St[:, hh, :], scalar=EX[:, hh * 4 + 2 + cc:hh * 4 + 3 + cc],
                                                                                      in1=pX[:, hsl(hh)], op0=ALU.mult, op1=ALU.add),
                              reads=[bS[hh], bEX] + bpX, writes=[bS[hh]])
                sc.op("dve", lambda v: v.tensor_copy(out=oT[:], in_=pB[:]), reads=bpB, writes=[boT])

        def post_a(t):
            i = t % 2
            if self.hg_parts >= 3:
                sc.op("act", lambda a: a.activation(out=osq[:], in_=oT[:], func=AF.Square), reads=[boT], writes=[bosq])
                sc.op("pe", [(lambda e, hf=hf: e.matmul(pX[:, hf * 512:(hf + 1) * 512], lhsT=o128[:], rhs=osq[:, hf * 512:(hf + 1) * 512], start=True, stop=True))
                             for hf in range(2)], reads=[bosq, bK], writes=bpX)
                sc.op("act", lambda a: a.activation(out=ort[:], in_=pX[:], func=AF.Sqrt, bias=self.eps_s[:, 0:1], scale=1.0), reads=bpX + [self.b_eps], writes=[bort])
                sc.op("dve", lambda v: v.reciprocal(out=osq[:], in_=ort[:]), reads=[bort], writes=[bosq])
                sc.op("dve", lambda v: v.scalar_tensor_tensor(out=ort[:], in0=oT[:], scalar=gg[:, j:j + 1], in1=osq[:], op0=ALU.mult, op1=ALU.mult),
                      reads=[boT, bosq, bK], writes=[bort])
                sc.op("dve", lambda v: v.tensor_tensor(out=on[:], in0=ort[:], in1=sgg[:], op=ALU.mult), reads=[bort, bsgg], writes=[bon])

        def post_b(t):
            i = t % 2
            if self.hg_parts >= 3:
                sc.op("pe", [(lambda e, o=o, c=c: e.matmul(pX[:, hsl(o)], lhsT=wo[:, c, o * 128:(o + 1) * 128], rhs=on[:, hsl(c)], start=(c == 0), stop=(c == C - 1)))
                             for o in range(C) for c in range(C)], reads=[bon, bwo], writes=bpX)
                for o in range(C):
                    sc.op("dve", lambda v, i=i, o=o: v.tensor_tensor(out=x[i][:, o, :], in0=pX[:, hsl(o)], in1=x[i][:, o, :], op=ALU.add),
                          reads=bpX + [bx[i]], writes=[bx[i]])
            dst = self.xT[:, t * NB:(t + 1) * NB].rearrange("(c p) n -> p c n", p=128)
            stores.append(sc.dma("sp", lambda q, i=i, dst=dst: q.dma_start(out=dst, in_=x[i][:]), reads=[bx[i]], writes=[self.b_xT], key=f"st{i}"))

        load(0)
        self.norm(x[0], bx[0], h2[0], bh2[0], NB, gcol, scr)
        stage1a(0)
        stage1b(0)
        for t in range(nblk):
            if t + 1 < nblk:
                load(t + 1)
            stage2(t)
            if t + 1 < nblk:
                self.norm(x[(t + 1) % 2], bx[(t + 1) % 2], h2[(t + 1) % 2], bh2[(t + 1) % 2], NB, gcol, scr)
            rec(t)
            post_a(t)
            if t + 1 < nblk:
                stage1a(t + 1)
            post_b(t)
            if t + 1 < nblk:
                stage1b(t + 1)
        self.b_xT.r = []
        self._xT_tokens = stores

    def _wait_xT(self, eng):
        self.sc._need(eng, self._xT_tokens)

    def norm(self, x, bx, h, bh, n, gcol, scr):
        sc = self.sc
        sq, bsq, pss, bpss, rt, brt, rs, brs = (scr[k] for k in ("sq", "bsq", "pss", "bpss", "rt", "brt", "rs", "brs"))
        sc.op("act", lambda a: a.activation(out=sq[:, :, :n], in_=x[:, :, :n], func=AF.Square), reads=[bx], writes=[bsq])
        fns = [(lambda e, c=c: e.matmul(pss[:, :n], lhsT=self.ones_s[:], rhs=sq[:, c, :n], start=(c == 0), stop=(c == C - 1)))
               for c in range(C)]
        sc.op("pe", fns, reads=[bsq, self.b_ones], writes=[bpss])
        sc.op("act", lambda a: a.activation(out=rt[:, :n], in_=pss[:, :n], func=AF.Sqrt, bias=self.eps_s[:, 0:1], scale=1.0),
              reads=[bpss, self.b_eps], writes=[brt])
        sc.op("dve", lambda v: v.reciprocal(out=rs[:, :n], in_=rt[:, :n]), reads=[brt], writes=[brs])
        for c in range(C):
            eng = "dve"
            sc.op(eng, lambda v, c=c: v.scalar_tensor_tensor(out=h[:, c, :n], in0=x[:, c, :n],
                                                             scalar=self.ng_s[:, gcol + c:gcol + c + 1], in1=rs[:, :n],
                                                             op0=ALU.mult, op1=ALU.mult),
                  reads=[bx, brs, self.b_ng], writes=[bh])

    def phase_mlp(self, l):
        sc = self.sc
        n = 256
        FC = DFF // 128
        w1s = self.sb(f"w1s_{l}", [128, C, DFF], BF16)
        w2s = self.sb(f"w2s_{l}", [128, FC, D], BF16)
        bw1 = [Buf(f"w1_{l}")] * 8
        bw2 = [Buf(f"w2_{l}_{i}") for i in range(8)]
        for c in range(C):
            sc.dma("pool", lambda q, c=c: q.dma_start(out=w1s[:, c, :], in_=self.w1[l, c * 128:(c + 1) * 128, :]), writes=[bw1[0]], key="w1")
        for fb in range(8):
            f0 = fb * 4
            src = self.w2[l, f0 * 128:(f0 + 4) * 128, :].rearrange("(f p) d -> p f d", p=128)
            sc.dma("pool", lambda q, f0=f0, src=src: q.dma_start(out=w2s[:, f0:f0 + 4, :], in_=src), writes=[bw2[fb]], key=f"w2_{fb}")
        NX = 3
        x = [self.sb(f"m{l}_x{i}", [128, C, n], F32) for i in range(NX)]
        h = [self.sb(f"m{l}_h{i}", [128, C, n], BF16) for i in range(2)]
        aT = [self.sb(f"m{l}_a{i}", [128, FC, n], BF16) for i in range(2)]
        r32 = [self.sb(f"m{l}_r{i}", [128, n], F32) for i in range(2)]
        scr = dict(sq=self.sb(f"m{l}_sq", [128, C, n], F32), bsq=Buf("sq"), pss=self.ps(f"m{l}_pss", [128, 512]), bpss=Buf("pss"),
                   rt=self.sb(f"m{l}_rt", [128, n], F32), brt=Buf("rt"), rs=self.sb(f"m{l}_rs", [128, n], F32), brs=Buf("rs"))
        pa = [self.ps(f"m{l}_pa{i}", [128, 512]) for i in range(3)]
        py = [self.ps(f"m{l}_py{i}", [128, 512]) for i in range(2)]
        bx = [Buf(f"mx{i}") for i in range(NX)]
        bh = [Buf(f"mh{i}") for i in range(2)]
        ba = [[Buf(f"ma{i}_{f}") for f in range(FC)] for i in range(2)]
        br = [Buf(f"mr{i}") for i in range(2)]
        bpa = [Buf(f"mpa{i}") for i in range(3)]
        bpy = [Buf(f"mpy{i}") for i in range(2)]
        stores = []
        gcol = (l * 2 + 1) * C
        T = S // n

        def load(t):
            src = self.xT[:, t * n:(t + 1) * n].rearrange("(c p) n -> p c n", p=128)
            self._wait_xT("sp")
            sc.dma("sp", lambda q, src=src: q.dma_start(out=x[t % NX][:], in_=src), writes=[bx[t % NX]], key=f"mx{t % NX}")

        def stage1(t):
            hi, ai = t % 2, t % 2
            for f in range(FC):
                k = f % 3
                fns = [(lambda e, f=f, c=c, k=k: e.matmul(pa[k][:, :n], lhsT=w1s[:, c, f * 128:(f + 1) * 128], rhs=h[hi][:, c, :],
                                                         start=(c == 0), stop=(c == C - 1))) for c in range(C)]
                sc.op("pe", fns, reads=[bh[hi], bw1[f // 4]], writes=[bpa[k]])
                j = f % 2
                sc.op("act", lambda a, j=j, k=k: a.activation(out=r32[j][:], in_=pa[k][:, :n], func=AF.Relu), reads=[bpa[k]], writes=[br[j]])
                eng = "dve" if f % 2 == 0 else "pool"
                sc.op(eng, lambda v, j=j, f=f: v.tensor_tensor(out=aT[ai][:, f, :], in0=r32[j][:], in1=r32[j][:], op=ALU.mult),
                      reads=[br[j]], writes=[ba[ai][f]])

        def stage2(t):
            xi, ai = t % NX, t % 2
            for o in range(C):
                k = o % 2
                for fb in range(8):
                    fns = [(lambda e, o=o, f=f, k=k: e.matmul(py[k][:, :n], lhsT=w2s[:, f, o * 128:(o + 1) * 128], rhs=aT[ai][:, f, :],
                                                             start=(f == 0), stop=(f == FC - 1))) for f in range(fb * 4, fb * 4 + 4)]
                    sc.op("pe", fns, reads=ba[ai][fb * 4:fb * 4 + 4] + [bw2[fb]], writes=[bpy[k]])
                sc.op("dve", lambda v, o=o, k=k: v.tensor_tensor(out=x[xi][:, o, :], in0=py[k][:, :n], in1=x[xi][:, o, :], op=ALU.add),
                      reads=[bpy[k], bx[xi]], writes=[bx[xi]])
            dst = self.xT[:, t * n:(t + 1) * n].rearrange("(c p) n -> p c n", p=128)
            stores.append(sc.dma("sp", lambda q, dst=dst: q.dma_start(out=dst, in_=x[xi][:]), reads=[bx[xi]], writes=[self.b_xT], key=f"st{t % 2}"))

        load(0)
        self.norm(x[0], bx[0], h[0], bh[0], n, gcol, scr)
        for t in range(T):
            if t + 1 < T:
                load(t + 1)
            stage1(t)
            if t + 1 < T:
                self.norm(x[(t + 1) % NX], bx[(t + 1) % NX], h[(t + 1) % 2], bh[(t + 1) % 2], n, gcol, scr)
            if t >= 1:
                stage2(t - 1)
        stage2(T - 1)
        self.b_xT.r = []
        self._xT_tokens = stores


_CACHE = {}
HG_PAD = 0
SKIP_ATTN = False
ATT_SUB = "dve"
SELFWAIT_ENGINES = ()
HG_DEBUG = ""


def _consts_np():
    blk = np.zeros((128, 128), np.float32)
    blk[:64, :64] = 1.0 / 64
    blk[64:, 64:] = 1.0 / 64
    q = np.arange(128)[:, None]; s_ = np.arange(128)[None, :]
    mask = np.where(s_ >= q, -240000.0, 0.0).astype(np.float32)
    p = np.arange(128); ch = p // 64; u = p % 64
    same = ch[:, None] == ch[None, :]
    tri = same & (p[:, None] <= p[None, :])
    mid = same & (u[:, None] <= 31)
    M1 = np.zeros((128, 256), np.float32)
    M1[:, :128] = tri.astype(np.float32) - mid.astype(np.float32)
    for c in range(2):
        M1[:, 128 + c] = ((ch == c) & (u <= 31)).astype(np.float32)
        M1[:, 130 + c] = (ch == c).astype(np.float32)
    M2 = (same & (p[:, None] > p[None, :])).astype(np.float32)
    Mc = (same & (p[:, None] <= p[None, :])).astype(np.float32)
    return {"ident": np.eye(128, dtype=np.float32), "onesD": np.full((128, 128), 1.0 / D, np.float32),
            "blk64": blk, "identb": np.eye(128, dtype=np.float32), "maskneg": mask,
            "hg_M1": M1, "hg_M2": M2, "hg_Mc": np.ascontiguousarray(np.tile(Mc, (1, 8))), "ones128": np.full((128, 128), 1.0 / 128, np.float32)}


def kernel(x, norm_gains, sb_w_qkv, sb_q_gain, sb_k_gain, sb_w_o, hg_w_in, hg_lb_logits, hg_norm_gain, hg_w_o,
           mlp_w1, mlp_w2, _layers=(0, 1, 2, 3), _mixers=None, _mlps=None, _hg_blocks=None, _hg_parts=3, _allow_hgrn=False, _cores=None, _return_maps=False):
    key = (tuple(_layers), None if _mixers is None else tuple(_mixers), None if _mlps is None else tuple(_mlps), _hg_blocks, _hg_parts, _allow_hgrn, HG_DEBUG, HG_PAD, SELFWAIT_ENGINES, ATT_SUB, SKIP_ATTN)
    if key not in _CACHE:
        pr = Prog(list(_layers), _mixers, _mlps, hg_blocks=_hg_blocks, hg_parts=_hg_parts, allow_hgrn=_allow_hgrn)
        _CACHE[key] = (pr.build(), pr.used_inputs)
    nc, used = _CACHE[key]
    cst = _consts_np()
    ng = np.ascontiguousarray(np.asarray(norm_gains, np.float32).reshape(DEPTH * 2, C, 128).transpose(2, 0, 1).reshape(128, DEPTH * 2 * C))
    qg, kg = np.asarray(sb_q_gain, np.float32), np.asarray(sb_k_gain, np.float32)
    qk = np.stack([np.tile(qg[0], 2), np.tile(kg[0], 2), np.tile(qg[1], 2), np.tile(kg[1], 2)], axis=1)
    lbl = np.ascontiguousarray(hg_lb_logits, np.float32)
    lblT = np.ascontiguousarray(lbl.reshape(2, C, 128).transpose(2, 0, 1).reshape(128, 2 * C))
    shared = {"hg_w_in": np.ascontiguousarray(hg_w_in, np.float32), "hg_w_o": np.ascontiguousarray(hg_w_o, np.float32),
              "hg_lb_logits": lbl, "hg_lb_logits_T": lblT, "hg_gain_T": np.ascontiguousarray(np.asarray(hg_norm_gain, np.float32).T),
              "qk_gain": np.ascontiguousarray(qk), "sb_w_qkv": np.ascontiguousarray(sb_w_qkv, np.float32),
              "sb_w_o": np.ascontiguousarray(sb_w_o, np.float32), "norm_gains": ng, "mlp_w1": np.ascontiguousarray(mlp_w1, np.float32), "mlp_w2": np.ascontiguousarray(mlp_w2, np.float32), **cst}
    shared = {k: v for k, v in shared.items() if k in used}
    nb = _cores or 8
    in_maps = [dict(shared, x=np.ascontiguousarray(x[b], np.float32)) for b in range(nb)]
    if _return_maps:
        return nc, in_maps
    res = run_bass_kernel_spmd(nc, in_maps, core_ids=list(range(nb)))
    return np.stack([r["out"] for r in res.results], axis=0)
```

```python
import contextlib
import numpy as np
import concourse.bass as bass
import concourse.mybir as mybir
from concourse.bass_utils import run_bass_kernel_spmd

F32 = mybir.dt.float32
BF16 = mybir.dt.bfloat16
AF = mybir.ActivationFunctionType
ALU = mybir.AluOpType

D = 1024
S = 4096
DEPTH = 4
NH = 16
DH = 64
DFF = 4096
C = D // 128
NT = 512
EPS = 1e-6


class Buf:
    def __init__(self, name):
        self.name = name
        self.w = None
        self.r = []


class Sched:
    ENG = ("pe", "act", "dve", "pool", "sp")

    def __init__(self, nc, stack):
        self.nc = nc
        self.stack = stack
        self.sem = {e: stack.enter_context(nc.semaphore("s_" + e)) for e in self.ENG if e != "sp"}
        self.cnt = {e: 0 for e in self.sem}
        self.prog = {e: [] for e in self.ENG}
        self.seen = {e: {} for e in self.ENG}
        self.dsem = {}
        self.dcnt = {}

    def _need(self, eng, toks):
        best = {}
        for t in toks:
            if t is None:
                continue
            s, v = t
            if v > best.get(id(s), (s, 0))[1]:
                best[id(s)] = (s, v)
        for s, v in best.values():
            if self.seen[eng].get(id(s), 0) < v:
                self.seen[eng][id(s)] = v
                self.prog[eng].append(("wait", s, v))

    def _deps(self, reads, writes):
        toks = [b.w for b in reads]
        for b in writes:
            toks.append(b.w)
            toks.extend(b.r)
        return toks

    def op(self, eng, fns, reads=(), writes=()):
        if callable(fns):
            fns = [fns]
        n0 = len(self.prog[eng])
        deps = self._deps(reads, writes)
        if eng == "pe":
            deps = [t for t in deps if t is not None and t[0] is not self.sem["pe"]]
        self._need(eng, deps)
        if eng in SELFWAIT_ENGINES and len(self.prog[eng]) == n0 and self.cnt[eng] > 0:
            self.seen[eng][id(self.sem[eng])] = self.cnt[eng]
            self.prog[eng].append(("wait", self.sem[eng], self.cnt[eng]))
        self.cnt[eng] += 1
        tok = (self.sem[eng], self.cnt[eng])
        self.prog[eng].append(("op", fns, self.sem[eng], 1))
        for b in reads:
            b.r.append(tok)
        for b in writes:
            b.w = tok
            b.r = []
        return tok

    def dma(self, q, fn, reads=(), writes=(), key=None):
        key = q + "_" + (key or writes[0].name)
        if key not in self.dsem:
            self.dsem[key] = self.stack.enter_context(self.nc.semaphore("d_" + key))
            self.dcnt[key] = 0
        self._need(q, self._deps(reads, writes))
        self.dcnt[key] += 16
        tok = (self.dsem[key], self.dcnt[key])
        self.prog[q].append(("op", [fn], self.dsem[key], 16))
        for b in reads:
            b.r.append(tok)
        for b in writes:
            b.w = tok
            b.r = []
        return tok

    def barrier(self):
        toks = [(self.sem[e], self.cnt[e]) for e in self.sem if self.cnt[e]]
        toks += [(self.dsem[k], self.dcnt[k]) for k in self.dsem if self.dcnt[k]]
        for e in self.ENG:
            self._need(e, toks)

    def emit(self):
        nc = self.nc
        engobj = {"pe": "tensor", "act": "scalar", "dve": "vector", "pool": "gpsimd", "sp": "sync"}
        with nc.Block() as block:
            for e in self.ENG:
                prog = self.prog[e]

                def body(eng, prog=prog):
                    for item in prog:
                        if item[0] == "wait":
                            eng.wait_ge(item[1], item[2])
                        else:
                            _, fns, sem, inc = item
                            ins = None
                            n_before = nc.n_instructions() if inc == 16 else 0
                            for f in fns:
                                ins = f(eng)
                            if inc == 16 and nc.n_instructions() - n_before != 1:
                                raise RuntimeError(f"dma_start was split into {nc.n_instructions() - n_before} instructions: the tracker's +16 per DMA "
                                                   f"would be wrong (each piece adds 16). Reshape this transfer.")
                            ins.then_inc(sem, inc)
                getattr(block, engobj[e])(body)
        self.prog = {e: [] for e in self.ENG}


class Prog:
    def __init__(self, layers, mixers=None, mlps=None, hg_blocks=None, hg_parts=3, allow_hgrn=False):
        self.allow_hgrn = allow_hgrn
        self.hg_blocks = hg_blocks
        self.hg_parts = hg_parts
        self.layers = layers
        self.mixers = layers if mixers is None else mixers
        self.mlps = layers if mlps is None else mlps

    def build(self):
        nc = bass.Bass("TRN2", target_bir_lowering=False)
        self.nc = nc
        nc.allow_low_precision("bf16 matmul operands with fp32 PSUM accumulation (problem tolerance is set for this)")
        dt = nc.dram_tensor
        self.x_in = dt("x", [S, D], F32, kind="ExternalInput").ap()
        self.out = dt("out", [S, D], F32, kind="ExternalOutput").ap()
        self.ng = dt("norm_gains", [128, DEPTH * 2 * C], F32, kind="ExternalInput").ap()
        self.ident = dt("ident", [128, 128], F32, kind="ExternalInput").ap()
        self.onesD = dt("onesD", [128, 128], F32, kind="ExternalInput").ap()
        sbm = [l for l in self.layers if l in self.mixers and l % 2 == 0]
        hgm = [l for l in self.layers if l in self.mixers and l % 2 == 1]
        mlm = [l for l in self.layers if l in self.mlps]
        self.used_inputs = {"x", "norm_gains", "ident", "onesD"}
        self.xT = dt("xT_scratch", [D, S], F32, kind="Internal").ap()
        if mlm:
            self.w1 = dt("mlp_w1", [DEPTH, D, DFF], F32, kind="ExternalInput").ap()
            self.w2 = dt("mlp_w2", [DEPTH, DFF, D], F32, kind="ExternalInput").ap()
            self.used_inputs |= {"mlp_w1", "mlp_w2"}
        if sbm or hgm:
            self.identb = dt("identb", [128, 128], F32, kind="ExternalInput").ap()
            self.used_inputs |= {"identb"}
        if sbm:
            self.wqkv = dt("sb_w_qkv", [2, D, 3 * D], F32, kind="ExternalInput").ap()
            self.wo_sb = dt("sb_w_o", [2, D, D], F32, kind="ExternalInput").ap()
            self.qkg = dt("qk_gain", [128, 4], F32, kind="ExternalInput").ap()
            self.blk64 = dt("blk64", [128, 128], F32, kind="ExternalInput").ap()
            self.maskneg = dt("maskneg", [128, 128], F32, kind="ExternalInput").ap()
            self.QT = dt("QT_scratch", [D, S], BF16, kind="Internal").ap()
            self.KT = dt("KT_scratch", [D, S], BF16, kind="Internal").ap()
            self.V = dt("V_scratch", [S, D], BF16, kind="Internal").ap()
            self.used_inputs |= {"sb_w_qkv", "sb_w_o", "qk_gain", "blk64", "maskneg"}
        if hgm:
            self.w_in = dt("hg_w_in", [2, D, 4 * D], F32, kind="ExternalInput").ap()
            self.wo_hg = dt("hg_w_o", [2, D, D], F32, kind="ExternalInput").ap()
            self.lbl = dt("hg_lb_logits", [2, D], F32, kind="ExternalInput").ap()
            self.lblT = dt("hg_lb_logits_T", [128, 2 * C], F32, kind="ExternalInput").ap()
            self.hgg = dt("hg_gain_T", [128, 2], F32, kind="ExternalInput").ap()
            self.M1 = dt("hg_M1", [128, 256], F32, kind="ExternalInput").ap()
            self.M2 = dt("hg_M2", [128, 128], F32, kind="ExternalInput").ap()
            self.Mc = dt("hg_Mc", [128, 1024], F32, kind="ExternalInput").ap()
            self.ones128 = dt("ones128", [128, 128], F32, kind="ExternalInput").ap()
            self.used_inputs |= {"hg_w_in", "hg_w_o", "hg_lb_logits", "hg_lb_logits_T", "hg_gain_T", "hg_M1", "hg_M2", "hg_Mc", "ones128"}
        with contextlib.ExitStack() as st:
            self.st = st
            self.sc = Sched(nc, st)
            self.ph = st
            self._consts()
            self.run_phase(self.phase_in)
            for l in self.layers:
                if l in self.mixers:
                    if l % 2 == 0:
                        self.run_phase(self.phase_qkv, l)
                        if not SKIP_ATTN:
                            self.run_phase(self.phase_attn, l)
                    else:
                        self.run_phase(self.phase_hgrn, l)
                if l in self.mlps:
                    self.run_phase(self.phase_mlp, l)
            self.run_phase(self.phase_out)
        return nc

    def _uniq(self, name):
        self._nalloc = getattr(self, "_nalloc", 0) + 1
        return f"{name}_{self._nalloc}"

    def sb(self, name, shape, dtype):
        return self.ph.enter_context(self.nc.sbuf_tensor(self._uniq(name), shape, dtype))

    def ps(self, name, shape, dtype=F32):
        return self.ph.enter_context(self.nc.psum_tensor(self._uniq(name), shape, dtype))

    def run_phase(self, fn, *a):
        with contextlib.ExitStack() as ph:
            keep, self.ph = self.ph, ph
            fn(*a)
            self.sc.barrier()
            self.sc.emit()
            self.ph = keep

    def _consts(self):
        sc = self.sc
        self.ident_s = self.sb("ident_s", [128, 128], F32)
        self.ones_s = self.sb("ones_s", [128, 128], F32)
        self.ng_s = self.sb("ng_s", [128, DEPTH * 2 * C], F32)
        self.eps_s = self.sb("eps_s", [128, 1], F32)
        self.b_const = Buf("consts")
        sc.dma("sp", lambda q: q.dma_start(out=self.ident_s[:], in_=self.ident[:, :]), writes=[self.b_const], key="c0")
        b1 = Buf("c1"); b2 = Buf("c2"); self.b_eps = Buf("eps")
        sc.dma("sp", lambda q: q.dma_start(out=self.ones_s[:], in_=self.onesD[:, :]), writes=[b1], key="c1")
        sc.dma("sp", lambda q: q.dma_start(out=self.ng_s[:], in_=self.ng[:, :]), writes=[b2], key="c2")
        sc.op("dve", lambda v: v.memset(self.eps_s[:], EPS), writes=[self.b_eps])
        self.b_ones = b1
        self.b_ng = b2
        self.b_xT = Buf("xT_dram")

    def phase_in(self):
        sc, nc = self.sc, self.nc
        xin = [self.sb(f"pin_x{i}", [128, 4, D], F32) for i in range(2)]
        xo = [self.sb(f"pin_o{i}", [128, C, NT], F32) for i in range(2)]
        pst = [self.ps(f"pin_ps{i}", [128, NT]) for i in range(4)]
        b_in = [Buf(f"pin_x{i}") for i in range(2)]
        b_o = [Buf(f"pin_o{i}") for i in range(2)]
        b_ps = [Buf(f"pin_ps{i}") for i in range(4)]
        stores = []
        for t in range(S // NT):
            i = t % 2
            src = self.x_in[t * NT:(t + 1) * NT, :].rearrange("(j p) d -> p j d", p=128)
            sc.dma("sp", lambda q, i=i, src=src: q.dma_start(out=xin[i][:], in_=src), writes=[b_in[i]])
            for c in range(C):
                k = c % 4
                fns = [(lambda e, i=i, c=c, j=j, k=k: e.matmul(pst[k][:, j * 128:(j + 1) * 128],
                                                              lhsT=xin[i][:, j, c * 128:(c + 1) * 128],
                                                              rhs=self.ident_s[:], start=True, stop=True))
                       for j in range(4)]
                sc.op("pe", fns, reads=[b_in[i], self.b_const], writes=[b_ps[k]])
                eng = "dve" if c % 2 == 0 else "act"
                if eng == "dve":
                    sc.op("dve", lambda v, i=i, c=c, k=k: v.tensor_copy(out=xo[i][:, c, :], in_=pst[k][:]),
                          reads=[b_ps[k]], writes=[b_o[i]])
                else:
                    sc.op("act", lambda a, i=i, c=c, k=k: a.copy(out=xo[i][:, c, :], in_=pst[k][:]),
                          reads=[b_ps[k]], writes=[b_o[i]])
            dst = self.xT[:, t * NT:(t + 1) * NT].rearrange("(c p) n -> p c n", p=128)
            stores.append(sc.dma("sp", lambda q, i=i, dst=dst: q.dma_start(out=dst, in_=xo[i][:]),
                                 reads=[b_o[i]], writes=[self.b_xT], key=f"st{i}"))
        self.b_xT.r = []
        self._xT_tokens = stores

    def phase_out(self):
        sc = self.sc
        xi = [self.sb(f"po_x{i}", [128, C, NT], F32) for i in range(2)]
        xo = [self.sb(f"po_o{i}", [128, 4, D], F32) for i in range(2)]
        pst = [self.ps(f"po_ps{i}", [128, NT]) for i in range(4)]
        b_i = [Buf(f"po_x{i}") for i in range(2)]
        b_o = [Buf(f"po_o{i}") for i in range(2)]
        b_ps = [Buf(f"po_ps{i}") for i in range(4)]
        b_out = Buf("out_dram")
        toks = []
        for t in range(S // NT):
            i = t % 2
            src = self.xT[:, t * NT:(t + 1) * NT].rearrange("(c p) n -> p c n", p=128)
            self._wait_xT("sp")
            sc.dma("sp", lambda q, i=i, src=src: q.dma_start(out=xi[i][:], in_=src), writes=[b_i[i]])
            n = 0
            for j in range(4):
                for h in range(2):
                    k = n % 4
                    fns = [(lambda e, i=i, j=j, c=h * 4 + cc, cc=cc, k=k:
                            e.matmul(pst[k][:, cc * 128:(cc + 1) * 128], lhsT=xi[i][:, c, j * 128:(j + 1) * 128],
                                     rhs=self.ident_s[:], start=True, stop=True)) for cc in range(4)]
                    sc.op("pe", fns, reads=[b_i[i], self.b_const], writes=[b_ps[k]])
                    if n % 2 == 0:
                        sc.op("dve", lambda v, i=i, j=j, h=h, k=k: v.tensor_copy(out=xo[i][:, j, h * 512:(h + 1) * 512], in_=pst[k][:]),
                              reads=[b_ps[k]], writes=[b_o[i]])
                    else:
                        sc.op("act", lambda a, i=i, j=j, h=h, k=k: a.copy(out=xo[i][:, j, h * 512:(h + 1) * 512], in_=pst[k][:]),
                              reads=[b_ps[k]], writes=[b_o[i]])
                    n += 1
            dst = self.out[t * NT:(t + 1) * NT, :].rearrange("(j p) d -> p j d", p=128)
            toks.append(sc.dma("sp", lambda q, i=i, dst=dst: q.dma_start(out=dst, in_=xo[i][:]),
                               reads=[b_o[i]], writes=[b_out], key=f"st{i}"))
        self.sc._need("sp", toks)

    def phase_qkv(self, l):
        sc = self.sc
        j = l // 2
        n = NT
        wq = self.sb("wqkv_s", [128, C, 3 * D], BF16)
        bw = Buf("wqkv")
        for c in range(C):
            sc.dma("pool", lambda q, c=c: q.dma_start(out=wq[:, c, :], in_=self.wqkv[j, c * 128:(c + 1) * 128, :]), writes=[bw], key="w1")
        blk = self.sb("blk_s", [128, 128], F32); bblk = Buf("blk")
        qkg = self.sb("qkg_s", [128, 4], F32); bg = Buf("qkg")
        sc.dma("sp", lambda q: q.dma_start(out=blk[:], in_=self.blk64[:, :]), writes=[bblk], key="c0")
        sc.dma("sp", lambda q: q.dma_start(out=qkg[:], in_=self.qkg[:, :]), writes=[bg], key="c1")
        x = [self.sb(f"q_x{i}", [128, C, n], F32) for i in range(2)]
        h = [self.sb(f"q_h{i}", [128, C, n], BF16) for i in range(2)]
        scr = dict(sq=self.sb("q_sq", [128, C, n], F32), bsq=Buf("sq"), pss=self.ps("q_pss", [128, 512]), bpss=Buf("pss"),
                   rt=self.sb("q_rt", [128, n], F32), brt=Buf("rt"), rs=self.sb("q_rs", [128, n], F32), brs=Buf("rs"))
        pp = [self.ps(f"q_pp{i}", [128, 512]) for i in range(3)]
        pm = [self.ps(f"q_pm{i}", [128, 512]) for i in range(2)]
        pv = [self.ps(f"q_pv{i}", [128, 512]) for i in range(2)]
        bpp = [Buf(f"pp{i}") for i in range(3)]; bpm = [Buf(f"pm{i}") for i in range(2)]; bpv = [Buf(f"pv{i}") for i in range(2)]
        sq2 = [self.sb(f"q_sq2{i}", [128, n], F32) for i in range(2)]; bsq2 = [Buf(f"sq2{i}") for i in range(2)]
        rt2 = [self.sb(f"q_rt2{i}", [128, n], F32) for i in range(2)]; brt2 = [Buf(f"rt2{i}") for i in range(2)]
        rs2 = [self.sb(f"q_rs2{i}", [128, n], F32) for i in range(2)]; brs2 = [Buf(f"rs2{i}") for i in range(2)]
        qo = [self.sb(f"q_qo{i}", [128, n], BF16) for i in range(4)]; bqo = [Buf(f"qo{i}") for i in range(4)]
        vt = [self.sb(f"q_vt{i}", [128, 4, D], BF16) for i in range(2)]; bvt = [Buf(f"vt{i}") for i in range(2)]
        bx = [Buf(f"qx{i}") for i in range(2)]; bh = [Buf(f"qh{i}") for i in range(2)]
        b_scr = Buf("qkv_dram")
        gcol = (l * 2) * C
        nq = 0
        T = S // n

        def load(t):
            src = self.xT[:, t * n:(t + 1) * n].rearrange("(c p) n -> p c n", p=128)
            self._wait_xT("sp")
            sc.dma("sp", lambda q, src=src: q.dma_start(out=x[t % 2][:], in_=src), writes=[bx[t % 2]], key=f"qx{t % 2}")

        load(0)
        for t in range(T):
            i = t % 2
            self.norm(x[i], bx[i], h[i], bh[i], n, gcol, scr)
            if t + 1 < T:
                load(t + 1)

            def qk_proj(m, i=i):
                k, kp = m % 2, m % 3
                col = m * 128
                fns = [(lambda e, c=c: e.matmul(pp[kp][:], lhsT=wq[:, c, col:col + 128], rhs=h[i][:, c, :], start=(c == 0), stop=(c == C - 1)))
                       for c in range(C)]
                sc.op("pe", fns, reads=[bh[i], bw], writes=[bpp[kp]])
                sc.op("act", lambda a: a.activation(out=sq2[k][:], in_=pp[kp][:], func=AF.Square), reads=[bpp[kp]], writes=[bsq2[k]])

            def qk_fin(m, o, t=t):
                k, kp = m % 2, m % 3
                sc.op("pe", lambda e: e.matmul(pm[k][:], lhsT=blk[:], rhs=sq2[k][:], start=True, stop=True), reads=[bsq2[k], bblk], writes=[bpm[k]])
                sc.op("act", lambda a: a.activation(out=rt2[k][:], in_=pm[k][:], func=AF.Sqrt, bias=self.eps_s[:, 0:1], scale=1.0),
                      reads=[bpm[k], self.b_eps], writes=[brt2[k]])
                sc.op("dve", lambda v: v.reciprocal(out=rs2[k][:], in_=rt2[k][:]), reads=[brt2[k]], writes=[brs2[k]])
                gc = j * 2 + (0 if m < 8 else 1)
                sc.op("dve", lambda v: v.scalar_tensor_tensor(out=qo[o][:], in0=pp[kp][:], scalar=qkg[:, gc:gc + 1], in1=rs2[k][:], op0=ALU.mult, op1=ALU.mult),
                      reads=[bpp[kp], brs2[k], bg], writes=[bqo[o]])
                dstT = (self.QT if m < 8 else self.KT)[(m % 8) * 128:(m % 8 + 1) * 128, t * n:(t + 1) * n]
                sc.dma("sp", lambda q: q.dma_start(out=dstT, in_=qo[o][:]), reads=[bqo[o]], writes=[], key=f"qst{o}")

            for m in range(17):
                if m < 16:
                    qk_proj(m)
                if m >= 1:
                    qk_fin(m - 1, nq % 4)
                    nq += 1
            for jj in range(4):
                for half in range(2):
                    k = (jj * 2 + half) % 2
                    fns = [(lambda e, i=i, c=c, k=k, jj=jj, half=half: e.matmul(pv[k][:], lhsT=h[i][:, c, jj * 128:(jj + 1) * 128],
                                                                                 rhs=wq[:, c, 2 * D + half * 512:2 * D + (half + 1) * 512],
                                                                                 start=(c == 0), stop=(c == C - 1))) for c in range(C)]
                    sc.op("pe", fns, reads=[bh[i], bw], writes=[bpv[k]])
                    sc.op("act", lambda a, i=i, k=k, jj=jj, half=half: a.copy(out=vt[i][:, jj, half * 512:(half + 1) * 512], in_=pv[k][:]),
                          reads=[bpv[k]], writes=[bvt[i]])
            dstV = self.V[t * n:(t + 1) * n, :].rearrange("(j p) d -> p j d", p=128)
            sc.dma("sp", lambda q, i=i, dstV=dstV: q.dma_start(out=dstV, in_=vt[i][:]), reads=[bvt[i]], writes=[], key=f"vst{i}")

    def phase_attn(self, l):
        sc = self.sc
        j = l // 2
        idb = self.sb("idb_s", [128, 128], BF16); bidb = Buf("idb")
        mk = self.sb("mk_s", [128, 128], BF16); bmk = Buf("mk")
        sc.dma("pool", lambda q: q.dma_start(out=idb[:], in_=self.identb[:, :]), writes=[bidb], key="c0")
        sc.dma("pool", lambda q: q.dma_start(out=mk[:], in_=self.maskneg[:, :]), writes=[bmk], key="c1")
        wo = self.sb("wo_s", [128, C, D], BF16); bwo = Buf("wo")
        for c0 in range(0, C, 4):
            src = self.wo_sb[j, c0 * 128:(c0 + 4) * 128, :].rearrange("(c p) d -> p c d", p=128)
            sc.dma("pool", lambda q, c0=c0, src=src: q.dma_start(out=wo[:, c0:c0 + 4, :], in_=src), writes=[bwo], key="w1")
        OT = self.sb("OT_s", [128, C, S], BF16)
        bOT = [Buf(f"OT{hp}") for hp in range(C)]
        po = [self.ps(f"a_po{i}", [128, 512]) for i in range(2)]; bpo = [Buf(f"po{i}") for i in range(2)]
        self.run_phase(self._attn_core, l, OT, bOT, po, bpo, idb, bidb, mk, bmk)
        self._attn_wo(l, OT, bOT, po, bpo, wo, bwo)

    def _attn_core(self, l, OT, bOT, po, bpo, idb, bidb, mk, bmk):
        sc = self.sc
        qT = [self.sb(f"a_q{i}", [128, S], BF16) for i in range(1)] * 2; bq = [Buf("aq")] * 2
        kT = [self.sb(f"a_k{i}", [128, S], BF16) for i in range(1)] * 2; bk = [Buf("ak")] * 2
        VA = [self.sb(f"a_va{i}", [128, 32, 128], BF16) for i in range(1)] * 2; bva = [Buf("va")] * 2
        VB = [self.sb(f"a_vb{i}", [128, 32, 128], BF16) for i in range(1)] * 2; bvb = [Buf("vb")] * 2
        for i in range(1):
            sc.op("pool", lambda v, i=i: v.memset(VA[i][:], 0.0), writes=[bva[i]])
            sc.op("pool", lambda v, i=i: v.memset(VB[i][:], 0.0), writes=[bvb[i]])
        CH = 1024
        g = [self.sb(f"a_g{i}", [128, CH], F32) for i in range(2)]; bgt = [Buf(f"ag{i}") for i in range(2)]
        Pf = [self.sb(f"a_P{i}", [128, S + 1], F32) for i in range(2)]; bP = [Buf(f"aP{i}") for i in range(2)]
        zer = self.sb("a_zero", [128, CH], F32); bz = Buf("zero")
        sc.op("pool", lambda v: v.memset(zer[:], 0.0), writes=[bz])
        w = [self.sb(f"a_w{i}", [128, CH], BF16) for i in range(2)]; bwt = [Buf(f"aw{i}") for i in range(2)]
        wT = [self.sb(f"a_wT{i}", [128, 8, 128], BF16) for i in range(2)]; bwT = [Buf(f"awT{i}") for i in range(2)]
        pz = [self.ps(f"a_pz{i}", [128, CH]) for i in range(2)]; bpz = [Buf(f"pz{i}") for i in range(2)]
        pt = self.ps("a_pt", [128, CH]); bpt = Buf("pt")
        chunks = []
        for hp in range(C):
            for Q in range(S // 128):
                t1 = 128 * (Q + 1)
                for hd in range(2):
                    b_ = t1
                    while b_ > 0:
                        a_ = max(0, b_ - CH)
                        chunks.append(dict(hp=hp, Q=Q, hd=hd, a=a_, b=b_, L=b_ - a_, t1=t1, first=(b_ == t1), newhp=(Q == 0 and hd == 0 and b_ == t1),
                                           first_pv=(hd == 0 and b_ == t1), last=(hd == 1 and a_ == 0),
                                           ip=(hp * 64 + Q * 2 + hd) % 2, io=(hp * 32 + Q) % 2))
                        b_ = a_
        for n_, ch in enumerate(chunks):
            ch["iz"] = n_ % 2

        def stage_A(ch):
            hp, Q, hd, a, b, L, iz = ch["hp"], ch["Q"], ch["hd"], ch["a"], ch["b"], ch["L"], ch["iz"]
            i = 0
            if ch["newhp"]:
                rows = slice(hp * 128, (hp + 1) * 128)
                sc.dma("sp", lambda q, rows=rows: q.dma_start(out=qT[i][:], in_=self.QT[rows, :]), writes=[bq[i]], key="ld0")
                sc.dma("sp", lambda q, rows=rows: q.dma_start(out=kT[i][:], in_=self.KT[rows, :]), writes=[bk[i]], key="ldk0")
                srcA = self.V[:, hp * 128:hp * 128 + 64].rearrange("(b p) d -> p b d", p=128)
                srcB = self.V[:, hp * 128 + 64:hp * 128 + 128].rearrange("(b p) d -> p b d", p=128)
                for b0 in range(0, 32, 8):
                    sc.dma("sp", lambda q, srcA=srcA, b0=b0: q.dma_start(out=VA[i][:, b0:b0 + 8, 0:64], in_=srcA[:, b0:b0 + 8, :]), writes=[bva[i]], key="lda")
                    sc.dma("sp", lambda q, srcB=srcB, b0=b0: q.dma_start(out=VB[i][:, b0:b0 + 8, 64:128], in_=srcB[:, b0:b0 + 8, :]), writes=[bvb[i]], key="ldb")
            pr = slice(hd * 64, hd * 64 + 64)
            fns = []
            for p0 in range(0, L, 512):
                pl = min(512, L - p0)
                diag = ch["first"] and (p0 + pl == L)
                fns.append(lambda e, p0=p0, pl=pl, diag=diag: e.matmul(pz[iz][:, p0:p0 + pl], lhsT=qT[i][pr, Q * 128:(Q + 1) * 128],
                                                                       rhs=kT[i][pr, a + p0:a + p0 + pl], start=True, stop=not diag))
                if diag:
                    fns.append(lambda e: e.matmul(pz[iz][:, L - 128:L], lhsT=idb[:], rhs=mk[:], start=False, stop=True))
            sc.op("pe", fns, reads=[bq[i], bk[i], bidb, bmk], writes=[bpz[iz]])
            sc.op("act", lambda a_: a_.activation(out=g[iz][:, :L], in_=pz[iz][:, :L], func=AF.Sigmoid, scale=-0.125), reads=[bpz[iz]], writes=[bgt[iz]])

        def stage_B(ch):
            a, b, L, iz, ip, t1 = ch["a"], ch["b"], ch["L"], ch["iz"], ch["ip"], ch["t1"]
            if ch["newhp"]:
                for ipx in range(2):
                    sc.op("dve", lambda v, ipx=ipx: v.memset(Pf[ipx][:, 128:S + 1:128], 1.0), writes=[bP[ipx]])
            sc.op("dve", lambda v: v.tensor_tensor_scan(out=Pf[ip][:, b - 1:(a - 1 if a > 0 else None):-1], data0=g[iz][:, L - 1::-1], data1=zer[:, :L],
                                                        initial=Pf[ip][:, b:b + 1], op0=ALU.mult, op1=ALU.add), reads=[bgt[iz], bP[ip], bz], writes=[bP[ip]])
            eng = {"pool": "pool", "dve": "dve", "alt": ("pool" if iz == 0 else "dve")}[ATT_SUB]
            sc.op(eng, lambda v: v.tensor_tensor(out=w[iz][:, :L], in0=Pf[ip][:, a + 1:b + 1], in1=Pf[ip][:, a:b], op=ALU.subtract),
                  reads=[bP[ip]], writes=[bwt[iz]])

        def stage_C1(ch):
            L, iz = ch["L"], ch["iz"]
            nb = L // 128
            fns = [(lambda e, bb=bb: e.matmul(pt[:, bb * 128:(bb + 1) * 128], lhsT=w[iz][:, bb * 128:(bb + 1) * 128], rhs=idb[:], start=True, stop=True))
                   for bb in range(nb)]
            sc.op("pe", fns, reads=[bwt[iz], bidb], writes=[bpt])
            sc.op("act", lambda a_: a_.copy(out=wT[iz][:, :nb, :], in_=pt[:, :nb * 128]), reads=[bpt], writes=[bwT[iz]])

        def stage_C2(ch):
            hp, Q, hd, a, L, iz, io = ch["hp"], ch["Q"], ch["hd"], ch["a"], ch["L"], ch["iz"], ch["io"]
            i = 0
            Vh, bV = (VA[i], bva[i]) if hd == 0 else (VB[i], bvb[i])
            nb = L // 128
            fns = []
            for bb in range(nb):
                kb = a // 128 + bb
                fns.append(lambda e, bb=bb, kb=kb, fp=(ch["first_pv"] and bb == 0), last=(ch["last"] and bb == nb - 1):
                           e.matmul(po[io][:, :128], lhsT=Vh[:, kb, :], rhs=wT[iz][:, bb, :], start=fp, stop=last))
            sc.op("pe", fns, reads=[bwT[iz], bV], writes=[bpo[io]])
            if ch["last"]:
                sc.op("act", lambda a_: a_.copy(out=OT[:, hp, Q * 128:(Q + 1) * 128], in_=po[io][:, :128]), reads=[bpo[io]], writes=[bOT[hp]])

        nchk = len(chunks)
        for n_ in range(nchk + 3):
            if n_ < nchk:
                stage_A(chunks[n_])
            if 0 <= n_ - 1 < nchk:
                stage_B(chunks[n_ - 1])
            if 0 <= n_ - 2 < nchk:
                stage_C1(chunks[n_ - 2])
            if 0 <= n_ - 3 < nchk:
                stage_C2(chunks[n_ - 3])

    def _attn_wo(self, l, OT, bOT, po, bpo, wo, bwo):
        sc = self.sc
        n = NT
        x = [self.sb(f"o_x{i}", [128, C, n], F32) for i in range(2)]; bx = [Buf(f"ox{i}") for i in range(2)]
        stores = []
        def load(t):
            src = self.xT[:, t * n:(t + 1) * n].rearrange("(c p) n -> p c n", p=128)
            self._wait_xT("sp")
            sc.dma("sp", lambda q, src=src: q.dma_start(out=x[t % 2][:], in_=src), writes=[bx[t % 2]], key=f"ldx{t % 2}")

        load(0)
        for t in range(S // n):
            i = t % 2
            if t + 1 < S // n:
                load(t + 1)
            for o in range(C):
                k = o % 2
                fns = [(lambda e, o=o, c=c, k=k, t=t: e.matmul(po[k][:], lhsT=wo[:, c, o * 128:(o + 1) * 128], rhs=OT[:, c, t * n:(t + 1) * n],
                                                             start=(c == 0), stop=(c == C - 1))) for c in range(C)]
                sc.op("pe", fns, reads=bOT + [bwo], writes=[bpo[k]])
                sc.op("dve", lambda v, i=i, o=o, k=k: v.tensor_tensor(out=x[i][:, o, :], in0=po[k][:], in1=x[i][:, o, :], op=ALU.add),
                      reads=[bpo[k], bx[i]], writes=[bx[i]])
            dst = self.xT[:, t * n:(t + 1) * n].rearrange("(c p) n -> p c n", p=128)
            stores.append(sc.dma("sp", lambda q, i=i, dst=dst: q.dma_start(out=dst, in_=x[i][:]), reads=[bx[i]], writes=[self.b_xT], key=f"st{i}"))
        self.b_xT.r = []
        self._xT_tokens = stores

    def phase_hgrn(self, l):
        sc = self.sc
        j = l // 2
        NB = 128
        HH = 8
        win = self.sb("win_s", [128, C, 4 * D], BF16); bwin = Buf("win")
        for c in range(C):
            for hf in range(2):
                sc.dma("pool", lambda q, c=c, hf=hf: q.dma_start(out=win[:, c, hf * 2048:(hf + 1) * 2048],
                                                                  in_=self.w_in[j, c * 128:(c + 1) * 128, hf * 2048:(hf + 1) * 2048]), writes=[bwin], key="w1")
        wo = self.sb("hwo_s", [128, C, D], BF16); bwo = Buf("hwo")
        for c0 in range(0, C, 4):
            src = self.wo_hg[j, c0 * 128:(c0 + 4) * 128, :].rearrange("(c p) d -> p c d", p=128)
            sc.dma("pool", lambda q, c0=c0, src=src: q.dma_start(out=wo[:, c0:c0 + 4, :], in_=src), writes=[bwo], key="w2")
        M1 = self.sb("M1_s", [128, 256], F32); M2 = self.sb("M2_s", [128, 128], F32); Mc = self.sb("Mc_s", [128, 1024], F32)
        o128 = self.sb("o128_s", [128, 128], F32); idb = self.sb("hidb_s", [128, 128], BF16)
        lrow = self.sb("lrow_s", [128, 2, D], F32)
        lT = self.sb("lT_s", [128, 2 * C], F32); gg = self.sb("gg_s", [128, 2], F32)
        bK = Buf("hconst")
        sc.dma("sp", lambda q: q.dma_start(out=M1[:], in_=self.M1[:, :]), writes=[bK], key="c0")
        sc.dma("sp", lambda q: q.dma_start(out=M2[:], in_=self.M2[:, :]), writes=[bK], key="c0")
        sc.dma("sp", lambda q: q.dma_start(out=Mc[:], in_=self.Mc[:, :]), writes=[bK], key="c0")
        sc.dma("sp", lambda q: q.dma_start(out=o128[:], in_=self.ones128[:, :]), writes=[bK], key="c0")
        sc.dma("sp", lambda q: q.dma_start(out=lT[:], in_=self.lblT[:, :]), writes=[bK], key="c0")
        sc.dma("sp", lambda q: q.dma_start(out=gg[:], in_=self.hgg[:, :]), writes=[bK], key="c0")
        for jj in range(2):
            sc.dma("sp", lambda q, jj=jj: q.dma_start(out=lrow[:, jj, :], in_=self.lbl[jj:jj + 1, :].partition_broadcast(128)), writes=[bK], key="c0")
        omr = self.sb("omr_s", [128, D], F32); omT = self.sb("omT_s", [128, C], F32); bom = Buf("om")
        one1 = self.sb("one1_s", [128, 1], F32)
        sc.op("dve", lambda v: v.memset(one1[:], 1.0), writes=[bom])
        if j == 0:
            sc.op("dve", lambda v: v.memset(omr[:], 1.0), writes=[bom])
            sc.op("dve", lambda v: v.memset(omT[:], 1.0), writes=[bom])
        else:
            dr = self.sb("dr_s", [128, D], F32); dT = self.sb("dT_s", [128, C], F32); bd = Buf("dlt")
            sc.op("dve", lambda v: v.tensor_tensor(out=dr[:], in0=lrow[:, 0, :], in1=lrow[:, 1, :], op=ALU.subtract), reads=[bK], writes=[bd])
            sc.op("dve", lambda v: v.tensor_tensor(out=dT[:], in0=lT[:, 0:C], in1=lT[:, C:2 * C], op=ALU.subtract), reads=[bK], writes=[bd])
            sc.op("act", lambda a: a.activation(out=omr[:], in_=dr[:], func=AF.Sigmoid), reads=[bd], writes=[bom])
            sc.op("act", lambda a: a.activation(out=omT[:], in_=dT[:], func=AF.Sigmoid), reads=[bd], writes=[bom])
        sc.dma("pool", lambda q: q.dma_start(out=idb[:], in_=self.identb[:, :]), writes=[bK], key="c1")
        St = self.sb("S_s", [128, HH, 128], F32); bS = [Buf(f"S{h}") for h in range(HH)]
        Sb = self.sb("Sb_s", [128, HH, 128], BF16); bSb = [Buf(f"Sb{h}") for h in range(HH)]
        sc.op("pool", lambda v: v.memset(St[:], 0.0), writes=bS)
        x = [self.sb(f"g_x{i}", [128, C, NB], F32) for i in range(2)]; bx = [Buf(f"gx{i}") for i in range(2)]
        h2 = [self.sb(f"g_h{i}", [128, C, NB], BF16) for i in range(2)]; bh2 = [Buf(f"gh{i}") for i in range(2)]
        scr = dict(sq=self.sb("g_sq", [128, C, NB], F32), bsq=Buf("sq"), pss=self.ps("g_pss", [128, 512]), bpss=Buf("pss"),
                   rt=self.sb("g_rt", [128, NB], F32), brt=Buf("rt"), rs=self.sb("g_rs", [128, NB], F32), brs=Buf("rs"))
        pA = self.ps("g_pA", [128, 1024]); bpA = [Buf("pA0"), Buf("pA1")]
        pB = self.ps("g_pB", [128, 1024]); bpB = [Buf("pB0"), Buf("pB1")]
        pX = self.ps("g_pX", [128, 1024]); bpX = [Buf("pX0"), Buf("pX1")]
        pE = self.ps("g_pE", [128, 512]); bpE = Buf("pE")
        W = HH * NB

        def t2(name, dtype=F32):
            return self.sb(name, [128, W], dtype), Buf(name)
        sbar, bsb = t2("g_sbar"); ktok, bkt = t2("g_ktok"); lf, blf = t2("g_lf")
        vtok, bvt = t2("g_vtok", BF16); khat, bkh = t2("g_khat", BF16)
        ed, bed = sbar, bsb
        qraw, bqr = t2("g_qraw"); graw, bgr = t2("g_graw"); fraw, bfr = t2("g_fraw")
        sgq, bsgq = t2("g_sgq"); sgg, bsgg = t2("g_sgg"); kf, bkf = t2("g_kf"); qf, bqf = t2("g_qf")
        E1, bE1 = t2("g_E1"); E2, bE2 = t2("g_E2")
        EX = self.sb("g_EX", [128, 4 * HH], F32); bEX = Buf("EX")
        qt, bqt = t2("g_qt", BF16); kt, bkt2 = t2("g_kt", BF16); scT, bsc = t2("g_scT", BF16)
        oT, boT = t2("g_oT"); osq, bosq = t2("g_osq"); ort, bort = t2("g_ort"); on, bon = t2("g_on", BF16)
        omF, bomF = t2("g_omF")
        sc.op("dve", lambda v: v.memset(omF[:], 1.0), writes=[bomF])
        if j != 0:
            for hh in range(HH):
                sc.op("dve", lambda v, hh=hh: v.tensor_scalar_mul(out=omF[:, hh * 128:(hh + 1) * 128], in0=omF[:, hh * 128:(hh + 1) * 128], scalar1=omT[:, hh:hh + 1]),
                      reads=[bom, bomF], writes=[bomF])
        hsl = lambda hh: slice(hh * 128, (hh + 1) * 128)
        bchain = Buf("act_chain")
        gcol = (l * 2) * C
        stores = []
        nblk = self.hg_blocks or (S // NB)

        def load(t):
            src = self.xT[:, t * NB:(t + 1) * NB].rearrange("(c p) n -> p c n", p=128)
            self._wait_xT("sp")
            sc.dma("sp", lambda q, src=src: q.dma_start(out=x[t % 2][:], in_=src), writes=[bx[t % 2]], key=f"ldx{t % 2}")

        def stage1a(t):
            h, bh = h2[t % 2], bh2[t % 2]
            for (pp, bpp, off) in ((pA, bpA, D), (pB, bpB, 2 * D)):
                fns = []
                for hf in range(2):
                    for c in range(C):
                        fns.append(lambda e, pp=pp, off=off, hf=hf, c=c, h=h: e.matmul(pp[:, hf * 512:(hf + 1) * 512], lhsT=h[:, c, :],
                                                                                  rhs=win[:, c, off + hf * 512:off + (hf + 1) * 512],
                                                                                  start=(c == 0), stop=(c == C - 1)))
                sc.op("pe", fns, reads=[bh, bwin], writes=bpp)
            sc.op("act", lambda a: a.activation(out=sbar[:], in_=pA[:], func=AF.Sigmoid, scale=-1.0), reads=bpA, writes=[bsb])
            sc.op("act", lambda a: a.copy(out=vtok[:], in_=pB[:]), reads=bpB, writes=[bvt])
            sc.op("dve", lambda v: v.tensor_tensor(out=ktok[:], in0=sbar[:], in1=omr[:], op=ALU.mult), reads=[bsb, bom], writes=[bkt])
            sc.op("act", lambda a: a.activation(out=lf[:], in_=ktok[:], func=AF.Ln, bias=one1[:, 0:1], scale=-1.0), reads=[bkt, bom], writes=[blf])

        def stage1b(t):
            fns = [(lambda e, hf=hf: e.matmul(pA[:, hf * 512:(hf + 1) * 512], lhsT=M2[:], rhs=lf[:, hf * 512:(hf + 1) * 512], start=True, stop=True))
                   for hf in range(2)]
            sc.op("pe", fns, reads=[blf, bK], writes=bpA)
            sc.op("act", lambda a: a.activation(out=ed[:], in_=pA[:], func=AF.Exp), reads=bpA, writes=[bed])
            sc.op("dve", lambda v: v.tensor_tensor(out=khat[:], in0=ed[:], in1=ktok[:], op=ALU.mult), reads=[bed, bkt], writes=[bkh])

        def stage2(t):
            h, bh = h2[t % 2], bh2[t % 2]
            if self.hg_parts >= 2:
                def proj(dst, base):
                    return [(lambda e, hh=hh, c=c, h=h: e.matmul(dst[:, hsl(hh)], lhsT=win[:, c, base + hh * 128:base + (hh + 1) * 128], rhs=h[:, c, :],
                                                            start=(c == 0), stop=(c == C - 1))) for hh in range(HH) for c in range(C)]
                for dst, bdst, base, raw_, braw_ in ((pX, bpX, 0, qraw, bqr), (pA, bpA, 3 * D, graw, bgr), (pB, bpB, D, fraw, bfr)):
                    sc.op("pe", proj(dst, base), reads=[bh, bwin], writes=bdst)
                    sc.op("dve", lambda v, dst=dst, raw_=raw_: v.tensor_copy(out=raw_[:], in_=dst[:]), reads=bdst, writes=[braw_])
                sc.op("act", lambda a: a.activation(out=sgq[:], in_=qraw[:], func=AF.Sigmoid), reads=[bqr], writes=[bsgq])
                sc.op("act", lambda a: a.activation(out=sgg[:], in_=graw[:], func=AF.Sigmoid), reads=[bgr], writes=[bsgg])
                sc.op("act", lambda a: a.activation(out=kf[:], in_=fraw[:], func=AF.Sigmoid, scale=-1.0), reads=[bfr], writes=[bkf])
                sc.op("dve", lambda v: v.tensor_tensor(out=qf[:], in0=qraw[:], in1=sgq[:], op=ALU.mult), reads=[bqr, bsgq], writes=[bqf])
                sc.op("pe", [(lambda e, hh=hh: e.matmul(pX[:, hsl(hh)], lhsT=lf[:, hsl(hh)], rhs=M1[:, 0:128], start=True, stop=True)) for hh in range(HH)],
                      reads=[blf, bK], writes=bpX)
                sc.op("pe", [(lambda e, hh=hh: e.matmul(pE[:, hh * 4:(hh + 1) * 4], lhsT=lf[:, hsl(hh)], rhs=M1[:, 128:132], start=True, stop=True)) for hh in range(HH)],
                      reads=[blf, bK], writes=[bpE])
                sc.op("act", lambda a: a.activation(out=E1[:], in_=pX[:], func=AF.Exp), reads=bpX, writes=[bE1])
                sc.op("act", lambda a: a.activation(out=E2[:], in_=pX[:], func=AF.Exp, scale=-1.0), reads=bpX, writes=[bE2])
                sc.op("act", lambda a: a.activation(out=EX[:], in_=pE[:, 0:4 * HH], func=AF.Exp), reads=[bpE], writes=[bEX])
                sc.op("dve", lambda v: v.tensor_tensor(out=qt[:], in0=qf[:], in1=E1[:], op=ALU.mult), reads=[bqf, bE1], writes=[bqt])
                sc.op("dve", lambda v: v.tensor_tensor(out=kf[:], in0=kf[:], in1=E2[:], op=ALU.mult), reads=[bkf, bE2], writes=[bkf])
                sc.op("dve", lambda v: v.tensor_tensor(out=kt[:], in0=kf[:], in1=omF[:], op=ALU.mult), reads=[bkf, bomF], writes=[bkt2])

        def rec(t):
            if self.hg_parts >= 3:
                sc.op("pe", [(lambda e, hh=hh: e.matmul(pA[:, hsl(hh)], lhsT=kt[:, hsl(hh)], rhs=qt[:, hsl(hh)], start=True, stop=True)) for hh in range(HH)],
                      reads=[bkt2, bqt], writes=bpA)
                sc.op("dve", lambda v: v.tensor_tensor(out=scT[:], in0=pA[:], in1=Mc[:], op=ALU.mult), reads=bpA + [bK], writes=[bsc])
                for cc in range(2):
                    cs = slice(cc * 64, cc * 64 + 64)
                    for hh in range(HH):
                        sc.op("act", lambda a, hh=hh, cc=cc: a.activation(out=Sb[:, hh, :], in_=St[:, hh, :], func=AF.Copy, scale=EX[:, hh * 4 + cc:hh * 4 + cc + 1]),
                              reads=[bS[hh], bEX], writes=[bSb[hh], bchain])
                    fns = []
                    for hh in range(HH):
                        col = slice(hh * 128 + cc * 64, hh * 128 + cc * 64 + 64)
                        fns.append(lambda e, hh=hh, col=col, cs=cs: e.matmul(pB[:, col], lhsT=vtok[cs, hsl(hh)], rhs=scT[cs, col], start=True, stop=False))
                        fns.append(lambda e, hh=hh, col=col: e.matmul(pB[:, col], lhsT=Sb[:, hh, :], rhs=qt[:, col], start=False, stop=True))
                    sc.op("pe", fns, reads=[bvt, bsc, bqt] + bSb, writes=bpB)
                    sc.op("pe", [(lambda e, hh=hh, cs=cs: e.matmul(pX[:, hsl(hh)], lhsT=khat[cs, hsl(hh)], rhs=vtok[cs, hsl(hh)], start=True, stop=True)) for hh in range(HH)],
                          reads=[bkh, bvt], writes=bpX)
                    for hh in range(HH):
                        sc.op("dve", lambda v, hh=hh, cc=cc: v.scalar_tensor_tensor(out=St[:, hh, :], in0=St[:, hh, :], scalar=EX[:, hh * 4 + 2 + cc:hh * 4 + 3 + cc],
                                                                                      in1=pX[:, hsl(hh)], op0=ALU.mult, op1=ALU.add),
                              reads=[bS[hh], bEX] + bpX, writes=[bS[hh]])
                sc.op("dve", lambda v: v.tensor_copy(out=oT[:], in_=pB[:]), reads=bpB, writes=[boT])

        def post_a(t):
            i = t % 2
            if self.hg_parts >= 3:
                sc.op("act", lambda a: a.activation(out=osq[:], in_=oT[:], func=AF.Square), reads=[boT], writes=[bosq])
                sc.op("pe", [(lambda e, hf=hf: e.matmul(pX[:, hf * 512:(hf + 1) * 512], lhsT=o128[:], rhs=osq[:, hf * 512:(hf + 1) * 512], start=True, stop=True))
                             for hf in range(2)], reads=[bosq, bK], writes=bpX)
                sc.op("act", lambda a: a.activation(out=ort[:], in_=pX[:], func=AF.Sqrt, bias=self.eps_s[:, 0:1], scale=1.0), reads=bpX + [self.b_eps], writes=[bort])
                sc.op("dve", lambda v: v.reciprocal(out=osq[:], in_=ort[:]), reads=[bort], writes=[bosq])
                sc.op("dve", lambda v: v.scalar_tensor_tensor(out=ort[:], in0=oT[:], scalar=gg[:, j:j + 1], in1=osq[:], op0=ALU.mult, op1=ALU.mult),
                      reads=[boT, bosq, bK], writes=[bort])
                sc.op("dve", lambda v: v.tensor_tensor(out=on[:], in0=ort[:], in1=sgg[:], op=ALU.mult), reads=[bort, bsgg], writes=[bon])

        def post_b(t):
            i = t % 2
            if self.hg_parts >= 3:
                sc.op("pe", [(lambda e, o=o, c=c: e.matmul(pX[:, hsl(o)], lhsT=wo[:, c, o * 128:(o + 1) * 128], rhs=on[:, hsl(c)], start=(c == 0), stop=(c == C - 1)))
                             for o in range(C) for c in range(C)], reads=[bon, bwo], writes=bpX)
                for o in range(C):
                    sc.op("dve", lambda v, i=i, o=o: v.tensor_tensor(out=x[i][:, o, :], in0=pX[:, hsl(o)], in1=x[i][:, o, :], op=ALU.add),
                          reads=bpX + [bx[i]], writes=[bx[i]])
            dst = self.xT[:, t * NB:(t + 1) * NB].rearrange("(c p) n -> p c n", p=128)
            stores.append(sc.dma("sp", lambda q, i=i, dst=dst: q.dma_start(out=dst, in_=x[i][:]), reads=[bx[i]], writes=[self.b_xT], key=f"st{i}"))

        load(0)
        self.norm(x[0], bx[0], h2[0], bh2[0], NB, gcol, scr)
        stage1a(0)
        stage1b(0)
        for t in range(nblk):
            if t + 1 < nblk:
                load(t + 1)
            stage2(t)
            if t + 1 < nblk:
                self.norm(x[(t + 1) % 2], bx[(t + 1) % 2], h2[(t + 1) % 2], bh2[(t + 1) % 2], NB, gcol, scr)
            rec(t)
            post_a(t)
            if t + 1 < nblk:
                stage1a(t + 1)
            post_b(t)
            if t + 1 < nblk:
                stage1b(t + 1)
        self.b_xT.r = []
        self._xT_tokens = stores

    def _wait_xT(self, eng):
        self.sc._need(eng, self._xT_tokens)

    def norm(self, x, bx, h, bh, n, gcol, scr):
        sc = self.sc
        sq, bsq, pss, bpss, rt, brt, rs, brs = (scr[k] for k in ("sq", "bsq", "pss", "bpss", "rt", "brt", "rs", "brs"))
        sc.op("act", lambda a: a.activation(out=sq[:, :, :n], in_=x[:, :, :n], func=AF.Square), reads=[bx], writes=[bsq])
        fns = [(lambda e, c=c: e.matmul(pss[:, :n], lhsT=self.ones_s[:], rhs=sq[:, c, :n], start=(c == 0), stop=(c == C - 1)))
               for c in range(C)]
        sc.op("pe", fns, reads=[bsq, self.b_ones], writes=[bpss])
        sc.op("act", lambda a: a.activation(out=rt[:, :n], in_=pss[:, :n], func=AF.Sqrt, bias=self.eps_s[:, 0:1], scale=1.0),
              reads=[bpss, self.b_eps], writes=[brt])
        sc.op("dve", lambda v: v.reciprocal(out=rs[:, :n], in_=rt[:, :n]), reads=[brt], writes=[brs])
        for c in range(C):
            eng = "dve"
            sc.op(eng, lambda v, c=c: v.scalar_tensor_tensor(out=h[:, c, :n], in0=x[:, c, :n],
                                                             scalar=self.ng_s[:, gcol + c:gcol + c + 1], in1=rs[:, :n],
                                                             op0=ALU.mult, op1=ALU.mult),
                  reads=[bx, brs, self.b_ng], writes=[bh])

    def phase_mlp(self, l):
        sc = self.sc
        n = 256
        FC = DFF // 128
        w1s = self.sb(f"w1s_{l}", [128, C, DFF], BF16)
        w2s = self.sb(f"w2s_{l}", [128, FC, D], BF16)
        bw1 = [Buf(f"w1_{l}")] * 8
        bw2 = [Buf(f"w2_{l}_{i}") for i in range(8)]
        for c in range(C):
            sc.dma("pool", lambda q, c=c: q.dma_start(out=w1s[:, c, :], in_=self.w1[l, c * 128:(c + 1) * 128, :]), writes=[bw1[0]], key="w1")
        for fb in range(8):
            f0 = fb * 4
            src = self.w2[l, f0 * 128:(f0 + 4) * 128, :].rearrange("(f p) d -> p f d", p=128)
            sc.dma("pool", lambda q, f0=f0, src=src: q.dma_start(out=w2s[:, f0:f0 + 4, :], in_=src), writes=[bw2[fb]], key=f"w2_{fb}")
        NX = 3
        x = [self.sb(f"m{l}_x{i}", [128, C, n], F32) for i in range(NX)]
        h = [self.sb(f"m{l}_h{i}", [128, C, n], BF16) for i in range(2)]
        aT = [self.sb(f"m{l}_a{i}", [128, FC, n], BF16) for i in range(2)]
        r32 = [self.sb(f"m{l}_r{i}", [128, n], F32) for i in range(2)]
        scr = dict(sq=self.sb(f"m{l}_sq", [128, C, n], F32), bsq=Buf("sq"), pss=self.ps(f"m{l}_pss", [128, 512]), bpss=Buf("pss"),
                   rt=self.sb(f"m{l}_rt", [128, n], F32), brt=Buf("rt"), rs=self.sb(f"m{l}_rs", [128, n], F32), brs=Buf("rs"))
        pa = [self.ps(f"m{l}_pa{i}", [128, 512]) for i in range(3)]
        py = [self.ps(f"m{l}_py{i}", [128, 512]) for i in range(2)]
        bx = [Buf(f"mx{i}") for i in range(NX)]
        bh = [Buf(f"mh{i}") for i in range(2)]
        ba = [[Buf(f"ma{i}_{f}") for f in range(FC)] for i in range(2)]
        br = [Buf(f"mr{i}") for i in range(2)]
        bpa = [Buf(f"mpa{i}") for i in range(3)]
        bpy = [Buf(f"mpy{i}") for i in range(2)]
        stores = []
        gcol = (l * 2 + 1) * C
        T = S // n

        def load(t):
            src = self.xT[:, t * n:(t + 1) * n].rearrange("(c p) n -> p c n", p=128)
            self._wait_xT("sp")
            sc.dma("sp", lambda q, src=src: q.dma_start(out=x[t % NX][:], in_=src), writes=[bx[t % NX]], key=f"mx{t % NX}")

        def stage1(t):
            hi, ai = t % 2, t % 2
            for f in range(FC):
                k = f % 3
                fns = [(lambda e, f=f, c=c, k=k: e.matmul(pa[k][:, :n], lhsT=w1s[:, c, f * 128:(f + 1) * 128], rhs=h[hi][:, c, :],
                                                         start=(c == 0), stop=(c == C - 1))) for c in range(C)]
                sc.op("pe", fns, reads=[bh[hi], bw1[f // 4]], writes=[bpa[k]])
                j = f % 2
                sc.op("act", lambda a, j=j, k=k: a.activation(out=r32[j][:], in_=pa[k][:, :n], func=AF.Relu), reads=[bpa[k]], writes=[br[j]])
                eng = "dve" if f % 2 == 0 else "pool"
                sc.op(eng, lambda v, j=j, f=f: v.tensor_tensor(out=aT[ai][:, f, :], in0=r32[j][:], in1=r32[j][:], op=ALU.mult),
                      reads=[br[j]], writes=[ba[ai][f]])

        def stage2(t):
            xi, ai = t % NX, t % 2
            for o in range(C):
                k = o % 2
                for fb in range(8):
                    fns = [(lambda e, o=o, f=f, k=k: e.matmul(py[k][:, :n], lhsT=w2s[:, f, o * 128:(o + 1) * 128], rhs=aT[ai][:, f, :],
                                                             start=(f == 0), stop=(f == FC - 1))) for f in range(fb * 4, fb * 4 + 4)]
                    sc.op("pe", fns, reads=ba[ai][fb * 4:fb * 4 + 4] + [bw2[fb]], writes=[bpy[k]])
                sc.op("dve", lambda v, o=o, k=k: v.tensor_tensor(out=x[xi][:, o, :], in0=py[k][:, :n], in1=x[xi][:, o, :], op=ALU.add),
                      reads=[bpy[k], bx[xi]], writes=[bx[xi]])
            dst = self.xT[:, t * n:(t + 1) * n].rearrange("(c p) n -> p c n", p=128)
            stores.append(sc.dma("sp", lambda q, dst=dst: q.dma_start(out=dst, in_=x[xi][:]), reads=[bx[xi]], writes=[self.b_xT], key=f"st{t % 2}"))

        load(0)
        self.norm(x[0], bx[0], h[0], bh[0], n, gcol, scr)
        for t in range(T):
            if t + 1 < T:
                load(t + 1)
            stage1(t)
            if t + 1 < T:
                self.norm(x[(t + 1) % NX], bx[(t + 1) % NX], h[(t + 1) % 2], bh[(t + 1) % 2], n, gcol, scr)
            if t >= 1:
                stage2(t - 1)
        stage2(T - 1)
        self.b_xT.r = []
        self._xT_tokens = stores


_CACHE = {}
HG_PAD = 0
SKIP_ATTN = False
ATT_SUB = "dve"
SELFWAIT_ENGINES = ()
HG_DEBUG = ""


def _consts_np():
    blk = np.zeros((128, 128), np.float32)
    blk[:64, :64] = 1.0 / 64
    blk[64:, 64:] = 1.0 / 64
    q = np.arange(128)[:, None]; s_ = np.arange(128)[None, :]
    mask = np.where(s_ >= q, -240000.0, 0.0).astype(np.float32)
    p = np.arange(128); ch = p // 64; u = p % 64
    same = ch[:, None] == ch[None, :]
    tri = same & (p[:, None] <= p[None, :])
    mid = same & (u[:, None] <= 31)
    M1 = np.zeros((128, 256), np.float32)
    M1[:, :128] = tri.astype(np.float32) - mid.astype(np.float32)
    for c in range(2):
        M1[:, 128 + c] = ((ch == c) & (u <= 31)).astype(np.float32)
        M1[:, 130 + c] = (ch == c).astype(np.float32)
    M2 = (same & (p[:, None] > p[None, :])).astype(np.float32)
    Mc = (same & (p[:, None] <= p[None, :])).astype(np.float32)
    return {"ident": np.eye(128, dtype=np.float32), "onesD": np.full((128, 128), 1.0 / D, np.float32),
            "blk64": blk, "identb": np.eye(128, dtype=np.float32), "maskneg": mask,
            "hg_M1": M1, "hg_M2": M2, "hg_Mc": np.ascontiguousarray(np.tile(Mc, (1, 8))), "ones128": np.full((128, 128), 1.0 / 128, np.float32)}


def kernel(x, norm_gains, sb_w_qkv, sb_q_gain, sb_k_gain, sb_w_o, hg_w_in, hg_lb_logits, hg_norm_gain, hg_w_o,
           mlp_w1, mlp_w2, _layers=(0, 1, 2, 3), _mixers=None, _mlps=None, _hg_blocks=None, _hg_parts=3, _allow_hgrn=False, _cores=None, _return_maps=False):
    key = (tuple(_layers), None if _mixers is None else tuple(_mixers), None if _mlps is None else tuple(_mlps), _hg_blocks, _hg_parts, _allow_hgrn, HG_DEBUG, HG_PAD, SELFWAIT_ENGINES, ATT_SUB, SKIP_ATTN)
    if key not in _CACHE:
        pr = Prog(list(_layers), _mixers, _mlps, hg_blocks=_hg_blocks, hg_parts=_hg_parts, allow_hgrn=_allow_hgrn)
        _CACHE[key] = (pr.build(), pr.used_inputs)
    nc, used = _CACHE[key]
    cst = _consts_np()
    ng = np.ascontiguousarray(np.asarray(norm_gains, np.float32).reshape(DEPTH * 2, C, 128).transpose(2, 0, 1).reshape(128, DEPTH * 2 * C))
    qg, kg = np.asarray(sb_q_gain, np.float32), np.asarray(sb_k_gain, np.float32)
    qk = np.stack([np.tile(qg[0], 2), np.tile(kg[0], 2), np.tile(qg[1], 2), np.tile(kg[1], 2)], axis=1)
    lbl = np.ascontiguousarray(hg_lb_logits, np.float32)
    lblT = np.ascontiguousarray(lbl.reshape(2, C, 128).transpose(2, 0, 1).reshape(128, 2 * C))
    shared = {"hg_w_in": np.ascontiguousarray(hg_w_in, np.float32), "hg_w_o": np.ascontiguousarray(hg_w_o, np.float32),
              "hg_lb_logits": lbl, "hg_lb_logits_T": lblT, "hg_gain_T": np.ascontiguousarray(np.asarray(hg_norm_gain, np.float32).T),
              "qk_gain": np.ascontiguousarray(qk), "sb_w_qkv": np.ascontiguousarray(sb_w_qkv, np.float32),
              "sb_w_o": np.ascontiguousarray(sb_w_o, np.float32), "norm_gains": ng, "mlp_w1": np.ascontiguousarray(mlp_w1, np.float32), "mlp_w2": np.ascontiguousarray(mlp_w2, np.float32), **cst}
    shared = {k: v for k, v in shared.items() if k in used}
    nb = _cores or 8
    in_maps = [dict(shared, x=np.ascontiguousarray(x[b], np.float32)) for b in range(nb)]
    if _return_maps:
        return nc, in_maps
    res = run_bass_kernel_spmd(nc, in_maps, core_ids=list(range(nb)))
    return np.stack([r["out"] for r in res.results], axis=0)
```

```python
import contextlib
import numpy as np
import concourse.bass as bass
import concourse.mybir as mybir
from concourse.bass_utils import run_bass_kernel_spmd

F32 = mybir.dt.float32
BF16 = mybir.dt.bfloat16
AF = mybir.ActivationFunctionType
ALU = mybir.AluOpType

D = 1024
S = 4096
DEPTH = 4
NH = 16
DH = 64
DFF = 4096
C = D // 128
NT = 512
EPS = 1e-6


class Buf:
    def __init__(self, name):
        self.name = name
        self.w = None
        self.r = []


class Sched:
    ENG = ("pe", "act", "dve", "pool", "sp")

    def __init__(self, nc, stack):
        self.nc = nc
        self.stack = stack
        self.sem = {e: stack.enter_context(nc.semaphore("s_" + e)) for e in self.ENG if e != "sp"}
        self.cnt = {e: 0 for e in self.sem}
        self.prog = {e: [] for e in self.ENG}
        self.seen = {e: {} for e in self.ENG}
        self.dsem = {}
        self.dcnt = {}

    def _need(self, eng, toks):
        best = {}
        for t in toks:
            if t is None:
                continue
            s, v = t
            if v > best.get(id(s), (s, 0))[1]:
                best[id(s)] = (s, v)
        for s, v in best.values():
            if self.seen[eng].get(id(s), 0) < v:
                self.seen[eng][id(s)] = v
                self.prog[eng].append(("wait", s, v))

    def _deps(self, reads, writes):
        toks = [b.w for b in reads]
        for b in writes:
            toks.append(b.w)
            toks.extend(b.r)
        return toks

    def op(self, eng, fns, reads=(), writes=()):
        if callable(fns):
            fns = [fns]
        n0 = len(self.prog[eng])
        deps = self._deps(reads, writes)
        if eng == "pe":
            deps = [t for t in deps if t is not None and t[0] is not self.sem["pe"]]
        self._need(eng, deps)
        if eng in SELFWAIT_ENGINES and len(self.prog[eng]) == n0 and self.cnt[eng] > 0:
            self.seen[eng][id(self.sem[eng])] = self.cnt[eng]
            self.prog[eng].append(("wait", self.sem[eng], self.cnt[eng]))
        self.cnt[eng] += 1
        tok = (self.sem[eng], self.cnt[eng])
        self.prog[eng].append(("op", fns, self.sem[eng], 1))
        for b in reads:
            b.r.append(tok)
        for b in writes:
            b.w = tok
            b.r = []
        return tok

    def dma(self, q, fn, reads=(), writes=(), key=None):
        key = q + "_" + (key or writes[0].name)
        if key not in self.dsem:
            self.dsem[key] = self.stack.enter_context(self.nc.semaphore("d_" + key))
            self.dcnt[key] = 0
        self._need(q, self._deps(reads, writes))
        self.dcnt[key] += 16
        tok = (self.dsem[key], self.dcnt[key])
        self.prog[q].append(("op", [fn], self.dsem[key], 16))
        for b in reads:
            b.r.append(tok)
        for b in writes:
            b.w = tok
            b.r = []
        return tok

    def barrier(self):
        toks = [(self.sem[e], self.cnt[e]) for e in self.sem if self.cnt[e]]
        toks += [(self.dsem[k], self.dcnt[k]) for k in self.dsem if self.dcnt[k]]
        for e in self.ENG:
            self._need(e, toks)

    def emit(self):
        nc = self.nc
        engobj = {"pe": "tensor", "act": "scalar", "dve": "vector", "pool": "gpsimd", "sp": "sync"}
        with nc.Block() as block:
            for e in self.ENG:
                prog = self.prog[e]

                def body(eng, prog=prog):
                    for item in prog:
                        if item[0] == "wait":
                            eng.wait_ge(item[1], item[2])
                        else:
                            _, fns, sem, inc = item
                            ins = None
                            n_before = nc.n_instructions() if inc == 16 else 0
                            for f in fns:
                                ins = f(eng)
                            if inc == 16 and nc.n_instructions() - n_before != 1:
                                raise RuntimeError(f"dma_start was split into {nc.n_instructions() - n_before} instructions: the tracker's +16 per DMA "
                                                   f"would be wrong (each piece adds 16). Reshape this transfer.")
                            ins.then_inc(sem, inc)
                getattr(block, engobj[e])(body)
        self.prog = {e: [] for e in self.ENG}


class Prog:
    def __init__(self, layers, mixers=None, mlps=None, hg_blocks=None, hg_parts=3, allow_hgrn=False):
        self.allow_hgrn = allow_hgrn
        self.hg_blocks = hg_blocks
        self.hg_parts = hg_parts
        self.layers = layers
        self.mixers = layers if mixers is None else mixers
        self.mlps = layers if mlps is None else mlps

    def build(self):
        nc = bass.Bass("TRN2", target_bir_lowering=False)
        self.nc = nc
        nc.allow_low_precision("bf16 matmul operands with fp32 PSUM accumulation (problem tolerance is set for this)")
        dt = nc.dram_tensor
        self.x_in = dt("x", [S, D], F32, kind="ExternalInput").ap()
        self.out = dt("out", [S, D], F32, kind="ExternalOutput").ap()
        self.ng = dt("norm_gains", [128, DEPTH * 2 * C], F32, kind="ExternalInput").ap()
        self.ident = dt("ident", [128, 128], F32, kind="ExternalInput").ap()
        self.onesD = dt("onesD", [128, 128], F32, kind="ExternalInput").ap()
        sbm = [l for l in self.layers if l in self.mixers and l % 2 == 0]
        hgm = [l for l in self.layers if l in self.mixers and l % 2 == 1]
        mlm = [l for l in self.layers if l in self.mlps]
        self.used_inputs = {"x", "norm_gains", "ident", "onesD"}
        self.xT = dt("xT_scratch", [D, S], F32, kind="Internal").ap()
        if mlm:
            self.w1 = dt("mlp_w1", [DEPTH, D, DFF], F32, kind="ExternalInput").ap()
            self.w2 = dt("mlp_w2", [DEPTH, DFF, D], F32, kind="ExternalInput").ap()
            self.used_inputs |= {"mlp_w1", "mlp_w2"}
        if sbm or hgm:
            self.identb = dt("identb", [128, 128], F32, kind="ExternalInput").ap()
            self.used_inputs |= {"identb"}
        if sbm:
            self.wqkv = dt("sb_w_qkv", [2, D, 3 * D], F32, kind="ExternalInput").ap()
            self.wo_sb = dt("sb_w_o", [2, D, D], F32, kind="ExternalInput").ap()
            self.qkg = dt("qk_gain", [128, 4], F32, kind="ExternalInput").ap()
            self.blk64 = dt("blk64", [128, 128], F32, kind="ExternalInput").ap()
            self.maskneg = dt("maskneg", [128, 128], F32, kind="ExternalInput").ap()
            self.QT = dt("QT_scratch", [D, S], BF16, kind="Internal").ap()
            self.KT = dt("KT_scratch", [D, S], BF16, kind="Internal").ap()
            self.V = dt("V_scratch", [S, D], BF16, kind="Internal").ap()
            self.used_inputs |= {"sb_w_qkv", "sb_w_o", "qk_gain", "blk64", "maskneg"}
        if hgm:
            self.w_in = dt("hg_w_in", [2, D, 4 * D], F32, kind="ExternalInput").ap()
            self.wo_hg = dt("hg_w_o", [2, D, D], F32, kind="ExternalInput").ap()
            self.lbl = dt("hg_lb_logits", [2, D], F32, kind="ExternalInput").ap()
            self.lblT = dt("hg_lb_logits_T", [128, 2 * C], F32, kind="ExternalInput").ap()
            self.hgg = dt("hg_gain_T", [128, 2], F32, kind="ExternalInput").ap()
            self.M1 = dt("hg_M1", [128, 256], F32, kind="ExternalInput").ap()
            self.M2 = dt("hg_M2", [128, 128], F32, kind="ExternalInput").ap()
            self.Mc = dt("hg_Mc", [128, 1024], F32, kind="ExternalInput").ap()
            self.ones128 = dt("ones128", [128, 128], F32, kind="ExternalInput").ap()
            self.used_inputs |= {"hg_w_in", "hg_w_o", "hg_lb_logits", "hg_lb_logits_T", "hg_gain_T", "hg_M1", "hg_M2", "hg_Mc", "ones128"}
        with contextlib.ExitStack() as st:
            self.st = st
            self.sc = Sched(nc, st)
            self.ph = st
            self._consts()
            self.run_phase(self.phase_in)
            for l in self.layers:
                if l in self.mixers:
                    if l % 2 == 0:
                        self.run_phase(self.phase_qkv, l)
                        if not SKIP_ATTN:
                            self.run_phase(self.phase_attn, l)
                    else:
                        self.run_phase(self.phase_hgrn, l)
                if l in self.mlps:
                    self.run_phase(self.phase_mlp, l)
            self.run_phase(self.phase_out)
        return nc

    def _uniq(self, name):
        self._nalloc = getattr(self, "_nalloc", 0) + 1
        return f"{name}_{self._nalloc}"

    def sb(self, name, shape, dtype):
        return self.ph.enter_context(self.nc.sbuf_tensor(self._uniq(name), shape, dtype))

    def ps(self, name, shape, dtype=F32):
        return self.ph.enter_context(self.nc.psum_tensor(self._uniq(name), shape, dtype))

    def run_phase(self, fn, *a):
        with contextlib.ExitStack() as ph:
            keep, self.ph = self.ph, ph
            fn(*a)
            self.sc.barrier()
            self.sc.emit()
            self.ph = keep

    def _consts(self):
        sc = self.sc
        self.ident_s = self.sb("ident_s", [128, 128], F32)
        self.ones_s = self.sb("ones_s", [128, 128], F32)
        self.ng_s = self.sb("ng_s", [128, DEPTH * 2 * C], F32)
        self.eps_s = self.sb("eps_s", [128, 1], F32)
        self.b_const = Buf("consts")
        sc.dma("sp", lambda q: q.dma_start(out=self.ident_s[:], in_=self.ident[:, :]), writes=[self.b_const], key="c0")
        b1 = Buf("c1"); b2 = Buf("c2"); self.b_eps = Buf("eps")
        sc.dma("sp", lambda q: q.dma_start(out=self.ones_s[:], in_=self.onesD[:, :]), writes=[b1], key="c1")
        sc.dma("sp", lambda q: q.dma_start(out=self.ng_s[:], in_=self.ng[:, :]), writes=[b2], key="c2")
        sc.op("dve", lambda v: v.memset(self.eps_s[:], EPS), writes=[self.b_eps])
        self.b_ones = b1
        self.b_ng = b2
        self.b_xT = Buf("xT_dram")

    def phase_in(self):
        sc, nc = self.sc, self.nc
        xin = [self.sb(f"pin_x{i}", [128, 4, D], F32) for i in range(2)]
        xo = [self.sb(f"pin_o{i}", [128, C, NT], F32) for i in range(2)]
        pst = [self.ps(f"pin_ps{i}", [128, NT]) for i in range(4)]
        b_in = [Buf(f"pin_x{i}") for i in range(2)]
        b_o = [Buf(f"pin_o{i}") for i in range(2)]
        b_ps = [Buf(f"pin_ps{i}") for i in range(4)]
        stores = []
        for t in range(S // NT):
            i = t % 2
            src = self.x_in[t * NT:(t + 1) * NT, :].rearrange("(j p) d -> p j d", p=128)
            sc.dma("sp", lambda q, i=i, src=src: q.dma_start(out=xin[i][:], in_=src), writes=[b_in[i]])
            for c in range(C):
                k = c % 4
                fns = [(lambda e, i=i, c=c, j=j, k=k: e.matmul(pst[k][:, j * 128:(j + 1) * 128],
                                                              lhsT=xin[i][:, j, c * 128:(c + 1) * 128],
                                                              rhs=self.ident_s[:], start=True, stop=True))
                       for j in range(4)]
                sc.op("pe", fns, reads=[b_in[i], self.b_const], writes=[b_ps[k]])
                eng = "dve" if c % 2 == 0 else "act"
                if eng == "dve":
                    sc.op("dve", lambda v, i=i, c=c, k=k: v.tensor_copy(out=xo[i][:, c, :], in_=pst[k][:]),
                          reads=[b_ps[k]], writes=[b_o[i]])
                else:
                    sc.op("act", lambda a, i=i, c=c, k=k: a.copy(out=xo[i][:, c, :], in_=pst[k][:]),
                          reads=[b_ps[k]], writes=[b_o[i]])
            dst = self.xT[:, t * NT:(t + 1) * NT].rearrange("(c p) n -> p c n", p=128)
            stores.append(sc.dma("sp", lambda q, i=i, dst=dst: q.dma_start(out=dst, in_=xo[i][:]),
                                 reads=[b_o[i]], writes=[self.b_xT], key=f"st{i}"))
        self.b_xT.r = []
        self._xT_tokens = stores

    def phase_out(self):
        sc = self.sc
        xi = [self.sb(f"po_x{i}", [128, C, NT], F32) for i in range(2)]
        xo = [self.sb(f"po_o{i}", [128, 4, D], F32) for i in range(2)]
        pst = [self.ps(f"po_ps{i}", [128, NT]) for i in range(4)]
        b_i = [Buf(f"po_x{i}") for i in range(2)]
        b_o = [Buf(f"po_o{i}") for i in range(2)]
        b_ps = [Buf(f"po_ps{i}") for i in range(4)]
        b_out = Buf("out_dram")
        toks = []
        for t in range(S // NT):
            i = t % 2
            src = self.xT[:, t * NT:(t + 1) * NT].rearrange("(c p) n -> p c n", p=128)
            self._wait_xT("sp")
            sc.dma("sp", lambda q, i=i, src=src: q.dma_start(out=xi[i][:], in_=src), writes=[b_i[i]])
            n = 0
            for j in range(4):
                for h in range(2):
                    k = n % 4
                    fns = [(lambda e, i=i, j=j, c=h * 4 + cc, cc=cc, k=k:
                            e.matmul(pst[k][:, cc * 128:(cc + 1) * 128], lhsT=xi[i][:, c, j * 128:(j + 1) * 128],
                                     rhs=self.ident_s[:], start=True, stop=True)) for cc in range(4)]
                    sc.op("pe", fns, reads=[b_i[i], self.b_const], writes=[b_ps[k]])
                    if n % 2 == 0:
                        sc.op("dve", lambda v, i=i, j=j, h=h, k=k: v.tensor_copy(out=xo[i][:, j, h * 512:(h + 1) * 512], in_=pst[k][:]),
                              reads=[b_ps[k]], writes=[b_o[i]])
                    else:
                        sc.op("act", lambda a, i=i, j=j, h=h, k=k: a.copy(out=xo[i][:, j, h * 512:(h + 1) * 512], in_=pst[k][:]),
                              reads=[b_ps[k]], writes=[b_o[i]])
                    n += 1
            dst = self.out[t * NT:(t + 1) * NT, :].rearrange("(j p) d -> p j d", p=128)
            toks.append(sc.dma("sp", lambda q, i=i, dst=dst: q.dma_start(out=dst, in_=xo[i][:]),
                               reads=[b_o[i]], writes=[b_out], key=f"st{i}"))
        self.sc._need("sp", toks)

    def phase_qkv(self, l):
        sc = self.sc
        j = l // 2
        n = NT
        wq = self.sb("wqkv_s", [128, C, 3 * D], BF16)
        bw = Buf("wqkv")
        for c in range(C):
            sc.dma("pool", lambda q, c=c: q.dma_start(out=wq[:, c, :], in_=self.wqkv[j, c * 128:(c + 1) * 128, :]), writes=[bw], key="w1")
        blk = self.sb("blk_s", [128, 128], F32); bblk = Buf("blk")
        qkg = self.sb("qkg_s", [128, 4], F32); bg = Buf("qkg")
        sc.dma("sp", lambda q: q.dma_start(out=blk[:], in_=self.blk64[:, :]), writes=[bblk], key="c0")
        sc.dma("sp", lambda q: q.dma_start(out=qkg[:], in_=self.qkg[:, :]), writes=[bg], key="c1")
        x = [self.sb(f"q_x{i}", [128, C, n], F32) for i in range(2)]
        h = [self.sb(f"q_h{i}", [128, C, n], BF16) for i in range(2)]
        scr = dict(sq=self.sb("q_sq", [128, C, n], F32), bsq=Buf("sq"), pss=self.ps("q_pss", [128, 512]), bpss=Buf("pss"),
                   rt=self.sb("q_rt", [128, n], F32), brt=Buf("rt"), rs=self.sb("q_rs", [128, n], F32), brs=Buf("rs"))
        pp = [self.ps(f"q_pp{i}", [128, 512]) for i in range(3)]
        pm = [self.ps(f"q_pm{i}", [128, 512]) for i in range(2)]
        pv = [self.ps(f"q_pv{i}", [128, 512]) for i in range(2)]
        bpp = [Buf(f"pp{i}") for i in range(3)]; bpm = [Buf(f"pm{i}") for i in range(2)]; bpv = [Buf(f"pv{i}") for i in range(2)]
        sq2 = [self.sb(f"q_sq2{i}", [128, n], F32) for i in range(2)]; bsq2 = [Buf(f"sq2{i}") for i in range(2)]
        rt2 = [self.sb(f"q_rt2{i}", [128, n], F32) for i in range(2)]; brt2 = [Buf(f"rt2{i}") for i in range(2)]
        rs2 = [self.sb(f"q_rs2{i}", [128, n], F32) for i in range(2)]; brs2 = [Buf(f"rs2{i}") for i in range(2)]
        qo = [self.sb(f"q_qo{i}", [128, n], BF16) for i in range(4)]; bqo = [Buf(f"qo{i}") for i in range(4)]
        vt = [self.sb(f"q_vt{i}", [128, 4, D], BF16) for i in range(2)]; bvt = [Buf(f"vt{i}") for i in range(2)]
        bx = [Buf(f"qx{i}") for i in range(2)]; bh = [Buf(f"qh{i}") for i in range(2)]
        b_scr = Buf("qkv_dram")
        gcol = (l * 2) * C
        nq = 0
        T = S // n

        def load(t):
            src = self.xT[:, t * n:(t + 1) * n].rearrange("(c p) n -> p c n", p=128)
            self._wait_xT("sp")
            sc.dma("sp", lambda q, src=src: q.dma_start(out=x[t % 2][:], in_=src), writes=[bx[t % 2]], key=f"qx{t % 2}")

        load(0)
        for t in range(T):
            i = t % 2
            self.norm(x[i], bx[i], h[i], bh[i], n, gcol, scr)
            if t + 1 < T:
                load(t + 1)

            def qk_proj(m, i=i):
                k, kp = m % 2, m % 3
                col = m * 128
                fns = [(lambda e, c=c: e.matmul(pp[kp][:], lhsT=wq[:, c, col:col + 128], rhs=h[i][:, c, :], start=(c == 0), stop=(c == C - 1)))
                       for c in range(C)]
                sc.op("pe", fns, reads=[bh[i], bw], writes=[bpp[kp]])
                sc.op("act", lambda a: a.activation(out=sq2[k][:], in_=pp[kp][:], func=AF.Square), reads=[bpp[kp]], writes=[bsq2[k]])

            def qk_fin(m, o, t=t):
                k, kp = m % 2, m % 3
                sc.op("pe", lambda e: e.matmul(pm[k][:], lhsT=blk[:], rhs=sq2[k][:], start=True, stop=True), reads=[bsq2[k], bblk], writes=[bpm[k]])
                sc.op("act", lambda a: a.activation(out=rt2[k][:], in_=pm[k][:], func=AF.Sqrt, bias=self.eps_s[:, 0:1], scale=1.0),
                      reads=[bpm[k], self.b_eps], writes=[brt2[k]])
                sc.op("dve", lambda v: v.reciprocal(out=rs2[k][:], in_=rt2[k][:]), reads=[brt2[k]], writes=[brs2[k]])
                gc = j * 2 + (0 if m < 8 else 1)
                sc.op("dve", lambda v: v.scalar_tensor_tensor(out=qo[o][:], in0=pp[kp][:], scalar=qkg[:, gc:gc + 1], in1=rs2[k][:], op0=ALU.mult, op1=ALU.mult),
                      reads=[bpp[kp], brs2[k], bg], writes=[bqo[o]])
                dstT = (self.QT if m < 8 else self.KT)[(m % 8) * 128:(m % 8 + 1) * 128, t * n:(t + 1) * n]
                sc.dma("sp", lambda q: q.dma_start(out=dstT, in_=qo[o][:]), reads=[bqo[o]], writes=[], key=f"qst{o}")

            for m in range(17):
                if m < 16:
                    qk_proj(m)
                if m >= 1:
                    qk_fin(m - 1, nq % 4)
                    nq += 1
            for jj in range(4):
                for half in range(2):
                    k = (jj * 2 + half) % 2
                    fns = [(lambda e, i=i, c=c, k=k, jj=jj, half=half: e.matmul(pv[k][:], lhsT=h[i][:, c, jj * 128:(jj + 1) * 128],
                                                                                 rhs=wq[:, c, 2 * D + half * 512:2 * D + (half + 1) * 512],
                                                                                 start=(c == 0), stop=(c == C - 1))) for c in range(C)]
                    sc.op("pe", fns, reads=[bh[i], bw], writes=[bpv[k]])
                    sc.op("act", lambda a, i=i, k=k, jj=jj, half=half: a.copy(out=vt[i][:, jj, half * 512:(half + 1) * 512], in_=pv[k][:]),
                          reads=[bpv[k]], writes=[bvt[i]])
            dstV = self.V[t * n:(t + 1) * n, :].rearrange("(j p) d -> p j d", p=128)
            sc.dma("sp", lambda q, i=i, dstV=dstV: q.dma_start(out=dstV, in_=vt[i][:]), reads=[bvt[i]], writes=[], key=f"vst{i}")

    def phase_attn(self, l):
        sc = self.sc
        j = l // 2
        idb = self.sb("idb_s", [128, 128], BF16); bidb = Buf("idb")
        mk = self.sb("mk_s", [128, 128], BF16); bmk = Buf("mk")
        sc.dma("pool", lambda q: q.dma_start(out=idb[:], in_=self.identb[:, :]), writes=[bidb], key="c0")
        sc.dma("pool", lambda q: q.dma_start(out=mk[:], in_=self.maskneg[:, :]), writes=[bmk], key="c1")
        wo = self.sb("wo_s", [128, C, D], BF16); bwo = Buf("wo")
        for c0 in range(0, C, 4):
            src = self.wo_sb[j, c0 * 128:(c0 + 4) * 128, :].rearrange("(c p) d -> p c d", p=128)
            sc.dma("pool", lambda q, c0=c0, src=src: q.dma_start(out=wo[:, c0:c0 + 4, :], in_=src), writes=[bwo], key="w1")
        OT = self.sb("OT_s", [128, C, S], BF16)
        bOT = [Buf(f"OT{hp}") for hp in range(C)]
        po = [self.ps(f"a_po{i}", [128, 512]) for i in range(2)]; bpo = [Buf(f"po{i}") for i in range(2)]
        self.run_phase(self._attn_core, l, OT, bOT, po, bpo, idb, bidb, mk, bmk)
        self._attn_wo(l, OT, bOT, po, bpo, wo, bwo)

    def _attn_core(self, l, OT, bOT, po, bpo, idb, bidb, mk, bmk):
        sc = self.sc
        qT = [self.sb(f"a_q{i}", [128, S], BF16) for i in range(2)]; bq = [Buf("aq0"), Buf("aq1")]
        kT = [self.sb(f"a_k{i}", [128, S], BF16) for i in range(2)]; bk = [Buf("ak0"), Buf("ak1")]
        VA = [self.sb(f"a_va{i}", [128, 32, 128], BF16) for i in range(2)]; bva = [Buf("va0"), Buf("va1")]
        VB = [self.sb(f"a_vb{i}", [128, 32, 128], BF16) for i in range(2)]; bvb = [Buf("vb0"), Buf("vb1")]
        for i in range(2):
            sc.op("pool", lambda v, i=i: v.memset(VA[i][:], 0.0), writes=[bva[i]])
            sc.op("pool", lambda v, i=i: v.memset(VB[i][:], 0.0), writes=[bvb[i]])
        CH = 1024
        g = [self.sb(f"a_g{i}", [128, CH], F32) for i in range(2)]; bgt = [Buf(f"ag{i}") for i in range(2)]
        Pf = [self.sb(f"a_P{i}", [128, S + 1], F32) for i in range(2)]; bP = [Buf(f"aP{i}") for i in range(2)]
        zer = self.sb("a_zero", [128, CH], F32); bz = Buf("zero")
        sc.op("pool", lambda v: v.memset(zer[:], 0.0), writes=[bz])
        w = [self.sb(f"a_w{i}", [128, CH], BF16) for i in range(2)]; bwt = [Buf(f"aw{i}") for i in range(2)]
        wT = [self.sb(f"a_wT{i}", [128, 8, 128], BF16) for i in range(2)]; bwT = [Buf(f"awT{i}") for i in range(2)]
        pz = [self.ps(f"a_pz{i}", [128, CH]) for i in range(2)]; bpz = [Buf(f"pz{i}") for i in range(2)]
        pt = self.ps("a_pt", [128, CH]); bpt = Buf("pt")
        chunks = []
        for hp in range(C):
            for Q in range(S // 128):
                t1 = 128 * (Q + 1)
                for hd in range(2):
                    b_ = t1
                    while b_ > 0:
                        a_ = max(0, b_ - CH)
                        chunks.append(dict(hp=hp, Q=Q, hd=hd, a=a_, b=b_, L=b_ - a_, t1=t1, first=(b_ == t1), newhp=(Q == 0 and hd == 0 and b_ == t1),
                                           first_pv=(hd == 0 and b_ == t1), last=(hd == 1 and a_ == 0),
                                           ip=(hp * 64 + Q * 2 + hd) % 2, io=(hp * 32 + Q) % 2))
                        b_ = a_
        cnt_hp = {}
        for n_, ch in enumerate(chunks):
            ch["iz"] = n_ % 2
            ch["k_in_hp"] = cnt_hp.get(ch["hp"], 0)
            cnt_hp[ch["hp"]] = ch["k_in_hp"] + 1

        def loads(hpx):
            i = hpx % 2
            rows = slice(hpx * 128, (hpx + 1) * 128)
            sc.dma("sp", lambda q: q.dma_start(out=qT[i][:], in_=self.QT[rows, :]), writes=[bq[i]], key=f"ld{i}")
            sc.dma("sp", lambda q: q.dma_start(out=kT[i][:], in_=self.KT[rows, :]), writes=[bk[i]], key=f"ldk{i}")
            srcA = self.V[:, hpx * 128:hpx * 128 + 64].rearrange("(b p) d -> p b d", p=128)
            srcB = self.V[:, hpx * 128 + 64:hpx * 128 + 128].rearrange("(b p) d -> p b d", p=128)
            for b0 in range(0, 32, 8):
                sc.dma("sp", lambda q, b0=b0: q.dma_start(out=VA[i][:, b0:b0 + 8, 0:64], in_=srcA[:, b0:b0 + 8, :]), writes=[bva[i]], key=f"lda{i}")
                sc.dma("sp", lambda q, b0=b0: q.dma_start(out=VB[i][:, b0:b0 + 8, 64:128], in_=srcB[:, b0:b0 + 8, :]), writes=[bvb[i]], key=f"ldb{i}")

        loads(0)

        def stage_A(ch):
            hp, Q, hd, a, b, L, iz = ch["hp"], ch["Q"], ch["hd"], ch["a"], ch["b"], ch["L"], ch["iz"]
            i = hp % 2
            if ch["k_in_hp"] == 4 and hp + 1 < C:
                loads(hp + 1)
            pr = slice(hd * 64, hd * 64 + 64)
            fns = []
            for p0 in range(0, L, 512):
                pl = min(512, L - p0)
                diag = ch["first"] and (p0 + pl == L)
                fns.append(lambda e, p0=p0, pl=pl, diag=diag: e.matmul(pz[iz][:, p0:p0 + pl], lhsT=qT[i][pr, Q * 128:(Q + 1) * 128],
                                                                       rhs=kT[i][pr, a + p0:a + p0 + pl], start=True, stop=not diag))
                if diag:
                    fns.append(lambda e: e.matmul(pz[iz][:, L - 128:L], lhsT=idb[:], rhs=mk[:], start=False, stop=True))
            sc.op("pe", fns, reads=[bq[i], bk[i], bidb, bmk], writes=[bpz[iz]])
            sc.op("act", lambda a_: a_.activation(out=g[iz][:, :L], in_=pz[iz][:, :L], func=AF.Sigmoid, scale=-0.125), reads=[bpz[iz]], writes=[bgt[iz]])

        def stage_B(ch):
            a, b, L, iz, ip, t1 = ch["a"], ch["b"], ch["L"], ch["iz"], ch["ip"], ch["t1"]
            if ch["newhp"]:
                for ipx in range(2):
                    sc.op("dve", lambda v, ipx=ipx: v.memset(Pf[ipx][:, 128:S + 1:128], 1.0), writes=[bP[ipx]])
            sc.op("dve", lambda v: v.tensor_tensor_scan(out=Pf[ip][:, b - 1:(a - 1 if a > 0 else None):-1], data0=g[iz][:, L - 1::-1], data1=zer[:, :L],
                                                        initial=Pf[ip][:, b:b + 1], op0=ALU.mult, op1=ALU.add), reads=[bgt[iz], bP[ip], bz], writes=[bP[ip]])
            eng = {"pool": "pool", "dve": "dve", "alt": ("pool" if iz == 0 else "dve")}[ATT_SUB]
            sc.op(eng, lambda v: v.tensor_tensor(out=w[iz][:, :L], in0=Pf[ip][:, a + 1:b + 1], in1=Pf[ip][:, a:b], op=ALU.subtract),
                  reads=[bP[ip]], writes=[bwt[iz]])

        def stage_C1(ch):
            L, iz = ch["L"], ch["iz"]
            nb = L // 128
            fns = [(lambda e, bb=bb: e.matmul(pt[:, bb * 128:(bb + 1) * 128], lhsT=w[iz][:, bb * 128:(bb + 1) * 128], rhs=idb[:], start=True, stop=True))
                   for bb in range(nb)]
            sc.op("pe", fns, reads=[bwt[iz], bidb], writes=[bpt])
            sc.op("act", lambda a_: a_.copy(out=wT[iz][:, :nb, :], in_=pt[:, :nb * 128]), reads=[bpt], writes=[bwT[iz]])

        def stage_C2(ch):
            hp, Q, hd, a, L, iz, io = ch["hp"], ch["Q"], ch["hd"], ch["a"], ch["L"], ch["iz"], ch["io"]
            i = hp % 2
            Vh, bV = (VA[i], bva[i]) if hd == 0 else (VB[i], bvb[i])
            nb = L // 128
            fns = []
            for bb in range(nb):
                kb = a // 128 + bb
                fns.append(lambda e, bb=bb, kb=kb, fp=(ch["first_pv"] and bb == 0), last=(ch["last"] and bb == nb - 1):
                           e.matmul(po[io][:, :128], lhsT=Vh[:, kb, :], rhs=wT[iz][:, bb, :], start=fp, stop=last))
            sc.op("pe", fns, reads=[bwT[iz], bV], writes=[bpo[io]])
            if ch["last"]:
                sc.op("act", lambda a_: a_.copy(out=OT[:, hp, Q * 128:(Q + 1) * 128], in_=po[io][:, :128]), reads=[bpo[io]], writes=[bOT[hp]])

        nchk = len(chunks)
        for n_ in range(nchk + 3):
            if n_ < nchk:
                stage_A(chunks[n_])
            if 0 <= n_ - 1 < nchk:
                stage_B(chunks[n_ - 1])
            if 0 <= n_ - 2 < nchk:
                stage_C1(chunks[n_ - 2])
            if 0 <= n_ - 3 < nchk:
                stage_C2(chunks[n_ - 3])

    def _attn_wo(self, l, OT, bOT, po, bpo, wo, bwo):
        sc = self.sc
        n = NT
        x = [self.sb(f"o_x{i}", [128, C, n], F32) for i in range(2)]; bx = [Buf(f"ox{i}") for i in range(2)]
        stores = []
        def load(t):
            src = self.xT[:, t * n:(t + 1) * n].rearrange("(c p) n -> p c n", p=128)
            self._wait_xT("sp")
            sc.dma("sp", lambda q, src=src: q.dma_start(out=x[t % 2][:], in_=src), writes=[bx[t % 2]], key=f"ldx{t % 2}")

        load(0)
        for t in range(S // n):
            i = t % 2
            if t + 1 < S // n:
                load(t + 1)
            for o in range(C):
                k = o % 2
                fns = [(lambda e, o=o, c=c, k=k, t=t: e.matmul(po[k][:], lhsT=wo[:, c, o * 128:(o + 1) * 128], rhs=OT[:, c, t * n:(t + 1) * n],
                                                             start=(c == 0), stop=(c == C - 1))) for c in range(C)]
                sc.op("pe", fns, reads=bOT + [bwo], writes=[bpo[k]])
                sc.op("dve", lambda v, i=i, o=o, k=k: v.tensor_tensor(out=x[i][:, o, :], in0=po[k][:], in1=x[i][:, o, :], op=ALU.add),
                      reads=[bpo[k], bx[i]], writes=[bx[i]])
            dst = self.xT[:, t * n:(t + 1) * n].rearrange("(c p) n -> p c n", p=128)
            stores.append(sc.dma("sp", lambda q, i=i, dst=dst: q.dma_start(out=dst, in_=x[i][:]), reads=[bx[i]], writes=[self.b_xT], key=f"st{i}"))
        self.b_xT.r = []
        self._xT_tokens = stores

    def phase_hgrn(self, l):
        sc = self.sc
        j = l // 2
        NB = 128
        HH = 8
        win = self.sb("win_s", [128, C, 4 * D], BF16); bwin = Buf("win")
        for c in range(C):
            for hf in range(2):
                sc.dma("pool", lambda q, c=c, hf=hf: q.dma_start(out=win[:, c, hf * 2048:(hf + 1) * 2048],
                                                                  in_=self.w_in[j, c * 128:(c + 1) * 128, hf * 2048:(hf + 1) * 2048]), writes=[bwin], key="w1")
        wo = self.sb("hwo_s", [128, C, D], BF16); bwo = Buf("hwo")
        for c0 in range(0, C, 4):
            src = self.wo_hg[j, c0 * 128:(c0 + 4) * 128, :].rearrange("(c p) d -> p c d", p=128)
            sc.dma("pool", lambda q, c0=c0, src=src: q.dma_start(out=wo[:, c0:c0 + 4, :], in_=src), writes=[bwo], key="w2")
        M1 = self.sb("M1_s", [128, 256], F32); M2 = self.sb("M2_s", [128, 128], F32); Mc = self.sb("Mc_s", [128, 1024], F32)
        o128 = self.sb("o128_s", [128, 128], F32); idb = self.sb("hidb_s", [128, 128], BF16)
        lrow = self.sb("lrow_s", [128, 2, D], F32)
        lT = self.sb("lT_s", [128, 2 * C], F32); gg = self.sb("gg_s", [128, 2], F32)
        bK = Buf("hconst")
        sc.dma("sp", lambda q: q.dma_start(out=M1[:], in_=self.M1[:, :]), writes=[bK], key="c0")
        sc.dma("sp", lambda q: q.dma_start(out=M2[:], in_=self.M2[:, :]), writes=[bK], key="c0")
        sc.dma("sp", lambda q: q.dma_start(out=Mc[:], in_=self.Mc[:, :]), writes=[bK], key="c0")
        sc.dma("sp", lambda q: q.dma_start(out=o128[:], in_=self.ones128[:, :]), writes=[bK], key="c0")
        sc.dma("sp", lambda q: q.dma_start(out=lT[:], in_=self.lblT[:, :]), writes=[bK], key="c0")
        sc.dma("sp", lambda q: q.dma_start(out=gg[:], in_=self.hgg[:, :]), writes=[bK], key="c0")
        for jj in range(2):
            sc.dma("sp", lambda q, jj=jj: q.dma_start(out=lrow[:, jj, :], in_=self.lbl[jj:jj + 1, :].partition_broadcast(128)), writes=[bK], key="c0")
        omr = self.sb("omr_s", [128, D], F32); omT = self.sb("omT_s", [128, C], F32); bom = Buf("om")
        one1 = self.sb("one1_s", [128, 1], F32)
        sc.op("dve", lambda v: v.memset(one1[:], 1.0), writes=[bom])
        if j == 0:
            sc.op("dve", lambda v: v.memset(omr[:], 1.0), writes=[bom])
            sc.op("dve", lambda v: v.memset(omT[:], 1.0), writes=[bom])
        else:
            dr = self.sb("dr_s", [128, D], F32); dT = self.sb("dT_s", [128, C], F32); bd = Buf("dlt")
            sc.op("dve", lambda v: v.tensor_tensor(out=dr[:], in0=lrow[:, 0, :], in1=lrow[:, 1, :], op=ALU.subtract), reads=[bK], writes=[bd])
            sc.op("dve", lambda v: v.tensor_tensor(out=dT[:], in0=lT[:, 0:C], in1=lT[:, C:2 * C], op=ALU.subtract), reads=[bK], writes=[bd])
            sc.op("act", lambda a: a.activation(out=omr[:], in_=dr[:], func=AF.Sigmoid), reads=[bd], writes=[bom])
            sc.op("act", lambda a: a.activation(out=omT[:], in_=dT[:], func=AF.Sigmoid), reads=[bd], writes=[bom])
        sc.dma("pool", lambda q: q.dma_start(out=idb[:], in_=self.identb[:, :]), writes=[bK], key="c1")
        St = self.sb("S_s", [128, HH, 128], F32); bS = [Buf(f"S{h}") for h in range(HH)]
        Sb = self.sb("Sb_s", [128, HH, 128], BF16); bSb = [Buf(f"Sb{h}") for h in range(HH)]
        sc.op("pool", lambda v: v.memset(St[:], 0.0), writes=bS)
        x = [self.sb(f"g_x{i}", [128, C, NB], F32) for i in range(2)]; bx = [Buf(f"gx{i}") for i in range(2)]
        h2 = [self.sb(f"g_h{i}", [128, C, NB], BF16) for i in range(2)]; bh2 = [Buf(f"gh{i}") for i in range(2)]
        scr = dict(sq=self.sb("g_sq", [128, C, NB], F32), bsq=Buf("sq"), pss=self.ps("g_pss", [128, 512]), bpss=Buf("pss"),
                   rt=self.sb("g_rt", [128, NB], F32), brt=Buf("rt"), rs=self.sb("g_rs", [128, NB], F32), brs=Buf("rs"))
        pA = self.ps("g_pA", [128, 1024]); bpA = [Buf("pA0"), Buf("pA1")]
        pB = self.ps("g_pB", [128, 1024]); bpB = [Buf("pB0"), Buf("pB1")]
        pX = self.ps("g_pX", [128, 1024]); bpX = [Buf("pX0"), Buf("pX1")]
        pE = self.ps("g_pE", [128, 512]); bpE = Buf("pE")
        W = HH * NB

        def t2(name, dtype=F32):
            return self.sb(name, [128, W], dtype), Buf(name)
        sbar, bsb = t2("g_sbar"); ktok, bkt = t2("g_ktok"); lf, blf = t2("g_lf")
        vtok, bvt = t2("g_vtok", BF16); khat, bkh = t2("g_khat", BF16)
        ed, bed = sbar, bsb
        qraw, bqr = t2("g_qraw"); graw, bgr = t2("g_graw"); fraw, bfr = t2("g_fraw")
        sgq, bsgq = t2("g_sgq"); sgg, bsgg = t2("g_sgg"); kf, bkf = t2("g_kf"); qf, bqf = t2("g_qf")
        E1, bE1 = t2("g_E1"); E2, bE2 = t2("g_E2")
        EX = self.sb("g_EX", [128, 4 * HH], F32); bEX = Buf("EX")
        qt, bqt = t2("g_qt", BF16); kt, bkt2 = t2("g_kt", BF16); scT, bsc = t2("g_scT", BF16)
        oT, boT = t2("g_oT"); osq, bosq = t2("g_osq"); ort, bort = t2("g_ort"); on, bon = t2("g_on", BF16)
        omF, bomF = t2("g_omF")
        sc.op("dve", lambda v: v.memset(omF[:], 1.0), writes=[bomF])
        if j != 0:
            for hh in range(HH):
                sc.op("dve", lambda v, hh=hh: v.tensor_scalar_mul(out=omF[:, hh * 128:(hh + 1) * 128], in0=omF[:, hh * 128:(hh + 1) * 128], scalar1=omT[:, hh:hh + 1]),
                      reads=[bom, bomF], writes=[bomF])
        hsl = lambda hh: slice(hh * 128, (hh + 1) * 128)
        bchain = Buf("act_chain")
        gcol = (l * 2) * C
        stores = []
        nblk = self.hg_blocks or (S // NB)

        def load(t):
            src = self.xT[:, t * NB:(t + 1) * NB].rearrange("(c p) n -> p c n", p=128)
            self._wait_xT("sp")
            sc.dma("sp", lambda q, src=src: q.dma_start(out=x[t % 2][:], in_=src), writes=[bx[t % 2]], key=f"ldx{t % 2}")

        def stage1a(t):
            h, bh = h2[t % 2], bh2[t % 2]
            for (pp, bpp, off) in ((pA, bpA, D), (pB, bpB, 2 * D)):
                fns = []
                for hf in range(2):
                    for c in range(C):
                        fns.append(lambda e, pp=pp, off=off, hf=hf, c=c, h=h: e.matmul(pp[:, hf * 512:(hf + 1) * 512], lhsT=h[:, c, :],
                                                                                  rhs=win[:, c, off + hf * 512:off + (hf + 1) * 512],
                                                                                  start=(c == 0), stop=(c == C - 1)))
                sc.op("pe", fns, reads=[bh, bwin], writes=bpp)
            sc.op("act", lambda a: a.activation(out=sbar[:], in_=pA[:], func=AF.Sigmoid, scale=-1.0), reads=bpA, writes=[bsb])
            sc.op("act", lambda a: a.copy(out=vtok[:], in_=pB[:]), reads=bpB, writes=[bvt])
            sc.op("dve", lambda v: v.tensor_tensor(out=ktok[:], in0=sbar[:], in1=omr[:], op=ALU.mult), reads=[bsb, bom], writes=[bkt])
            sc.op("act", lambda a: a.activation(out=lf[:], in_=ktok[:], func=AF.Ln, bias=one1[:, 0:1], scale=-1.0), reads=[bkt, bom], writes=[blf])

        def stage1b(t):
            fns = [(lambda e, hf=hf: e.matmul(pA[:, hf * 512:(hf + 1) * 512], lhsT=M2[:], rhs=lf[:, hf * 512:(hf + 1) * 512], start=True, stop=True))
                   for hf in range(2)]
            sc.op("pe", fns, reads=[blf, bK], writes=bpA)
            sc.op("act", lambda a: a.activation(out=ed[:], in_=pA[:], func=AF.Exp), reads=bpA, writes=[bed])
            sc.op("dve", lambda v: v.tensor_tensor(out=khat[:], in0=ed[:], in1=ktok[:], op=ALU.mult), reads=[bed, bkt], writes=[bkh])

        def stage2(t):
            h, bh = h2[t % 2], bh2[t % 2]
            if self.hg_parts >= 2:
                def proj(dst, base):
                    return [(lambda e, hh=hh, c=c, h=h: e.matmul(dst[:, hsl(hh)], lhsT=win[:, c, base + hh * 128:base + (hh + 1) * 128], rhs=h[:, c, :],
                                                            start=(c == 0), stop=(c == C - 1))) for hh in range(HH) for c in range(C)]
                for dst, bdst, base, raw_, braw_ in ((pX, bpX, 0, qraw, bqr), (pA, bpA, 3 * D, graw, bgr), (pB, bpB, D, fraw, bfr)):
                    sc.op("pe", proj(dst, base), reads=[bh, bwin], writes=bdst)
                    sc.op("dve", lambda v, dst=dst, raw_=raw_: v.tensor_copy(out=raw_[:], in_=dst[:]), reads=bdst, writes=[braw_])
                sc.op("act", lambda a: a.activation(out=sgq[:], in_=qraw[:], func=AF.Sigmoid), reads=[bqr], writes=[bsgq])
                sc.op("act", lambda a: a.activation(out=sgg[:], in_=graw[:], func=AF.Sigmoid), reads=[bgr], writes=[bsgg])
                sc.op("act", lambda a: a.activation(out=kf[:], in_=fraw[:], func=AF.Sigmoid, scale=-1.0), reads=[bfr], writes=[bkf])
                sc.op("dve", lambda v: v.tensor_tensor(out=qf[:], in0=qraw[:], in1=sgq[:], op=ALU.mult), reads=[bqr, bsgq], writes=[bqf])
                sc.op("pe", [(lambda e, hh=hh: e.matmul(pX[:, hsl(hh)], lhsT=lf[:, hsl(hh)], rhs=M1[:, 0:128], start=True, stop=True)) for hh in range(HH)],
                      reads=[blf, bK], writes=bpX)
                sc.op("pe", [(lambda e, hh=hh: e.matmul(pE[:, hh * 4:(hh + 1) * 4], lhsT=lf[:, hsl(hh)], rhs=M1[:, 128:132], start=True, stop=True)) for hh in range(HH)],
                      reads=[blf, bK], writes=[bpE])
                sc.op("act", lambda a: a.activation(out=E1[:], in_=pX[:], func=AF.Exp), reads=bpX, writes=[bE1])
                sc.op("act", lambda a: a.activation(out=E2[:], in_=pX[:], func=AF.Exp, scale=-1.0), reads=bpX, writes=[bE2])
                sc.op("act", lambda a: a.activation(out=EX[:], in_=pE[:, 0:4 * HH], func=AF.Exp), reads=[bpE], writes=[bEX])
                sc.op("dve", lambda v: v.tensor_tensor(out=qt[:], in0=qf[:], in1=E1[:], op=ALU.mult), reads=[bqf, bE1], writes=[bqt])
                sc.op("dve", lambda v: v.tensor_tensor(out=kf[:], in0=kf[:], in1=E2[:], op=ALU.mult), reads=[bkf, bE2], writes=[bkf])
                sc.op("dve", lambda v: v.tensor_tensor(out=kt[:], in0=kf[:], in1=omF[:], op=ALU.mult), reads=[bkf, bomF], writes=[bkt2])

        def rec(t):
            if self.hg_parts >= 3:
                sc.op("pe", [(lambda e, hh=hh: e.matmul(pA[:, hsl(hh)], lhsT=kt[:, hsl(hh)], rhs=qt[:, hsl(hh)], start=True, stop=True)) for hh in range(HH)],
                      reads=[bkt2, bqt], writes=bpA)
                sc.op("dve", lambda v: v.tensor_tensor(out=scT[:], in0=pA[:], in1=Mc[:], op=ALU.mult), reads=bpA + [bK], writes=[bsc])
                for cc in range(2):
                    cs = slice(cc * 64, cc * 64 + 64)
                    for hh in range(HH):
                        sc.op("act", lambda a, hh=hh, cc=cc: a.activation(out=Sb[:, hh, :], in_=St[:, hh, :], func=AF.Copy, scale=EX[:, hh * 4 + cc:hh * 4 + cc + 1]),
                              reads=[bS[hh], bEX], writes=[bSb[hh], bchain])
                    fns = []
                    for hh in range(HH):
                        col = slice(hh * 128 + cc * 64, hh * 128 + cc * 64 + 64)
                        fns.append(lambda e, hh=hh, col=col, cs=cs: e.matmul(pB[:, col], lhsT=vtok[cs, hsl(hh)], rhs=scT[cs, col], start=True, stop=False))
                        fns.append(lambda e, hh=hh, col=col: e.matmul(pB[:, col], lhsT=Sb[:, hh, :], rhs=qt[:, col], start=False, stop=True))
                    sc.op("pe", fns, reads=[bvt, bsc, bqt] + bSb, writes=bpB)
                    sc.op("pe", [(lambda e, hh=hh, cs=cs: e.matmul(pX[:, hsl(hh)], lhsT=khat[cs, hsl(hh)], rhs=vtok[cs, hsl(hh)], start=True, stop=True)) for hh in range(HH)],
                          reads=[bkh, bvt], writes=bpX)
                    for hh in range(HH):
                        sc.op("dve", lambda v, hh=hh, cc=cc: v.scalar_tensor_tensor(out=St[:, hh, :], in0=St[:, hh, :], scalar=EX[:, hh * 4 + 2 + cc:hh * 4 + 3 + cc],
                                                                                      in1=pX[:, hsl(hh)], op0=ALU.mult, op1=ALU.add),
                              reads=[bS[hh], bEX] + bpX, writes=[bS[hh]])
                sc.op("dve", lambda v: v.tensor_copy(out=oT[:], in_=pB[:]), reads=bpB, writes=[boT])

        def post_a(t):
            i = t % 2
            if self.hg_parts >= 3:
                sc.op("act", lambda a: a.activation(out=osq[:], in_=oT[:], func=AF.Square), reads=[boT], writes=[bosq])
                sc.op("pe", [(lambda e, hf=hf: e.matmul(pX[:, hf * 512:(hf + 1) * 512], lhsT=o128[:], rhs=osq[:, hf * 512:(hf + 1) * 512], start=True, stop=True))
                             for hf in range(2)], reads=[bosq, bK], writes=bpX)
                sc.op("act", lambda a: a.activation(out=ort[:], in_=pX[:], func=AF.Sqrt, bias=self.eps_s[:, 0:1], scale=1.0), reads=bpX + [self.b_eps], writes=[bort])
                sc.op("dve", lambda v: v.reciprocal(out=osq[:], in_=ort[:]), reads=[bort], writes=[bosq])
                sc.op("dve", lambda v: v.scalar_tensor_tensor(out=ort[:], in0=oT[:], scalar=gg[:, j:j + 1], in1=osq[:], op0=ALU.mult, op1=ALU.mult),
                      reads=[boT, bosq, bK], writes=[bort])
                sc.op("dve", lambda v: v.tensor_tensor(out=on[:], in0=ort[:], in1=sgg[:], op=ALU.mult), reads=[bort, bsgg], writes=[bon])

        def post_b(t):
            i = t % 2
            if self.hg_parts >= 3:
                sc.op("pe", [(lambda e, o=o, c=c: e.matmul(pX[:, hsl(o)], lhsT=wo[:, c, o * 128:(o + 1) * 128], rhs=on[:, hsl(c)], start=(c == 0), stop=(c == C - 1)))
                             for o in range(C) for c in range(C)], reads=[bon, bwo], writes=bpX)
                for o in range(C):
                    sc.op("dve", lambda v, i=i, o=o: v.tensor_tensor(out=x[i][:, o, :], in0=pX[:, hsl(o)], in1=x[i][:, o, :], op=ALU.add),
                          reads=bpX + [bx[i]], writes=[bx[i]])
            dst = self.xT[:, t * NB:(t + 1) * NB].rearrange("(c p) n -> p c n", p=128)
            stores.append(sc.dma("sp", lambda q, i=i, dst=dst: q.dma_start(out=dst, in_=x[i][:]), reads=[bx[i]], writes=[self.b_xT], key=f"st{i}"))

        load(0)
        self.norm(x[0], bx[0], h2[0], bh2[0], NB, gcol, scr)
        stage1a(0)
        stage1b(0)
        for t in range(nblk):
            if t + 1 < nblk:
                load(t + 1)
            stage2(t)
            if t + 1 < nblk:
                self.norm(x[(t + 1) % 2], bx[(t + 1) % 2], h2[(t + 1) % 2], bh2[(t + 1) % 2], NB, gcol, scr)
            rec(t)
            post_a(t)
            if t + 1 < nblk:
                stage1a(t + 1)
            post_b(t)
            if t + 1 < nblk:
                stage1b(t + 1)
        self.b_xT.r = []
        self._xT_tokens = stores

    def _wait_xT(self, eng):
        self.sc._need(eng, self._xT_tokens)

    def norm(self, x, bx, h, bh, n, gcol, scr):
        sc = self.sc
        sq, bsq, pss, bpss, rt, brt, rs, brs = (scr[k] for k in ("sq", "bsq", "pss", "bpss", "rt", "brt", "rs", "brs"))
        sc.op("act", lambda a: a.activation(out=sq[:, :, :n], in_=x[:, :, :n], func=AF.Square), reads=[bx], writes=[bsq])
        fns = [(lambda e, c=c: e.matmul(pss[:, :n], lhsT=self.ones_s[:], rhs=sq[:, c, :n], start=(c == 0), stop=(c == C - 1)))
               for c in range(C)]
        sc.op("pe", fns, reads=[bsq, self.b_ones], writes=[bpss])
        sc.op("act", lambda a: a.activation(out=rt[:, :n], in_=pss[:, :n], func=AF.Sqrt, bias=self.eps_s[:, 0:1], scale=1.0),
              reads=[bpss, self.b_eps], writes=[brt])
        sc.op("dve", lambda v: v.reciprocal(out=rs[:, :n], in_=rt[:, :n]), reads=[brt], writes=[brs])
        for c in range(C):
            eng = "dve"
            sc.op(eng, lambda v, c=c: v.scalar_tensor_tensor(out=h[:, c, :n], in0=x[:, c, :n],
                                                             scalar=self.ng_s[:, gcol + c:gcol + c + 1], in1=rs[:, :n],
                                                             op0=ALU.mult, op1=ALU.mult),
                  reads=[bx, brs, self.b_ng], writes=[bh])

    def phase_mlp(self, l):
        sc = self.sc
        n = 256
        FC = DFF // 128
        w1s = self.sb(f"w1s_{l}", [128, C, DFF], BF16)
        w2s = self.sb(f"w2s_{l}", [128, FC, D], BF16)
        bw1 = [Buf(f"w1_{l}")] * 8
        bw2 = [Buf(f"w2_{l}_{i}") for i in range(8)]
        for c in range(C):
            sc.dma("pool", lambda q, c=c: q.dma_start(out=w1s[:, c, :], in_=self.w1[l, c * 128:(c + 1) * 128, :]), writes=[bw1[0]], key="w1")
        for fb in range(8):
            f0 = fb * 4
            src = self.w2[l, f0 * 128:(f0 + 4) * 128, :].rearrange("(f p) d -> p f d", p=128)
            sc.dma("pool", lambda q, f0=f0, src=src: q.dma_start(out=w2s[:, f0:f0 + 4, :], in_=src), writes=[bw2[fb]], key=f"w2_{fb}")
        NX = 3
        x = [self.sb(f"m{l}_x{i}", [128, C, n], F32) for i in range(NX)]
        h = [self.sb(f"m{l}_h{i}", [128, C, n], BF16) for i in range(2)]
        aT = [self.sb(f"m{l}_a{i}", [128, FC, n], BF16) for i in range(2)]
        r32 = [self.sb(f"m{l}_r{i}", [128, n], F32) for i in range(2)]
        scr = dict(sq=self.sb(f"m{l}_sq", [128, C, n], F32), bsq=Buf("sq"), pss=self.ps(f"m{l}_pss", [128, 512]), bpss=Buf("pss"),
                   rt=self.sb(f"m{l}_rt", [128, n], F32), brt=Buf("rt"), rs=self.sb(f"m{l}_rs", [128, n], F32), brs=Buf("rs"))
        pa = [self.ps(f"m{l}_pa{i}", [128, 512]) for i in range(3)]
        py = [self.ps(f"m{l}_py{i}", [128, 512]) for i in range(2)]
        bx = [Buf(f"mx{i}") for i in range(NX)]
        bh = [Buf(f"mh{i}") for i in range(2)]
        ba = [[Buf(f"ma{i}_{f}") for f in range(FC)] for i in range(2)]
        br = [Buf(f"mr{i}") for i in range(2)]
        bpa = [Buf(f"mpa{i}") for i in range(3)]
        bpy = [Buf(f"mpy{i}") for i in range(2)]
        stores = []
        gcol = (l * 2 + 1) * C
        T = S // n

        def load(t):
            src = self.xT[:, t * n:(t + 1) * n].rearrange("(c p) n -> p c n", p=128)
            self._wait_xT("sp")
            sc.dma("sp", lambda q, src=src: q.dma_start(out=x[t % NX][:], in_=src), writes=[bx[t % NX]], key=f"mx{t % NX}")

        def stage1(t):
            hi, ai = t % 2, t % 2
            for f in range(FC):
                k = f % 3
                fns = [(lambda e, f=f, c=c, k=k: e.matmul(pa[k][:, :n], lhsT=w1s[:, c, f * 128:(f + 1) * 128], rhs=h[hi][:, c, :],
                                                         start=(c == 0), stop=(c == C - 1))) for c in range(C)]
                sc.op("pe", fns, reads=[bh[hi], bw1[f // 4]], writes=[bpa[k]])
                j = f % 2
                sc.op("act", lambda a, j=j, k=k: a.activation(out=r32[j][:], in_=pa[k][:, :n], func=AF.Relu), reads=[bpa[k]], writes=[br[j]])
                eng = "dve" if f % 2 == 0 else "pool"
                sc.op(eng, lambda v, j=j, f=f: v.tensor_tensor(out=aT[ai][:, f, :], in0=r32[j][:], in1=r32[j][:], op=ALU.mult),
                      reads=[br[j]], writes=[ba[ai][f]])

        def stage2(t):
            xi, ai = t % NX, t % 2
            for o in range(C):
                k = o % 2
                for fb in range(8):
                    fns = [(lambda e, o=o, f=f, k=k: e.matmul(py[k][:, :n], lhsT=w2s[:, f, o * 128:(o + 1) * 128], rhs=aT[ai][:, f, :],
                                                             start=(f == 0), stop=(f == FC - 1))) for f in range(fb * 4, fb * 4 + 4)]
                    sc.op("pe", fns, reads=ba[ai][fb * 4:fb * 4 + 4] + [bw2[fb]], writes=[bpy[k]])
                sc.op("dve", lambda v, o=o, k=k: v.tensor_tensor(out=x[xi][:, o, :], in0=py[k][:, :n], in1=x[xi][:, o, :], op=ALU.add),
                      reads=[bpy[k], bx[xi]], writes=[bx[xi]])
            dst = self.xT[:, t * n:(t + 1) * n].rearrange("(c p) n -> p c n", p=128)
            stores.append(sc.dma("sp", lambda q, dst=dst: q.dma_start(out=dst, in_=x[xi][:]), reads=[bx[xi]], writes=[self.b_xT], key=f"st{t % 2}"))

        load(0)
        self.norm(x[0], bx[0], h[0], bh[0], n, gcol, scr)
        for t in range(T):
            if t + 1 < T:
                load(t + 1)
            stage1(t)
            if t + 1 < T:
                self.norm(x[(t + 1) % NX], bx[(t + 1) % NX], h[(t + 1) % 2], bh[(t + 1) % 2], n, gcol, scr)
            if t >= 1:
                stage2(t - 1)
        stage2(T - 1)
        self.b_xT.r = []
        self._xT_tokens = stores


_CACHE = {}
HG_PAD = 0
SKIP_ATTN = False
ATT_SUB = "dve"
SELFWAIT_ENGINES = ()
HG_DEBUG = ""


def _consts_np():
    blk = np.zeros((128, 128), np.float32)
    blk[:64, :64] = 1.0 / 64
    blk[64:, 64:] = 1.0 / 64
    q = np.arange(128)[:, None]; s_ = np.arange(128)[None, :]
    mask = np.where(s_ >= q, -240000.0, 0.0).astype(np.float32)
    p = np.arange(128); ch = p // 64; u = p % 64
    same = ch[:, None] == ch[None, :]
    tri = same & (p[:, None] <= p[None, :])
    mid = same & (u[:, None] <= 31)
    M1 = np.zeros((128, 256), np.float32)
    M1[:, :128] = tri.astype(np.float32) - mid.astype(np.float32)
    for c in range(2):
        M1[:, 128 + c] = ((ch == c) & (u <= 31)).astype(np.float32)
        M1[:, 130 + c] = (ch == c).astype(np.float32)
    M2 = (same & (p[:, None] > p[None, :])).astype(np.float32)
    Mc = (same & (p[:, None] <= p[None, :])).astype(np.float32)
    return {"ident": np.eye(128, dtype=np.float32), "onesD": np.full((128, 128), 1.0 / D, np.float32),
            "blk64": blk, "identb": np.eye(128, dtype=np.float32), "maskneg": mask,
            "hg_M1": M1, "hg_M2": M2, "hg_Mc": np.ascontiguousarray(np.tile(Mc, (1, 8))), "ones128": np.full((128, 128), 1.0 / 128, np.float32)}


def kernel(x, norm_gains, sb_w_qkv, sb_q_gain, sb_k_gain, sb_w_o, hg_w_in, hg_lb_logits, hg_norm_gain, hg_w_o,
           mlp_w1, mlp_w2, _layers=(0, 1, 2, 3), _mixers=None, _mlps=None, _hg_blocks=None, _hg_parts=3, _allow_hgrn=False, _cores=None, _return_maps=False):
    key = (tuple(_layers), None if _mixers is None else tuple(_mixers), None if _mlps is None else tuple(_mlps), _hg_blocks, _hg_parts, _allow_hgrn, HG_DEBUG, HG_PAD, SELFWAIT_ENGINES, ATT_SUB, SKIP_ATTN)
    if key not in _CACHE:
        pr = Prog(list(_layers), _mixers, _mlps, hg_blocks=_hg_blocks, hg_parts=_hg_parts, allow_hgrn=_allow_hgrn)
        _CACHE[key] = (pr.build(), pr.used_inputs)
    nc, used = _CACHE[key]
    cst = _consts_np()
    ng = np.ascontiguousarray(np.asarray(norm_gains, np.float32).reshape(DEPTH * 2, C, 128).transpose(2, 0, 1).reshape(128, DEPTH * 2 * C))
    qg, kg = np.asarray(sb_q_gain, np.float32), np.asarray(sb_k_gain, np.float32)
    qk = np.stack([np.tile(qg[0], 2), np.tile(kg[0], 2), np.tile(qg[1], 2), np.tile(kg[1], 2)], axis=1)
    lbl = np.ascontiguousarray(hg_lb_logits, np.float32)
    lblT = np.ascontiguousarray(lbl.reshape(2, C, 128).transpose(2, 0, 1).reshape(128, 2 * C))
    shared = {"hg_w_in": np.ascontiguousarray(hg_w_in, np.float32), "hg_w_o": np.ascontiguousarray(hg_w_o, np.float32),
              "hg_lb_logits": lbl, "hg_lb_logits_T": lblT, "hg_gain_T": np.ascontiguousarray(np.asarray(hg_norm_gain, np.float32).T),
              "qk_gain": np.ascontiguousarray(qk), "sb_w_qkv": np.ascontiguousarray(sb_w_qkv, np.float32),
              "sb_w_o": np.ascontiguousarray(sb_w_o, np.float32), "norm_gains": ng, "mlp_w1": np.ascontiguousarray(mlp_w1, np.float32), "mlp_w2": np.ascontiguousarray(mlp_w2, np.float32), **cst}
    shared = {k: v for k, v in shared.items() if k in used}
    nb = _cores or 8
    in_maps = [dict(shared, x=np.ascontiguousarray(x[b], np.float32)) for b in range(nb)]
    if _return_maps:
        return nc, in_maps
    res = run_bass_kernel_spmd(nc, in_maps, core_ids=list(range(nb)))
    return np.stack([r["out"] for r in res.results], axis=0)
```
